# Optimizing a Trainium2 kernel written in Bass

```python
import jax
import jax.numpy as jnp
from jax import lax
import numpy as np

D_MODEL = 2048
BATCH = 4
SEQ = 2048
DEPTH = 2

GRID_W = 64
CTX_LEN = 256
NORM_EPS = 1e-6
N_BRANCH = 4
BRANCH_W = D_MODEL // N_BRANCH

FNO_GROUPS = 4
FNO_GROUP_W = BRANCH_W // FNO_GROUPS

RWKV_HEAD = 64
RWKV_HEADS = BRANCH_W // RWKV_HEAD
N_DIR = 2
DECAY_LORA = 64
AICL_LORA = 64
GATE_LORA = 128
DIR_LORA_W = DECAY_LORA + AICL_LORA
SHIFT_W = 3 * BRANCH_W + DIR_LORA_W
GN_EPS = 64e-5

ATT_HEAD = 64
ATT_HEADS = BRANCH_W // ATT_HEAD
ATT_KV_HEADS = 2
ATT_REP = ATT_HEADS // ATT_KV_HEADS
ATT_KV_W = ATT_KV_HEADS * ATT_HEAD
WINDOW = 128
BLOCK = 128
ROPE_BASE = 10000.0
NEG_INF = -1e30

CONV_K = 31
LN_EPS = 1e-5

MLP_HIDDEN = 4 * D_MODEL

O_RKV = 0
O_LORA = O_RKV + 3 * BRANCH_W
O_KV = O_LORA + N_DIR * DIR_LORA_W
CTX_STATE_COLS = O_KV + 2 * ATT_KV_W
O_G = CTX_STATE_COLS
O_Q = O_G + GATE_LORA
O_FNO = O_Q + BRANCH_W
O_CONV = O_FNO + BRANCH_W
O_GATE = O_CONV + 2 * BRANCH_W
IN_W = O_GATE + N_BRANCH * D_MODEL

F32 = jnp.float32

kernel_name = 'hybrid_fourier_rwkv7_swa_conformer_dit'


def to_heads(t, n):
    return t.reshape(t.shape[:-1] + (n, t.shape[-1] // n))


def rmsnorm(t, g):
    t32 = t.astype(F32)
    y = t32 * lax.rsqrt(jnp.mean(t32 * t32, axis=-1, keepdims=True) + NORM_EPS)
    return (y * g.astype(F32)).astype(t.dtype)


def token_shift(t, forward):
    if forward:
        return jnp.pad(t[:, :-1], ((0, 0), (1, 0), (0, 0)))
    return jnp.pad(t[:, 1:], ((0, 0), (0, 1), (0, 0)))


def fourier_mix(u):
    B, L, _ = u.shape
    g = u.astype(F32).reshape(B, L, FNO_GROUPS, FNO_GROUP_W)
    f = jnp.fft.fft2(g, axes=(1, 3), norm='ortho').real
    return f.reshape(B, L, BRANCH_W).astype(u.dtype)


def rwkv_scan_inputs(p, mu, w0, w_up, a0, a_up, k_k, k_a):
    rkv = p[..., :3 * BRANCH_W]
    per_dir = []
    for d in range(N_DIR):
        lo = O_LORA + d * DIR_LORA_W
        f = jnp.concatenate([rkv, p[..., lo:lo + DIR_LORA_W]], axis=-1)
        f = f + mu[d] * (token_shift(f, forward=(d == 0)) - f)
        r, k, v, wl, al = jnp.split(
            f, [BRANCH_W, 2 * BRANCH_W, 3 * BRANCH_W, 3 * BRANCH_W + DECAY_LORA], axis=-1)
        w_raw = (w0[d] + jnp.tanh(wl) @ w_up[d]).astype(F32)
        decay = jnp.exp(-jnp.exp(-jax.nn.softplus(-w_raw) - 0.5))
        a = jax.nn.sigmoid((a0[d] + al @ a_up[d]).astype(F32))
        kk = to_heads((k * k_k).astype(F32), RWKV_HEADS)
        kk = kk / jnp.maximum(jnp.sqrt(jnp.sum(kk * kk, axis=-1, keepdims=True)), 1e-12)
        k_rep = k.astype(F32) * (1.0 + (a - 1.0) * k_a.astype(F32))
        per_dir.append([to_heads(r.astype(F32), RWKV_HEADS), to_heads(decay, RWKV_HEADS),
                        to_heads(k_rep, RWKV_HEADS), to_heads(v.astype(F32), RWKV_HEADS),
                        kk, to_heads(a, RWKV_HEADS)])
    return tuple(jnp.stack([per_dir[0][i], per_dir[1][i]]) for i in range(6))


def wkv7_scan(state0, ins, emit):
    def time_major(t):
        t = jnp.stack([t[0], jnp.flip(t[1], axis=1)])
        return jnp.moveaxis(t, 2, 0)

    xs = tuple(time_major(t) for t in ins)

    def step(S, inp):
        r, w, k, v, kk, a = inp
        sa = jnp.einsum('dbhvk,dbhk->dbhv', S, -kk)
        S = (S * w[..., None, :] + sa[..., :, None] * (kk * a)[..., None, :]
             + v[..., :, None] * k[..., None, :])
        y = jnp.einsum('dbhvk,dbhk->dbhv', S, r) if emit else None
        return S, y

    S, ys = lax.scan(step, state0, xs)
    if emit:
        ys = jnp.moveaxis(ys, 0, 2)
        ys = ys[0] + jnp.flip(ys[1], axis=1)
    return S, ys


def rwkv_readout(y, ins, g_in, g_up, r_k, lnx_g, lnx_b, dtype):
    r, _, k, v, _, _ = ins
    B, L = y.shape[:2]
    mean = jnp.mean(y, axis=-1, keepdims=True)
    var = jnp.mean(jnp.square(y - mean), axis=-1, keepdims=True)
    yn = ((y - mean) * lax.rsqrt(var + GN_EPS)).reshape(B, L, BRANCH_W)
    yn = yn * lnx_g.astype(F32) + lnx_b.astype(F32)
    bonus = jnp.sum(jnp.sum(r * k * r_k[:, None, None].astype(F32), axis=-1, keepdims=True) * v, axis=0)
    g = (jax.nn.sigmoid(g_in) @ g_up).astype(F32)
    return ((yn + bonus.reshape(B, L, BRANCH_W)) * g).astype(dtype)


def axial_rope(t, row_ids, col_ids):
    half = t.shape[-1] // 2

    def rotate(u, pos):
        nf = u.shape[-1] // 2
        inv = ROPE_BASE ** (-jnp.arange(nf, dtype=F32) / nf)
        ang = pos.astype(F32)[:, None] * inv[None, :]
        cos = jnp.cos(ang)[None, :, None, :]
        sin = jnp.sin(ang)[None, :, None, :]
        u1 = u[..., :nf].astype(F32)
        u2 = u[..., nf:].astype(F32)
        return jnp.concatenate([u1 * cos - u2 * sin, u2 * cos + u1 * sin], axis=-1)

    out = jnp.concatenate([rotate(t[..., :half], row_ids), rotate(t[..., half:], col_ids)], axis=-1)
    return out.astype(t.dtype)


def window_attention(q, k, v, kc, vc, sink):
    B, S, H, dh = q.shape
    nb = S // BLOCK
    scale = dh ** -0.5
    qb = q.reshape(B, nb, BLOCK, ATT_KV_HEADS, ATT_REP, dh)

    def band(t):
        tp = jnp.pad(t, ((0, 0), (BLOCK, BLOCK), (0, 0), (0, 0)))
        tp = tp.reshape(B, nb + 2, BLOCK, ATT_KV_HEADS, dh)
        return jnp.concatenate([tp[:, :-2], tp[:, 1:-1], tp[:, 2:]], axis=2)

    kw, vw = band(k), band(v)
    s_loc = jnp.einsum('bnqgrd,bnkgd->bngrqk', qb, kw).astype(F32) * scale
    blk = jnp.arange(nb)[:, None, None] * BLOCK
    qpos = blk + jnp.arange(BLOCK)[None, :, None]
    kpos = blk - BLOCK + jnp.arange(3 * BLOCK)[None, None, :]
    valid = (jnp.abs(kpos - qpos) <= WINDOW) & (kpos >= 0) & (kpos < S)
    s_loc = jnp.where(valid[None, :, None, None], s_loc, NEG_INF)
    s_ctx = jnp.einsum('bnqgrd,bcgd->bngrqc', qb, kc).astype(F32) * scale
    s_sink = jnp.broadcast_to(sink.astype(F32).reshape(1, 1, ATT_KV_HEADS, ATT_REP, 1, 1),
                              s_loc.shape[:-1] + (1,))
    p = jax.nn.softmax(jnp.concatenate([s_loc, s_ctx, s_sink], axis=-1), axis=-1)
    nk = 3 * BLOCK
    nc = kc.shape[1]
    o = (jnp.einsum('bngrqk,bnkgd->bnqgrd', p[..., :nk].astype(v.dtype), vw)
         + jnp.einsum('bngrqc,bcgd->bnqgrd', p[..., nk:nk + nc].astype(vc.dtype), vc))
    return o.reshape(B, S, H * dh)


def context_attention(qc, kc, vc, sink):
    B, C, H, dh = qc.shape
    qg = qc.reshape(B, C, ATT_KV_HEADS, ATT_REP, dh)
    s = jnp.einsum('bqgrd,bkgd->bgrqk', qg, kc).astype(F32) * (dh ** -0.5)
    s_sink = jnp.broadcast_to(sink.astype(F32).reshape(1, ATT_KV_HEADS, ATT_REP, 1, 1), s.shape[:-1] + (1,))
    p = jax.nn.softmax(jnp.concatenate([s, s_sink], axis=-1), axis=-1)
    o = jnp.einsum('bgrqk,bkgd->bqgrd', p[..., :C].astype(vc.dtype), vc)
    return o.reshape(B, C, H * dh)


def conformer_conv(u, dw, dw_b, ln_g, ln_b):
    a, b = jnp.split(u, 2, axis=-1)
    h = a * jax.nn.sigmoid(b)
    h = lax.conv_general_dilated(h, dw[:, None, :].astype(h.dtype), (1,), 'SAME',
                                 dimension_numbers=('NWC', 'WIO', 'NWC'),
                                 feature_group_count=BRANCH_W) + dw_b
    h32 = h.astype(F32)
    mean = jnp.mean(h32, axis=-1, keepdims=True)
    var = jnp.mean(jnp.square(h32 - mean), axis=-1, keepdims=True)
    h32 = (h32 - mean) * lax.rsqrt(var + LN_EPS) * ln_g.astype(F32) + ln_b.astype(F32)
    return (h32 * jax.nn.sigmoid(h32)).astype(u.dtype)


def kv_heads(p):
    k = to_heads(p[..., O_KV:O_KV + ATT_KV_W], ATT_KV_HEADS)
    v = to_heads(p[..., O_KV + ATT_KV_W:CTX_STATE_COLS], ATT_KV_HEADS)
    return k, v


def branch_stack(p, ins, y, att, lp):
    fno = fourier_mix(p[..., O_FNO:O_FNO + BRANCH_W])
    rw = rwkv_readout(y, ins, p[..., O_G:O_G + GATE_LORA], lp['rwkv_g_up'], lp['rwkv_r_k'],
                      lp['rwkv_lnx_g'], lp['rwkv_lnx_b'], p.dtype)
    cv = conformer_conv(p[..., O_CONV:O_CONV + 2 * BRANCH_W], lp['conv_dw'], lp['conv_dw_b'],
                        lp['conv_ln_g'], lp['conv_ln_b'])
    return jnp.stack([fno, rw, att.astype(p.dtype), cv], axis=2)


def merge_branches(feats, gate_logits, w_branch, w_out):
    B, L = feats.shape[:2]
    proj = jnp.einsum('blif,ifd->blid', feats, w_branch)
    gate = jax.nn.sigmoid(gate_logits.reshape(B, L, N_BRANCH, D_MODEL))
    return jnp.einsum('blid,de->ble', proj * gate, w_out)


def hybrid_mixer(hx, hc, row_ids, col_ids, lp, with_ctx):
    B = hx.shape[0]
    px = hx @ lp['w_in']
    pc = hc @ (lp['w_in'] if with_ctx else lp['w_in'][:, :CTX_STATE_COLS])
    scan_args = (lp['rwkv_mu'], lp['rwkv_w0'], lp['rwkv_w_up'], lp['rwkv_a0'], lp['rwkv_a_up'],
                 lp['rwkv_k_k'], lp['rwkv_k_a'])
    ins_c = rwkv_scan_inputs(pc[..., :O_KV], *scan_args)
    ins_x = rwkv_scan_inputs(px[..., :O_KV], *scan_args)
    state0 = jnp.zeros((N_DIR, B, RWKV_HEADS, RWKV_HEAD, RWKV_HEAD), F32)
    state_c, y_c = wkv7_scan(state0, ins_c, emit=with_ctx)
    _, y_x = wkv7_scan(state_c, ins_x, emit=True)
    kc, vc = kv_heads(pc)
    kx, vx = kv_heads(px)
    qx = axial_rope(to_heads(px[..., O_Q:O_Q + BRANCH_W], ATT_HEADS), row_ids, col_ids)
    kx = axial_rope(kx, row_ids, col_ids)
    att_x = window_attention(qx, kx, vx, kc, vc, lp['att_sink'])
    out_x = merge_branches(branch_stack(px, ins_x, y_x, att_x, lp), px[..., O_GATE:],
                           lp['w_branch'], lp['w_out'])
    if not with_ctx:
        return out_x, None
    qc = to_heads(pc[..., O_Q:O_Q + BRANCH_W], ATT_HEADS)
    att_c = context_attention(qc, kc, vc, lp['att_sink'])
    out_c = merge_branches(branch_stack(pc, ins_c, y_c, att_c, lp), pc[..., O_GATE:],
                           lp['w_branch'], lp['w_out'])
    return out_x, out_c


def sq_relu_mlp(h, w1, w2):
    return jnp.square(jax.nn.relu(h @ w1)) @ w2


def setup_inputs(seed: int = 0) -> dict:
    key = jax.random.key(seed)
    ks = jax.random.split(key, 32)
    D = D_MODEL

    def nrm(k, shape, s):
        return jax.random.normal(k, shape, F32) * s

    return {
        'x': nrm(ks[0], (BATCH, SEQ, D), 1.0),
        'c': nrm(ks[1], (BATCH, D), 1.0),
        'ctx': nrm(ks[2], (BATCH, CTX_LEN, D), 1.0),
        'c_ctx': nrm(ks[3], (D,), 1.0),
        'ada_w': nrm(ks[4], (DEPTH, D, 6 * D), 0.5 * D ** -0.5),
        'ada_b': nrm(ks[5], (DEPTH, 6 * D), 0.02),
        'norm1_g': 1.0 + nrm(ks[6], (DEPTH, D), 0.02),
        'norm2_g': 1.0 + nrm(ks[7], (DEPTH, D), 0.02),
        'w_in': nrm(ks[8], (DEPTH, D, IN_W), D ** -0.5),
        'rwkv_mu': jax.random.uniform(ks[9], (DEPTH, N_DIR, SHIFT_W), F32),
        'rwkv_w0': nrm(ks[10], (DEPTH, N_DIR, BRANCH_W), 0.5),
        'rwkv_w_up': nrm(ks[11], (DEPTH, N_DIR, DECAY_LORA, BRANCH_W), DECAY_LORA ** -0.5),
        'rwkv_a0': nrm(ks[12], (DEPTH, N_DIR, BRANCH_W), 0.5),
        'rwkv_a_up': nrm(ks[13], (DEPTH, N_DIR, AICL_LORA, BRANCH_W), AICL_LORA ** -0.5),
        'rwkv_k_k': 0.85 + nrm(ks[14], (DEPTH, BRANCH_W), 0.05),
        'rwkv_k_a': 1.0 + nrm(ks[15], (DEPTH, BRANCH_W), 0.05),
        'rwkv_r_k': nrm(ks[16], (DEPTH, N_DIR, RWKV_HEADS, RWKV_HEAD), 0.1),
        'rwkv_g_up': nrm(ks[17], (DEPTH, GATE_LORA, BRANCH_W), GATE_LORA ** -0.5),
        'rwkv_lnx_g': 1.0 + nrm(ks[18], (DEPTH, BRANCH_W), 0.02),
        'rwkv_lnx_b': nrm(ks[19], (DEPTH, BRANCH_W), 0.02),
        'att_sink': nrm(ks[20], (DEPTH, ATT_HEADS), 1.0),
        'conv_dw': nrm(ks[21], (DEPTH, CONV_K, BRANCH_W), CONV_K ** -0.5),
        'conv_dw_b': nrm(ks[22], (DEPTH, BRANCH_W), 0.02),
        'conv_ln_g': 1.0 + nrm(ks[23], (DEPTH, BRANCH_W), 0.02),
        'conv_ln_b': nrm(ks[24], (DEPTH, BRANCH_W), 0.02),
        'w_branch': nrm(ks[25], (DEPTH, N_BRANCH, BRANCH_W, D), BRANCH_W ** -0.5),
        'w_out': nrm(ks[26], (DEPTH, D, D), D ** -0.5),
        'w_mlp1': nrm(ks[27], (DEPTH, D, MLP_HIDDEN), D ** -0.5),
        'w_mlp2': nrm(ks[28], (DEPTH, MLP_HIDDEN, D), MLP_HIDDEN ** -0.5),
        'final_g': 1.0 + nrm(ks[29], (D,), 0.02),
    }


def reference(x, c, ctx, c_ctx, ada_w, ada_b, norm1_g, norm2_g, w_in, rwkv_mu, rwkv_w0, rwkv_w_up,
              rwkv_a0, rwkv_a_up, rwkv_k_k, rwkv_k_a, rwkv_r_k, rwkv_g_up, rwkv_lnx_g, rwkv_lnx_b,
              att_sink, conv_dw, conv_dw_b, conv_ln_g, conv_ln_b, w_branch, w_out, w_mlp1, w_mlp2,
              final_g):
    D = D_MODEL
    rows = x.shape[1] // GRID_W
    row_ids = jnp.repeat(jnp.arange(rows, dtype=jnp.int32), GRID_W)
    col_ids = jnp.tile(jnp.arange(GRID_W, dtype=jnp.int32), rows)
    silu_c = jax.nn.silu(c)
    silu_cc = jax.nn.silu(c_ctx)
    h_ctx = ctx
    for l in range(DEPTH):
        with_ctx = l < DEPTH - 1
        lp = {
            'w_in': w_in[l], 'rwkv_mu': rwkv_mu[l], 'rwkv_w0': rwkv_w0[l], 'rwkv_w_up': rwkv_w_up[l],
            'rwkv_a0': rwkv_a0[l], 'rwkv_a_up': rwkv_a_up[l], 'rwkv_k_k': rwkv_k_k[l],
            'rwkv_k_a': rwkv_k_a[l], 'rwkv_r_k': rwkv_r_k[l], 'rwkv_g_up': rwkv_g_up[l],
            'rwkv_lnx_g': rwkv_lnx_g[l], 'rwkv_lnx_b': rwkv_lnx_b[l], 'att_sink': att_sink[l],
            'conv_dw': conv_dw[l], 'conv_dw_b': conv_dw_b[l], 'conv_ln_g': conv_ln_g[l],
            'conv_ln_b': conv_ln_b[l], 'w_branch': w_branch[l], 'w_out': w_out[l],
        }
        mod_x = silu_c @ ada_w[l] + ada_b[l]
        sh1, sc1, g1, sh2, sc2, g2 = jnp.split(mod_x[:, None, :], 6, axis=-1)
        n_mod = 6 if with_ctx else 2
        mod_c = jnp.split(silu_cc @ ada_w[l][:, :n_mod * D] + ada_b[l][:n_mod * D], n_mod)
        hx = rmsnorm(x, norm1_g[l]) * (1.0 + sc1) + sh1
        hc = rmsnorm(h_ctx, norm1_g[l]) * (1.0 + mod_c[1]) + mod_c[0]
        mix_x, mix_c = hybrid_mixer(hx, hc, row_ids, col_ids, lp, with_ctx)
        x = x + g1 * mix_x
        x = x + g2 * sq_relu_mlp(rmsnorm(x, norm2_g[l]) * (1.0 + sc2) + sh2, w_mlp1[l], w_mlp2[l])
        if with_ctx:
            h_ctx = h_ctx + mod_c[2] * mix_c
            hc2 = rmsnorm(h_ctx, norm2_g[l]) * (1.0 + mod_c[4]) + mod_c[3]
            h_ctx = h_ctx + mod_c[5] * sq_relu_mlp(hc2, w_mlp1[l], w_mlp2[l])
    return rmsnorm(x, final_g)
```

```python
import os
import ml_dtypes
from concourse.bass_utils import run_bass_kernel_spmd
import contextlib
import numpy as np
import concourse.bass as bass
import concourse.mybir as mybir

F32 = mybir.dt.float32
BF16 = mybir.dt.bfloat16
I32 = mybir.dt.int32
AF = mybir.ActivationFunctionType
ALU = mybir.AluOpType
AX = mybir.AxisListType

NDMA = 64


class Buf:
    __slots__ = ("name", "w", "r")

    def __init__(self, name=""):
        self.name = name
        self.w = None
        self.r = {}


class Ctx:
    def __init__(self):
        self.nc = bass.Bass("TRN2", target_bir_lowering=False)
        nc = self.nc
        self.stack = contextlib.ExitStack()
        self.eng = {"pe": nc.tensor, "act": nc.scalar, "dve": nc.vector, "pool": nc.gpsimd, "sp": nc.sync}
        self.semh = {}
        for e in ["pe", "act", "dve", "pool"]:
            self.semh[e] = self.stack.enter_context(nc.semaphore("s_" + e))
        self.cnt = {e: 0 for e in ["pe", "act", "dve", "pool"]}
        self.seen = {e: {} for e in self.eng}
        self.dcnt = [0] * NDMA
        for i in range(NDMA):
            self.semh[("d", i)] = self.stack.enter_context(nc.semaphore(f"s_d{i}"))
        self.dnext = 0
        self.dnext_sw = 0
        self.n_ops = 0
        self.n_waits = 0
        self.out_tokens = []

    def sb(self, name, shape, dtype=F32):
        self._uid = getattr(self, "_uid", 0) + 1
        nm = f"{name}_{self._uid}"
        st = getattr(self, "cur_stack", None)
        if st is not None:
            return st.enter_context(self.nc.sbuf_tensor(nm, list(shape), dtype))
        return self.nc.alloc_sbuf_tensor(nm, list(shape), dtype)

    def barrier(self):
        for e in ["pe", "act", "dve", "pool", "sp"]:
            deps = {}
            for o in ["pe", "act", "dve", "pool"]:
                if self.cnt[o] > 0:
                    deps[o] = self.cnt[o]
            for i in range(NDMA):
                if self.dcnt[i] > 0:
                    deps[("d", i)] = 16 * self.dcnt[i]
            eng = self.eng[e]
            for k, v in deps.items():
                if self.seen[e].get(k, 0) < v:
                    eng.wait_ge(self.semh[k], v)
                    self.seen[e][k] = v
                    self.n_waits += 1

    @contextlib.contextmanager
    def scope(self):
        st = contextlib.ExitStack()
        prev = getattr(self, "cur_stack", None)
        saved = dict(self.pools) if hasattr(self, "pools") else None
        self.cur_stack = st
        try:
            yield
        finally:
            self.barrier()
            st.close()
            self.cur_stack = prev
            if saved is not None:
                self.pools = saved

    def ps(self, name, shape, dtype=F32):
        return self.nc.alloc_psum_tensor(name, list(shape), dtype)

    def dram(self, name, shape, dtype=F32, kind=None):
        if kind is None:
            return self.nc.dram_tensor(name, list(shape), dtype)
        return self.nc.dram_tensor(name, list(shape), dtype, kind=kind)

    def _deps(self, reads, writes):
        deps = {}

        def add(tok):
            if tok is None:
                return
            k, v = tok
            if deps.get(k, 0) < v:
                deps[k] = v

        for b in reads:
            add(b.w)
        for b in writes:
            add(b.w)
            for k, v in b.r.items():
                add((k, v))
        return deps

    def _wait(self, e, deps):
        eng = self.eng[e]
        for k, v in deps.items():
            if e == "pe" and k == "pe":
                continue
            if self.seen[e].get(k, 0) < v:
                eng.wait_ge(self.semh[k], v)
                self.seen[e][k] = v
                self.n_waits += 1

    def _mark(self, tok, reads, writes):
        k, v = tok
        for b in reads:
            if b.r.get(k, 0) < v:
                b.r[k] = v
        for b in writes:
            b.w = tok
            b.r = {}

    def op(self, e, fn, reads=(), writes=()):
        self._wait(e, self._deps(reads, writes))
        ins = fn(self.eng[e])
        self.cnt[e] += 1
        ins.then_inc(self.semh[e], 1)
        self._mark((e, self.cnt[e]), reads, writes)
        self.n_ops += 1
        return ins

    def dma(self, q, out, in_, reads=(), writes=(), **kw):
        if q == "pool":
            i = NDMA // 2 + self.dnext_sw
            self.dnext_sw = (self.dnext_sw + 1) % (NDMA // 2)
        else:
            i = self.dnext
            self.dnext = (i + 1) % (NDMA // 2)
        key = ("d", i)
        deps = self._deps(reads, writes)
        if self.dcnt[i] > 0:
            v = 16 * self.dcnt[i]
            if deps.get(key, 0) < v:
                deps[key] = v
        self._wait(q, deps)
        ins = self.eng[q].dma_start(out=out, in_=in_, **kw)
        self.dcnt[i] += 1
        ins.then_inc(self.semh[key], 16)
        tok = (key, 16 * self.dcnt[i])
        self._mark(tok, reads, writes)
        self.n_ops += 1
        return tok

    def finish(self, bufs):
        deps = self._deps(bufs, ())
        self._wait("sp", deps)
        for e in ["pe", "act", "dve", "pool"]:
            if self.cnt[e] > 0 and self.seen["sp"].get(e, 0) < self.cnt[e]:
                self.eng["sp"].wait_ge(self.semh[e], self.cnt[e])
        for i in range(NDMA):
            if self.dcnt[i] > 0 and self.seen["sp"].get(("d", i), 0) < 16 * self.dcnt[i]:
                self.eng["sp"].wait_ge(self.semh[("d", i)], 16 * self.dcnt[i])


D = 2048
KC = 16
SEQ = 2048
CTX = 256
T = SEQ + CTX
DEPTH = 2
O_RKV = 0
O_LORA = 1536
O_KV = 1792
O_G = 2048
O_Q = 2176
O_FNO = 2688
O_CONV = 3200
O_GATE = 4224
IN_W = 12416
N64 = 48
N128 = 13
TBS = [(0, 256), (256, 512), (768, 512), (1280, 512), (1792, 512)]
NORM_EPS = 1e-6


def rope_partner():
    p = np.zeros(64, np.int64)
    for d in range(64):
        half = (d // 32) * 32
        i = d - half
        p[d] = half + (i + 16 if i < 16 else i - 16)
    return p


def cols64(l=None):
    cols = []
    for h in range(8):
        cols.append(np.arange(O_RKV + h * 64, O_RKV + (h + 1) * 64))
    for h in range(8):
        cols.append(np.arange(512 + h * 64, 512 + (h + 1) * 64))
    for h in range(8):
        cols.append(np.arange(1024 + h * 64, 1024 + (h + 1) * 64))
    for d in range(2):
        lo = O_LORA + d * 128
        cols.append(np.arange(lo, lo + 64))
        cols.append(np.arange(lo + 64, lo + 128))
    pr = rope_partner()
    for h in range(8):
        cols.append(np.arange(O_Q + h * 64, O_Q + (h + 1) * 64))
    for h in range(8):
        cols.append(O_Q + h * 64 + pr)
    for g in range(2):
        cols.append(np.arange(O_KV + g * 64, O_KV + (g + 1) * 64))
    for g in range(2):
        cols.append(O_KV + g * 64 + pr)
    assert len(cols) == N64
    return cols


def cols128():
    cols = [np.arange(O_G, O_G + 128)]
    for j in range(4):
        cols.append(np.arange(O_FNO + j * 128, O_FNO + (j + 1) * 128))
    for j in range(8):
        cols.append(np.arange(O_CONV + j * 128, O_CONV + (j + 1) * 128))
    assert len(cols) == N128
    return cols


def wlayout(w):
    M = w.shape[1]
    return np.ascontiguousarray(w.reshape(KC, 128, M).transpose(1, 0, 2))


def colvec(v):
    return np.ascontiguousarray(v.reshape(-1, 128).T)


class MK(Ctx):
    def __init__(self, debug=()):
        super().__init__()
        self.debug = set(debug)
        self.dbg_out = {}
        self.psb = [(self.ps(f"psb{i}", [128, 512], F32), Buf(f"psb{i}")) for i in range(8)]
        self.psi = 0
        self.pools = {}

    def psum(self):
        t, b = self.psb[self.psi]
        self.psi = (self.psi + 1) % 6
        return t, b

    def run_gens(self, gens):
        gens = [g for g in gens if g is not None]
        while gens:
            for g_ in list(gens):
                try:
                    next(g_)
                except StopIteration:
                    gens.remove(g_)

    def psum_hold(self):
        self.psh = getattr(self, "psh", 0)
        t, b = self.psb[6 + self.psh]
        self.psh = 1 - self.psh
        return t, b

    def pool(self, name, shape, dtype, n):
        if name not in self.pools:
            self.pools[name] = [[(self.sb(f"{name}{i}", shape, dtype), Buf(f"{name}{i}")) for i in range(n)], 0]
        p = self.pools[name]
        t, b = p[0][p[1]]
        p[1] = (p[1] + 1) % len(p[0])
        return t, b

    def tap(self, name, src_ap, src_buf, shape, dtype=F32):
        if name not in self.debug:
            return
        if name not in self.dbg_out:
            self.dbg_out[name] = (self.dram("dbg_" + name, shape, dtype, kind="ExternalOutput"), Buf("dbg_" + name))
        return self.dbg_out[name]

    def declare_inputs(self):
        self.xin = self.dram("xin", [KC, 128, T], F32, kind="ExternalInput")
        self.ccol = self.dram("ccol", [128, KC, 2], F32, kind="ExternalInput")
        self.adaw = self.dram("adaw", [DEPTH, 96, 128, KC, 128], F32, kind="ExternalInput")
        self.adab = self.dram("adab", [DEPTH, 128, 96], F32, kind="ExternalInput")
        self.ng = self.dram("ng", [128, 5, KC], F32, kind="ExternalInput")
        self.wa64 = self.dram("wa64", [DEPTH, N64, 128, KC, 64], F32, kind="ExternalInput")
        self.wa128 = self.dram("wa128", [DEPTH, N128, 128, KC, 128], F32, kind="ExternalInput")
        self.wav = self.dram("wav", [DEPTH, 128, KC, 128], F32, kind="ExternalInput")
        self.HX = self.dram("HX", [KC, 128, T], BF16)
        self.HXb = [Buf(f"HX{i}") for i in range(len(TBS))]
        self.P64 = self.dram("P64", [N64, 64, T], F32)
        self.P64b = [[Buf() for _ in TBS] for _ in range(N64)]
        self.P128 = self.dram("P128", [N128, 128, T], F32)
        self.P128b = [[Buf() for _ in TBS] for _ in range(N128)]
        self.VTM = self.dram("VTM", [T, 128], BF16)
        self.VTMb = [Buf() for _ in range(T // 128)]

    def setup_consts(self):
        self.ones_bf = self.sb("ones_bf", [128, 128], BF16)
        self.ones_b = Buf("ones_bf")
        self.op("pool", lambda e: e.memset(self.ones_bf[:], 1.0), writes=[self.ones_b])
        self.ones_f = self.sb("ones_f", [128, 128], F32)
        self.ones_fb = Buf("ones_f")
        self.op("pool", lambda e: e.memset(self.ones_f[:], 1.0), writes=[self.ones_fb])
        self.ngt = self.sb("ngt", [128, 5, KC], F32)
        self.ngb = Buf("ng")
        self.dma("sp", self.ngt[:], self.ng[:], writes=[self.ngb])
        self.sc = self.sb("sc", [128, KC, 2], F32)
        self.scb = Buf("sc")
        self.dma("sp", self.sc[:], self.ccol[:], writes=[self.scb])
        self.op("act", lambda e: e.activation(out=self.sc[:], in_=self.sc[:], func=AF.Silu),
                reads=[self.scb], writes=[self.scb])
        self.mod = [self.sb(f"mod{l}", [128, 96, 2], F32) for l in range(DEPTH)]
        self.modb = [Buf(f"mod{l}") for l in range(DEPTH)]
        self.geff = [[self.sb(f"geff{l}_{n}", [128, KC, 2], F32) for n in range(2)] for l in range(DEPTH)]
        self.geffb = [[Buf() for n in range(2)] for l in range(DEPTH)]

    def phase_mod(self, l):
        pt, pb = self.psum()
        for fc in range(96):
            wt, wb = self.pool("adaw", [128, KC, 128], F32, 3)
            self.dma("sp", wt[:], self.adaw[l, fc], writes=[wb])
            for kc in range(KC):
                self.op("pe", lambda e, kc=kc, fc=fc, wt=wt: e.matmul(
                    pt[:, 2 * fc:2 * fc + 2], lhsT=wt[:, kc, :], rhs=self.sc[:, kc, :],
                    start=(kc == 0), stop=(kc == KC - 1)),
                    reads=[wb, self.scb], writes=[pb])
        bt, bb = self.pool("adab", [128, 96], F32, 1)
        self.dma("sp", bt[:], self.adab[l], writes=[bb])
        mod = self.mod[l]
        self.op("dve", lambda e: e.tensor_tensor(
            out=mod[:], in0=pt[:, 0:192].rearrange("p (f c) -> p f c", c=2),
            in1=bt[:].unsqueeze(2).to_broadcast([128, 96, 2]), op=ALU.add),
            reads=[pb, bb], writes=[self.modb[l]])
        self.phase_geff(l)

    def phase_geff(self, l):
        mod = self.mod[l]
        for n, (gi, sci) in enumerate([(l, 1), (2 + l, 4)]):
            ge = self.geff[l][n]
            self.op("dve", lambda e, ge=ge, gi=gi, sci=sci: e.scalar_tensor_tensor(
                out=ge[:], in0=mod[:, sci * KC:(sci + 1) * KC, :], scalar=1.0,
                in1=self.ngt[:, gi, :].unsqueeze(2).to_broadcast([128, KC, 2]),
                op0=ALU.add, op1=ALU.mult),
                reads=[self.modb[l], self.ngb], writes=[self.geffb[l][n]])

    def norm_block(self, l, n, xt, xb, t0, tb, is_ctx, out_t, out_b, sq_tile=None):
        ci = 1 if is_ctx else 0
        shi = 0 if n == 0 else 3
        sq, sqb = sq_tile if sq_tile is not None else self.pool("sq", [128, KC, 512], BF16, 1)
        self.op("act", lambda e: e.activation(out=sq[:, :, :tb], in_=xt[:, :, :tb], func=AF.Square),
                reads=[xb], writes=[sqb])
        pt, pb = self.psum()
        for kc in range(KC):
            self.op("pe", lambda e, kc=kc: e.matmul(pt[:, :tb], lhsT=self.ones_bf[:], rhs=sq[:, kc, :tb],
                                                   start=(kc == 0), stop=(kc == KC - 1)),
                    reads=[sqb, self.ones_b], writes=[pb])
        rs, rsb = self.pool("rstd", [128, 512], F32, 2)
        self.op("act", lambda e: e.activation(out=rs[:, :tb], in_=pt[:, :tb], func=AF.Sqrt,
                                              scale=1.0 / D, bias=self.epsc[:, 0:1]),
                reads=[pb, self.epsb], writes=[rsb])
        self.op("dve", lambda e: e.reciprocal(out=rs[:, :tb], in_=rs[:, :tb]), reads=[rsb], writes=[rsb])
        ge = self.geff[l][n]
        mod = self.mod[l]
        for kc in range(KC):
            tmp, tmpb = self.pool("ntmp", [128, 512], F32, 3)
            self.op("dve", lambda e, kc=kc, tmp=tmp: e.scalar_tensor_tensor(
                out=tmp[:, :tb], in0=xt[:, kc, :tb], scalar=ge[:, kc, ci:ci + 1], in1=rs[:, :tb],
                op0=ALU.mult, op1=ALU.mult),
                reads=[xb, rsb, self.geffb[l][n]], writes=[tmpb])
            self.op("act", lambda e, kc=kc, tmp=tmp: e.activation(
                out=out_t[:, kc, :tb], in_=tmp[:, :tb], func=AF.Identity,
                bias=mod[:, shi * KC + kc, ci:ci + 1], scale=1.0),
                reads=[tmpb, self.modb[l]], writes=[out_b])

    def setup_eps(self):
        self.epsc = self.sb("epsc", [128, 4], F32)
        self.epsb = Buf("eps")
        self.op("pool", lambda e: e.memset(self.epsc[:, 0:1], NORM_EPS), writes=[self.epsb])
        self.op("pool", lambda e: e.memset(self.epsc[:, 1:2], 64e-5), writes=[self.epsb])
        self.op("pool", lambda e: e.memset(self.epsc[:, 2:3], 1e-5), writes=[self.epsb])
        self.op("pool", lambda e: e.memset(self.epsc[:, 3:4], 0.0), writes=[self.epsb])

    def phase_a(self, l, xsrc, xsrc_bufs):
        hx = self.sb(f"hx_res{l}", [128, KC, T], BF16)
        hxb = [Buf() for _ in TBS]
        for bi, (t0, tb) in enumerate(TBS):
            xt, xb = self.pool("xblk", [128, KC, 512], F32, 1)
            for q in range(4):
                self.dma("sp", xt[:, 4 * q:4 * q + 4, :tb],
                         xsrc[4 * q:4 * q + 4, :, t0:t0 + tb].rearrange("k p t -> p k t"),
                         reads=[xsrc_bufs[bi]], writes=[xb])
            self.norm_block(l, 0, xt, xb, t0, tb, bi == 0, hx[:, :, t0:t0 + tb], hxb[bi])
            self.dma("sp", self.HX[:, :, t0:t0 + tb].rearrange("k p t -> p k t"), hx[:, :, t0:t0 + tb],
                     reads=[hxb[bi]], writes=[self.HXb[bi]])
        for grp, n, wsrc, dst, dstb in [(64, N64, self.wa64, self.P64, self.P64b),
                                        (128, N128, self.wa128, self.P128, self.P128b)]:
            nj = n // 2 if grp == 64 else n
            for c in range(nj):
                wt, wb = self.pool("wa128", [128, KC, 128], BF16, 3)
                if grp == 64:
                    self.dma("pool", wt[:, :, 0:64], wsrc[l, 2 * c], writes=[wb])
                    self.dma("pool", wt[:, :, 64:128], wsrc[l, 2 * c + 1], writes=[wb])
                else:
                    self.dma("pool", wt[:], wsrc[l, c], writes=[wb])
                for bi, (t0, tb) in enumerate(TBS):
                    pt, pb = self.psum()
                    for kc in range(KC):
                        self.op("pe", lambda e, kc=kc, wt=wt, pt=pt: e.matmul(
                            pt[:, :tb], lhsT=wt[:, kc, :], rhs=hx[:, kc, t0:t0 + tb],
                            start=(kc == 0), stop=(kc == KC - 1)),
                            reads=[wb, hxb[bi]], writes=[pb])
                    ot, ob = self.pool(f"pa_o", [128, 512], F32, 4)
                    if bi % 2 == 0:
                        self.op("act", lambda e, ot=ot, pt=pt: e.copy(out=ot[:, :tb], in_=pt[:, :tb]),
                                reads=[pb], writes=[ob])
                    else:
                        self.op("dve", lambda e, ot=ot, pt=pt: e.tensor_copy(out=ot[:, :tb], in_=pt[:, :tb]),
                                reads=[pb], writes=[ob])
                    if grp == 64:
                        self.dma("sp", dst[2 * c, :, t0:t0 + tb], ot[0:64, :tb], reads=[ob], writes=[dstb[2 * c][bi]])
                        self.dma("sp", dst[2 * c + 1, :, t0:t0 + tb], ot[64:128, :tb], reads=[ob], writes=[dstb[2 * c + 1][bi]])
                    else:
                        self.dma("sp", dst[c, :, t0:t0 + tb], ot[:, :tb], reads=[ob], writes=[dstb[c][bi]])
        wt, wb = self.pool("wa128", [128, KC, 128], BF16, 3)
        self.dma("pool", wt[:], self.wav[l], writes=[wb])
        for ti in range(T // 128):
            bi = 0 if ti < 2 else 1 + (ti - 2) // 4
            pt, pb = self.psum()
            for kc in range(KC):
                self.op("pe", lambda e, kc=kc, pt=pt: e.matmul(
                    pt[:, :128], lhsT=hx[:, kc, ti * 128:(ti + 1) * 128], rhs=wt[:, kc, :],
                    start=(kc == 0), stop=(kc == KC - 1)),
                    reads=[wb, hxb[bi]], writes=[pb])
            ot, ob = self.pool("pav_o", [128, 128], BF16, 3)
            self.op("act", lambda e, ot=ot, pt=pt: e.copy(out=ot[:], in_=pt[:, :128]), reads=[pb], writes=[ob])
            self.dma("sp", self.VTM[ti * 128:(ti + 1) * 128, :], ot[:], reads=[ob], writes=[self.VTMb[ti]])
        return hx


def host_prep(inputs):
    f32 = np.float32
    x = np.asarray(inputs["x"], f32)
    ctx = np.asarray(inputs["ctx"], f32)
    c = np.asarray(inputs["c"], f32)
    c_ctx = np.asarray(inputs["c_ctx"], f32)
    B = x.shape[0]
    shared = {}
    ada_w = np.asarray(inputs["ada_w"], f32)
    shared["adaw"] = np.ascontiguousarray(
        ada_w.reshape(DEPTH, KC, 128, 96, 128).transpose(0, 3, 2, 1, 4))
    ada_b = np.asarray(inputs["ada_b"], f32)
    shared["adab"] = np.ascontiguousarray(ada_b.reshape(DEPTH, 96, 128).transpose(0, 2, 1))
    ng = np.stack([inputs["norm1_g"][0], inputs["norm1_g"][1], inputs["norm2_g"][0], inputs["norm2_g"][1],
                   inputs["final_g"]]).astype(f32)
    shared["ng"] = np.ascontiguousarray(ng.reshape(5, KC, 128).transpose(2, 0, 1))
    w_in = np.asarray(inputs["w_in"], f32)
    c64 = cols64()
    c128 = cols128()
    shared["wa64"] = np.stack([np.stack([wlayout(w_in[l][:, cc]) for cc in c64]) for l in range(DEPTH)])
    shared["wa128"] = np.stack([np.stack([wlayout(w_in[l][:, cc]) for cc in c128]) for l in range(DEPTH)])
    shared["wav"] = np.stack([wlayout(w_in[l][:, O_KV + 128:O_KV + 256]) for l in range(DEPTH)])
    per_core = []
    for b in range(B):
        xin = np.concatenate([ctx[b].T, x[b].T], axis=1)
        m = {"xin": np.ascontiguousarray(xin.reshape(KC, 128, T))}
        cc = np.stack([c[b], c_ctx], axis=-1)
        m["ccol"] = np.ascontiguousarray(cc.reshape(KC, 128, 2).transpose(1, 0, 2))
        per_core.append(m)
    return shared, per_core


TBR = 64
TRO = 128
DECAY_C = float(np.exp(-0.5))
RC_MU = 0
RC_W0 = 52
RC_A0 = 68
RC_KK = 84
RC_KA = 92
RC_RK = 100
RC_LG = 116
RC_LB = 124
RC_N = 132


def rwkv_consts():
    c = np.zeros((128, 1024), np.float32)
    s = np.arange(64)[:, None]
    t = np.arange(64)[None, :]
    for d in range(2):
        before = ((s < t) if d == 0 else (s > t)).astype(np.float32)
        beq = ((s <= t) if d == 0 else (s >= t)).astype(np.float32)
        o = d * 256
        c[0:64, o + 0:o + 64] = -before
        c[64:128, o + 0:o + 64] = before
        c[0:64, o + 64:o + 128] = beq
        c[64:128, o + 64:o + 128] = beq
        c[0:64, o + 128:o + 192] = -(before.T)
    c[:, 512:640] = np.eye(128, dtype=np.float32)
    m = np.ones(128, np.float32)
    m[0] = 0
    m[64] = 0
    c[:, 640:768] = m[None, :]
    return c


class MKR(MK):
    def declare_rwkv(self):
        self.rwc_d = self.dram("rwc", [DEPTH, 64, RC_N], F32, kind="ExternalInput")
        self.wup_d = self.dram("wup", [DEPTH, 2, 64, 512], F32, kind="ExternalInput")
        self.aup_d = self.dram("aup", [DEPTH, 2, 64, 512], F32, kind="ExternalInput")
        self.gup_d = self.dram("gup", [DEPTH, 128, 512], F32, kind="ExternalInput")
        self.rconst_d = self.dram("rconst", [128, 1024], F32, kind="ExternalInput")
        self.YD = self.dram("YD", [2, 8, 64, T], F32)
        self.BOND = self.dram("BOND", [2, 8, 64, T], F32)
        self.FEATS = self.dram("FEATS", [16, 128, T], BF16)
        self.FEATSb = [[Buf() for _ in range(T // 128)] for _ in range(4)]

    def rwkv_setup(self):
        self.rconst = self.sb("rconst_t", [128, 1024], F32)
        self.rconstb = Buf("rconst")
        self.dma("sp", self.rconst[:], self.rconst_d[:], writes=[self.rconstb])
        self.ident = self.rconst[:, 512:640]

    def rwkv(self, l):
        k = self
        nblk = T // TBR
        rwc = k.sb(f"rwc{l}", [64, RC_N + 8], F32)
        rwcb = Buf("rwc")
        k.dma("sp", rwc[:, :RC_N], k.rwc_d[l], writes=[rwcb])
        k.op("dve", lambda e: e.tensor_scalar(out=rwc[:, RC_N:RC_N + 8], in0=rwc[:, RC_KA:RC_KA + 8],
                                              scalar1=-1.0, scalar2=1.0, op0=ALU.mult, op1=ALU.add),
             reads=[rwcb], writes=[rwcb])
        wup = k.sb(f"wup{l}", [64, 2, 512], F32)
        aup = k.sb(f"aup{l}", [64, 2, 512], F32)
        gup = k.sb(f"gup{l}", [128, 512], F32)
        wb_ = Buf("wupaup")
        k.dma("sp", wup[:], k.wup_d[l].rearrange("d j c -> j d c"), writes=[wb_])
        k.dma("sp", aup[:], k.aup_d[l].rearrange("d j c -> j d c"), writes=[wb_])
        k.dma("sp", gup[:], k.gup_d[l], writes=[wb_])
        YDb = [[[Buf() for _ in range(nblk)] for _ in range(2)] for _ in range(2)]
        k._uvub = [Buf(), Buf()]
        CB = [rwcb, k.rconstb]

        def bc(ap2, n):
            return ap2.unsqueeze(2).to_broadcast([64, 8, n])

        import os as _os
        FR = mybir.dt.float32r if _os.environ.get('RW_F32R', '1') == '1' else F32
        identr = k.sb("identr", [64, 64], FR)
        identrb = Buf()
        k.op("dve", lambda e: e.tensor_copy(out=identr[:], in_=k.ident[0:64, 0:64]), reads=[k.rconstb], writes=[identrb])
        def run_dir(d, h0, NH):
            kp = lambda name, shape, dt, n: k.pool(f"{name}_d{d}h{h0}", shape, dt, n)
            bcl = lambda col, n: rwc[:, col + h0:col + h0 + NH].unsqueeze(2).to_broadcast([64, NH, n])
            UVub_s = Buf()
            SV = [k.sb(f"SV{l}{d}{h0}{i}", [128, NH, 64], FR) for i in range(2)]
            SVs = [Buf(), Buf()]
            SVv = [Buf(), Buf()]
            k.op("dve", lambda e: e.tensor_scalar(out=SV[0][0:64].rearrange("p h x -> p (h x)"), in0=k.rconst[0:64, 0:NH * 64], scalar1=0.0, scalar2=None, op0=ALU.mult), reads=[k.rconstb], writes=[SVs[0]])
            cidx = 0
            order = list(range(nblk))
            if d == 1:
                nc_ = CTX // TBR
                order = list(range(nc_ - 1, -1, -1)) + list(range(nblk - 1, nc_ - 1, -1))
            nb_done = 0
            for bi in order:
                t0 = bi * TBR
                is_ctx = t0 < CTX
                seq_lo, seq_hi = (0, CTX) if is_ctx else (CTX, T)
                mixed = []
                for g in range(4):
                    ng = NH if g < 3 else 2
                    c0 = g * 8 + h0 if g < 3 else 24 + 2 * d
                    raw, rawb = kp(f"raw{min(g,3)}", [64, ng, TBR + 1], F32, 2)
                    if d == 0:
                        lo = t0 - 1
                        if lo < seq_lo:
                            k.op("pool", lambda e, raw=raw: e.memset(raw[:, :, 0:1], 0.0), writes=[rawb])
                            k.dma("sp", raw[:, :, 1:TBR + 1],
                                  k.P64[c0:c0 + ng, :, t0:t0 + TBR].rearrange("c p t -> p c t"),
                                  reads=[b_ for c_ in range(c0, c0 + ng) for b_ in k.P64b[c_]], writes=[rawb])
                        else:
                            k.dma("sp", raw[:, :, 0:TBR + 1],
                                  k.P64[c0:c0 + ng, :, lo:t0 + TBR].rearrange("c p t -> p c t"),
                                  reads=[b_ for c_ in range(c0, c0 + ng) for b_ in k.P64b[c_]], writes=[rawb])
                        cur = raw[:, :, 1:TBR + 1]
                        prev = raw[:, :, 0:TBR]
                    else:
                        hi = t0 + TBR + 1
                        if hi > seq_hi:
                            k.op("pool", lambda e, raw=raw: e.memset(raw[:, :, TBR:TBR + 1], 0.0), writes=[rawb])
                            k.dma("sp", raw[:, :, 0:TBR],
                                  k.P64[c0:c0 + ng, :, t0:t0 + TBR].rearrange("c p t -> p c t"),
                                  reads=[b_ for c_ in range(c0, c0 + ng) for b_ in k.P64b[c_]], writes=[rawb])
                        else:
                            k.dma("sp", raw[:, :, 0:TBR + 1],
                                  k.P64[c0:c0 + ng, :, t0:hi].rearrange("c p t -> p c t"),
                                  reads=[b_ for c_ in range(c0, c0 + ng) for b_ in k.P64b[c_]], writes=[rawb])
                        cur = raw[:, :, 0:TBR]
                        prev = raw[:, :, 1:TBR + 1]
                    if g == 2:
                        mx, mxb = kp("vpad", [64, NH, 64 + TBR], F32, 1)
                        mxv = mx[:, :, 64:64 + TBR]
                    else:
                        mx, mxb = kp(f"mix{g}", [64, ng, TBR], F32, 1)
                        mxv = mx[:, :, :]
                    mcol = RC_MU + d * 26 + (g * 8 + h0 if g < 3 else 24)
                    mub = rwc[:, mcol:mcol + ng].unsqueeze(2).to_broadcast([64, ng, TBR])
                    eng = "pool" if g % 2 == 0 else "dve"
                    k.op(eng, lambda e, mxv=mxv, prev=prev, cur=cur: e.tensor_tensor(
                        out=mxv, in0=prev, in1=cur, op=ALU.subtract), reads=[rawb], writes=[mxb])
                    k.op(eng, lambda e, mxv=mxv, mub=mub: e.tensor_tensor(
                        out=mxv, in0=mxv, in1=mub, op=ALU.mult), reads=[mxb, rwcb], writes=[mxb])
                    k.op(eng, lambda e, mxv=mxv, cur=cur: e.tensor_tensor(
                        out=mxv, in0=mxv, in1=cur, op=ALU.add), reads=[mxb, rawb], writes=[mxb])
                    mixed.append((mx, mxv, mxb))
                (rm, rmv, rmb), (km, kmv, kmb), (vp, vmv, vpb), (lm, lmv, lmb) = mixed
                if nb_done == 0:
                    k.op("pool", lambda e, vp=vp: e.memset(vp[:, :, 0:64], 0.0), writes=[vpb])
                yield
                tw, twb = kp("tw", [64, TBR], F32, 1)
                k.op("act", lambda e: e.activation(out=tw[:], in_=lm[:, 0, :], func=AF.Tanh),
                     reads=[lmb], writes=[twb])
                sg, sgb = kp("sg", [64, NH, TBR], F32, 1)
                at, ab = kp("a", [64, NH, TBR], F32, 1)
                for (dst, dstb, up, rhs, rhsb, bcol) in [(sg, sgb, wup, tw[:], twb, RC_W0),
                                                         (at, ab, aup, lm[:, 1, :], lmb, RC_A0)]:
                    for hh in range(NH // 4):
                        pt, pb = k.psum()
                        for h4 in range(4):
                            h = hh * 4 + h4
                            k.op("pe", lambda e, pt=pt, h=h, h4=h4, up=up, rhs=rhs: e.matmul(
                                pt[:64, h4 * TBR:(h4 + 1) * TBR], lhsT=up[:, d, (h0 + h) * 64:(h0 + h + 1) * 64], rhs=rhs,
                                start=True, stop=True), reads=[wb_, rhsb], writes=[pb])
                        for h4 in range(4):
                            h = hh * 4 + h4
                            k.op("act", lambda e, pt=pt, h=h, h4=h4, dst=dst, bcol=bcol: e.activation(
                                out=dst[:, h, :], in_=pt[:64, h4 * TBR:(h4 + 1) * TBR], func=AF.Sigmoid,
                                bias=rwc[:, bcol + d * 8 + h0 + h:bcol + d * 8 + h0 + h + 1], scale=1.0),
                                reads=[pb, rwcb], writes=[dstb])
                yield
                cs, csb = kp("cs", [64, NH, TBR], F32, 1)
                for h in range(NH):
                    k.op("dve", lambda e, h=h: e.tensor_tensor_scan(
                        out=cs[:, h, :], data0=k.rconst[0:64, 640:640 + TBR], data1=sg[:, h, :], initial=0.0,
                        op0=ALU.mult, op1=ALU.add), reads=[sgb, k.rconstb], writes=[csb])
                cs4 = cs[:].rearrange("p h (c t) -> p (h c) t", t=64)
                sg4 = sg[:].rearrange("p h (c t) -> p (h c) t", t=64)
                if d == 1:
                    tot, totb = kp("tot", [64, NH * (TBR // 64), 1], F32, 1)
                    k.op("pool", lambda e: e.tensor_copy(out=tot[:], in_=cs4[:, :, 63:64]), reads=[csb], writes=[totb])
                    k.op("dve", lambda e: e.tensor_tensor(out=cs4, in0=sg4, in1=cs4, op=ALU.subtract),
                         reads=[sgb, csb], writes=[csb])
                    k.op("dve", lambda e: e.tensor_tensor(out=cs4, in0=cs4, in1=tot[:].to_broadcast([64, NH * (TBR // 64), 64]),
                                                          op=ALU.add), reads=[csb, totb], writes=[csb])
                Pt, Pb = kp("P", [64, NH, TBR], F32, 1)
                Pi, Pib = kp("Pi", [64, NH, TBR], F32, 1)
                Pp, Ppb = kp("Pp", [64, NH, TBR], F32, 1)
                k.op("act", lambda e: e.activation(out=Pt[:], in_=cs[:], func=AF.Exp, scale=-DECAY_C),
                     reads=[csb], writes=[Pb])
                k.op("act", lambda e: e.activation(out=Pi[:], in_=cs[:], func=AF.Exp, scale=DECAY_C),
                     reads=[csb], writes=[Pib])
                k.op("pool", lambda e: e.tensor_tensor(out=Pp[:], in0=cs[:], in1=sg[:], op=ALU.subtract),
                     reads=[csb, sgb], writes=[Ppb])
                k.op("act", lambda e: e.activation(out=Pp[:], in_=Pp[:], func=AF.Exp, scale=-DECAY_C),
                     reads=[Ppb], writes=[Ppb])
                yield
                kap, kapb = kp("kap", [64, NH, TBR], F32, 1)
                k.op("dve", lambda e: e.tensor_tensor(out=kap[:], in0=kmv, in1=bcl(RC_KK, TBR),
                                                      op=ALU.mult), reads=[kmb, rwcb], writes=[kapb])
                sq, sqb = kp("rsq", [64, NH, TBR], F32, 1)
                k.op("act", lambda e: e.activation(out=sq[:], in_=kap[:], func=AF.Square), reads=[kapb], writes=[sqb])
                rin, rinb = kp("rin", [64, NH, TBR], F32, 1)
                for hh in range(NH // 4):
                    pt, pb = k.psum()
                    k.op("pe", lambda e, pt=pt, hh=hh: e.matmul(
                        pt[:64, :4 * TBR], lhsT=k.ones_f[0:64, 0:64],
                        rhs=sq[:, hh * 4:(hh + 1) * 4, :].rearrange("p h t -> p (h t)"),
                        start=True, stop=True), reads=[sqb, k.ones_fb], writes=[pb])
                    k.op("act", lambda e, pt=pt, hh=hh: e.activation(
                        out=rin[:, hh * 4:(hh + 1) * 4, :].rearrange("p h t -> p (h t)"), in_=pt[:64, :4 * TBR],
                        func=AF.Sqrt), reads=[pb], writes=[rinb])
                k.op("dve", lambda e: e.tensor_scalar(out=rin[:], in0=rin[:], scalar1=1e-12, scalar2=None,
                                                      op0=ALU.max), reads=[rinb], writes=[rinb])
                k.op("dve", lambda e: e.reciprocal(out=rin[:], in_=rin[:]), reads=[rinb], writes=[rinb])
                k.op("dve", lambda e: e.tensor_tensor(out=kap[:], in0=kap[:], in1=rin[:], op=ALU.mult),
                     reads=[kapb, rinb], writes=[kapb])
                yield
                kr, krb = kp("krep", [64, NH, TBR], F32, 1)
                k.op("pool", lambda e: e.tensor_tensor(out=kr[:], in0=at[:], in1=bcl(RC_KA, TBR),
                                                       op=ALU.mult), reads=[ab, rwcb], writes=[krb])
                k.op("pool", lambda e: e.tensor_tensor(out=kr[:], in0=kr[:], in1=bcl(RC_N, TBR),
                                                       op=ALU.add), reads=[krb, rwcb], writes=[krb])
                k.op("pool", lambda e: e.tensor_tensor(out=kr[:], in0=kr[:], in1=kmv, op=ALU.mult),
                     reads=[krb, kmb], writes=[krb])
                bt_, btb = kp("bb", [64, NH, TBR], F32, 1)
                k.op("pool", lambda e: e.tensor_tensor(out=bt_[:], in0=kap[:], in1=at[:], op=ALU.mult),
                     reads=[kapb, ab], writes=[btb])
                yield
                bon, bonb = kp("bon", [64, NH, TBR], F32, 1)
                k.op("dve", lambda e: e.tensor_tensor(out=bon[:], in0=rmv, in1=kr[:], op=ALU.mult),
                     reads=[rmb, krb], writes=[bonb])
                k.op("dve", lambda e: e.tensor_tensor(out=bon[:], in0=bon[:],
                                                      in1=bcl(RC_RK + d * 8, TBR), op=ALU.mult),
                     reads=[bonb, rwcb], writes=[bonb])
                for hh in range(NH // 4):
                    pt, pb = k.psum()
                    k.op("pe", lambda e, pt=pt, hh=hh: e.matmul(
                        pt[:64, :4 * TBR], lhsT=k.ones_f[0:64, 0:64],
                        rhs=bon[:, hh * 4:(hh + 1) * 4, :].rearrange("p h t -> p (h t)"),
                        start=True, stop=True), reads=[bonb, k.ones_fb], writes=[pb])
                    k.op("dve", lambda e, pt=pt, hh=hh: e.tensor_tensor(
                        out=bon[:, hh * 4:(hh + 1) * 4, :],
                        in0=pt[:64, :4 * TBR].rearrange("p (h t) -> p h t", t=TBR),
                        in1=vmv[:, hh * 4:(hh + 1) * 4, :], op=ALU.mult), reads=[pb, vpb, bonb], writes=[bonb])
                yield
                QR, QRb = kp("QR", [64, NH, TBR // 64, 128], FR, 1)
                BK, BKb = kp("BK", [64, NH, TBR // 64, 128], FR, 1)

                def v4(ap):
                    return ap.rearrange("p h (c t) -> p h c t", t=64)
                k.op("dve", lambda e: e.tensor_tensor(out=QR[:, :, :, 0:64], in0=v4(kap[:]), in1=v4(Pp[:]), op=ALU.mult),
                     reads=[kapb, Ppb], writes=[QRb])
                k.op("dve", lambda e: e.tensor_tensor(out=QR[:, :, :, 64:128], in0=v4(rmv), in1=v4(Pt[:]), op=ALU.mult),
                     reads=[rmb, Pb], writes=[QRb])
                k.op("dve", lambda e: e.tensor_tensor(out=BK[:, :, :, 0:64], in0=v4(bt_[:]), in1=v4(Pi[:]), op=ALU.mult),
                     reads=[btb, Pib], writes=[BKb])
                k.op("dve", lambda e: e.tensor_tensor(out=BK[:, :, :, 64:128], in0=v4(kr[:]), in1=v4(Pi[:]), op=ALU.mult),
                     reads=[krb, Pib], writes=[BKb])
                yield
                Yt, Yb = kp("Yt", [64, NH, TBR], F32, 1)
                mo = d * 256
                for c in (list(range(TBR // 64)) if d == 0 else list(range(TBR // 64 - 1, -1, -1))):
                    cur_sv = cidx % 2
                    nxt_sv = (cidx + 1) % 2
                    QB, QBb = kp("QB", [128, NH, 64], FR, 1)
                    AMR, AMRb = kp("AMR", [128, NH, 64], FR, 1)
                    X = [kp("X", [64, NH, 64], FR, 2) for _ in range(1)]
                    Xa, Xab = X[0]
                    XTa, XTab = kp("XT", [64, NH, 64], FR, 2)
                    Tm, Tmb = kp("Tm", [64, NH, 64], FR, 1)
                    k.op("act", lambda e, QB=QB, c=c: e.copy(out=QB[0:64, :, :], in_=QR[:, :, c, 0:64]),
                         reads=[QRb], writes=[QBb])
                    for hh in range(NH // 4):
                        pt, pb = k.psum()
                        for h4 in range(4):
                            h = hh * 4 + h4
                            k.op("pe", lambda e, pt=pt, h=h, h4=h4, c=c: e.matmul(
                                pt[:, h4 * 128:(h4 + 1) * 128], lhsT=BK[:, h, c, :], rhs=QR[:, h, c, :],
                                start=True, stop=True), reads=[BKb, QRb], writes=[pb])
                        p4 = pt[:, :].rearrange("p (h x) -> p h x", x=128)
                        hs = slice(hh * 4, hh * 4 + 4)
                        k.op("dve", lambda e, p4=p4, hs=hs, Xa=Xa: e.tensor_tensor(
                            out=Xa[:, hs, :], in0=p4[0:64, :, 0:64],
                            in1=k.rconst[0:64, mo:mo + 64].unsqueeze(1).to_broadcast([64, 4, 64]), op=ALU.mult),
                            reads=[pb, k.rconstb], writes=[Xab])
                        k.op("dve", lambda e, p4=p4, hs=hs, QB=QB: e.tensor_tensor(
                            out=QB[64:128, hs, :], in0=p4[64:128, :, 0:64],
                            in1=k.rconst[64:128, mo:mo + 64].unsqueeze(1).to_broadcast([64, 4, 64]), op=ALU.mult),
                            reads=[pb, k.rconstb], writes=[QBb])
                        k.op("dve", lambda e, p4=p4, hs=hs, AMR=AMR: e.tensor_tensor(
                            out=AMR[:, hs, :], in0=p4[:, :, 64:128],
                            in1=k.rconst[:, mo + 64:mo + 128].unsqueeze(1).to_broadcast([128, 4, 64]), op=ALU.mult),
                            reads=[pb, k.rconstb], writes=[AMRb])
                    yield
                    pt, pb = k.psum()
                    for h in range(NH):
                        k.op("pe", lambda e, pt=pt, h=h, c=c: e.matmul(
                            pt[:64, h * 64:(h + 1) * 64], lhsT=QR[:, h, c, 0:64], rhs=BK[:, h, c, 0:64],
                            start=True, stop=True), reads=[BKb, QRb], writes=[pb])
                    k.op("dve", lambda e, pt=pt, XTa=XTa: e.tensor_tensor(
                        out=XTa[:], in0=pt[:64, :NH * 64].rearrange("p (h x) -> p h x", x=64),
                        in1=k.rconst[0:64, mo + 128:mo + 192].unsqueeze(1).to_broadcast([64, NH, 64]), op=ALU.mult),
                        reads=[pb, k.rconstb], writes=[XTab])
                    k.op("dve", lambda e, Tm=Tm, Xa=Xa: e.tensor_tensor(
                        out=Tm[:], in0=Xa[:], in1=k.ident[0:64, 0:64].unsqueeze(1).to_broadcast([64, NH, 64]),
                        op=ALU.add), reads=[Xab, k.rconstb], writes=[Tmb])
                    yield
                    for j in range(1, 6):
                        XTn, XTnb = kp("XT", [64, NH, 64], FR, 2)
                        pt, pb = k.psum()
                        for h in range(NH):
                            k.op("pe", lambda e, pt=pt, h=h, Xa=Xa, XTa=XTa: e.matmul(
                                pt[:64, h * 64:(h + 1) * 64], lhsT=Xa[:, h, :], rhs=XTa[:, h, :],
                                start=True, stop=True), reads=[Xab, XTab], writes=[pb])
                        k.op("act", lambda e, pt=pt, XTn=XTn: e.copy(
                            out=XTn[:].rearrange("p h x -> p (h x)"), in_=pt[:64, :NH * 64]), reads=[pb], writes=[XTnb])
                        if j < 5:
                            Xn, Xnb = kp("X", [64, NH, 64], FR, 2)
                            pt2, pb2 = k.psum()
                            for h in range(NH):
                                k.op("pe", lambda e, pt2=pt2, h=h, Xa=Xa, XTa=XTa: e.matmul(
                                    pt2[:64, h * 64:(h + 1) * 64], lhsT=XTa[:, h, :], rhs=Xa[:, h, :],
                                    start=True, stop=True), reads=[Xab, XTab], writes=[pb2])
                            k.op("act", lambda e, pt2=pt2, Xn=Xn: e.copy(
                                out=Xn[:].rearrange("p h x -> p (h x)"), in_=pt2[:64, :NH * 64]), reads=[pb2], writes=[Xnb])
                        pt3, pb3 = k.psum()
                        for h in range(NH):
                            k.op("pe", lambda e, pt3=pt3, h=h, XTn=XTn, Tm=Tm: e.matmul(
                                pt3[:64, h * 64:(h + 1) * 64], lhsT=XTn[:, h, :], rhs=Tm[:, h, :],
                                start=True, stop=True), reads=[XTnb, Tmb], writes=[pb3])
                        k.op("dve", lambda e, pt3=pt3, Tm=Tm: e.tensor_tensor(
                            out=Tm[:].rearrange("p h x -> p (h x)"), in0=pt3[:64, :NH * 64],
                            in1=Tm[:].rearrange("p h x -> p (h x)"), op=ALU.add), reads=[pb3, Tmb], writes=[Tmb])
                        XTa, XTab = XTn, XTnb
                        yield
                        if j < 5:
                            Xa, Xab = Xn, Xnb
                    yield
                    BKT, BKTb = kp("BKT", [128, NH, 64], FR, 1)
                    pt, pb = k.psum()
                    for h in range(NH):
                        k.op("pe", lambda e, pt=pt, h=h, c=c: e.transpose(
                            pt[:, h * 64:(h + 1) * 64], BK[:, h, c, :].bitcast(F32), k.ident[0:64, 0:64]),
                            reads=[BKb, k.rconstb], writes=[pb])
                    k.op("act", lambda e, pt=pt, BKT=BKT: e.copy(
                        out=BKT[:].rearrange("p h x -> p (h x)"), in_=pt[:, :NH * 64]), reads=[pb], writes=[BKTb])
                    yield
                    UV, UVvb = kp("UV", [128, NH, 64], FR, 1)
                    UVub = UVub_s
                    pt, pb = k.psum()
                    for h in range(NH):
                        k.op("pe", lambda e, pt=pt, h=h, c=c: e.transpose(
                            pt[:, h * 64:(h + 1) * 64], vp[:, h, c * 64:c * 64 + 128], k.ident[0:64, 0:64]),
                            reads=[vpb, k.rconstb], writes=[pb])
                    k.op("dve", lambda e, pt=pt, UV=UV: e.tensor_copy(
                        out=UV[64:128].rearrange("p h x -> p (h x)"), in_=pt[64:128, :NH * 64]), reads=[pb], writes=[UVvb])
                    k.op("dve", lambda e, pt=pt, cur_sv=cur_sv: e.tensor_copy(
                        out=SV[cur_sv][64:128].rearrange("p h x -> p (h x)"), in_=pt[64:128, :NH * 64]),
                        reads=[pb], writes=[SVv[cur_sv]])
                    yield
                    WT, WTb = kp("WT", [64, NH, 64], FR, 1)
                    pt, pb = k.psum()
                    for h in range(NH):
                        k.op("pe", lambda e, pt=pt, h=h, QB=QB, cur_sv=cur_sv: e.matmul(
                            pt[:64, h * 64:(h + 1) * 64], lhsT=QB[:, h, :], rhs=SV[cur_sv][:, h, :],
                            start=True, stop=True), reads=[QBb, SVs[cur_sv], SVv[cur_sv]], writes=[pb])
                    k.op("act", lambda e, pt=pt, WT=WT: e.copy(
                        out=WT[:].rearrange("p h x -> p (h x)"), in_=pt[:64, :NH * 64]), reads=[pb], writes=[WTb])
                    yield
                    pt, pb = k.psum()
                    for h in range(NH):
                        k.op("pe", lambda e, pt=pt, h=h, Tm=Tm, WT=WT: e.matmul(
                            pt[:64, h * 64:(h + 1) * 64], lhsT=Tm[:, h, :], rhs=WT[:, h, :],
                            start=True, stop=True), reads=[Tmb, WTb], writes=[pb])
                    k.op("act", lambda e, pt=pt, UV=UV: e.mul(
                        out=UV[0:64].rearrange("p h x -> p (h x)"), in_=pt[:64, :NH * 64], mul=-1.0), reads=[pb], writes=[UVub])
                    yield
                    pt, pb = k.psum()
                    for h in range(NH):
                        k.op("pe", lambda e, pt=pt, h=h, c=c, cur_sv=cur_sv: e.matmul(
                            pt[:64, h * 64:(h + 1) * 64], lhsT=SV[cur_sv][0:64, h, :], rhs=QR[:, h, c, 64:128],
                            start=True, stop=False), reads=[SVs[cur_sv], QRb], writes=[pb])
                        k.op("pe", lambda e, pt=pt, h=h, UV=UV, AMR=AMR: e.matmul(
                            pt[:64, h * 64:(h + 1) * 64], lhsT=UV[:, h, :], rhs=AMR[:, h, :],
                            start=False, stop=True), reads=[UVub, UVvb, AMRb], writes=[pb])
                    k.op("act", lambda e, pt=pt, c=c: e.copy(
                        out=Yt[:, :, c * 64:(c + 1) * 64], in_=pt[:64, :NH * 64].rearrange("p (h x) -> p h x", x=64)),
                        reads=[pb], writes=[Yb])
                    yield
                    pt, pb = k.psum()
                    for h in range(NH):
                        k.op("pe", lambda e, pt=pt, h=h, cur_sv=cur_sv: e.matmul(
                            pt[:64, h * 64:(h + 1) * 64], lhsT=identr[:], rhs=SV[cur_sv][0:64, h, :],
                            start=True, stop=False), reads=[SVs[cur_sv], identrb], writes=[pb])
                        k.op("pe", lambda e, pt=pt, h=h, BKT=BKT, UV=UV: e.matmul(
                            pt[:64, h * 64:(h + 1) * 64], lhsT=BKT[:, h, :], rhs=UV[:, h, :],
                            start=False, stop=True), reads=[BKTb, UVub, UVvb], writes=[pb])
                    pcol = (c * 64 + 63) if d == 0 else (c * 64)
                    k.op("dve", lambda e, pt=pt, nxt_sv=nxt_sv, pcol=pcol: e.tensor_tensor(
                        out=SV[nxt_sv][0:64], in0=pt[:64, :NH * 64].rearrange("p (h x) -> p h x", x=64),
                        in1=Pt[:, :, pcol:pcol + 1].to_broadcast([64, NH, 64]), op=ALU.mult),
                        reads=[pb, Pb], writes=[SVs[nxt_sv]])
                    cidx += 1
                k.dma("sp", k.YD[d, h0:h0 + NH, :, t0:t0 + TBR].rearrange("h p t -> p h t"), Yt[:], reads=[Yb], writes=[YDb[d][h0 // NH][bi]])
                k.dma("sp", k.BOND[d, h0:h0 + NH, :, t0:t0 + TBR].rearrange("h p t -> p h t"), bon[:], reads=[bonb], writes=[YDb[d][h0 // NH][bi]])
                nb_done += 1
                yield

        with k.scope():
            NHG = int(_os.environ.get('RW_NH', '8'))
            gens = [run_dir(d_, h0_, NHG) for h0_ in range(0, 8, NHG) for d_ in range(2)]
            while gens:
                for g_ in list(gens):
                    try:
                        next(g_)
                    except StopIteration:
                        gens.remove(g_)
        k._ydb = YDb

    def rwkv_readout_gen(self, l):
        k = self
        YDb = k._ydb
        rwc = k.sb(f"rwc_ro{l}", [64, RC_N + 8], F32)
        rwcb = Buf("rwc_ro")
        k.dma("sp", rwc[:, :RC_N], k.rwc_d[l], writes=[rwcb])
        gup = k.sb(f"gup_ro{l}", [128, 512], F32)
        wb_ = Buf("gup_ro")
        k.dma("sp", gup[:], k.gup_d[l], writes=[wb_])
        for ro in range(T // TRO):
            yield
            k.rwkv_readout(l, ro, ro * TRO, [b_ for d_ in range(2) for g_ in range(2) for b_ in YDb[d_][g_][ro * (TRO // TBR):(ro + 1) * (TRO // TBR)]],
                           rwc, rwcb, gup, wb_)

    def rwkv_readout(self, l, bi, t0, ydbufs, rwc, rwcb, gup, gupb):
        k = self
        yf, yfbuf = k.pool("yf", [64, 8, TRO], F32, 1)
        bf_, bfbuf = k.pool("bonf", [64, 8, TRO], F32, 1)
        yb_, ybbuf = k.pool("yb", [64, 8, TRO], F32, 1)
        bb_, bbbuf = k.pool("bonb", [64, 8, TRO], F32, 1)
        for (dst, dstb, src, dd) in [(yf, yfbuf, k.YD, 0), (yb_, ybbuf, k.YD, 1), (bf_, bfbuf, k.BOND, 0), (bb_, bbbuf, k.BOND, 1)]:
            k.dma("sp", dst[:], src[dd, :, :, t0:t0 + TRO].rearrange("h p t -> p h t"), reads=ydbufs, writes=[dstb])
        k.op("pool", lambda e: e.tensor_tensor(out=yf[:], in0=yf[:], in1=yb_[:], op=ALU.add),
             reads=[yfbuf, ybbuf], writes=[yfbuf])
        k.op("pool", lambda e: e.tensor_tensor(out=bf_[:], in0=bf_[:], in1=bb_[:], op=ALU.add),
             reads=[bfbuf, bbbuf], writes=[bfbuf])
        if "rwkv_y" in k.debug and l == 0:
            if "o_y" not in k.dbg_out:
                k.dbg_out["o_y"] = k.dram("o_y", [8, 64, T], F32, kind="ExternalOutput")
            ob = Buf()
            k.dma("sp", k.dbg_out["o_y"][:, :, t0:t0 + TRO].rearrange("h p t -> p h t"), yf[:], reads=[yfbuf], writes=[ob])
            k.out_tokens.append(ob)
        mean, meanb = k.pool("gn_m", [64, 8, TRO], F32, 1)
        for hh in range(2):
            pt, pb = k.psum()
            k.op("pe", lambda e, pt=pt, hh=hh: e.matmul(
                pt[:64, :], lhsT=k.ones_f[0:64, 0:64], rhs=yf[:, hh * 4:(hh + 1) * 4, :].rearrange("p h t -> p (h t)"),
                start=True, stop=True), reads=[yfbuf, k.ones_fb], writes=[pb])
            k.op("dve", lambda e, pt=pt, hh=hh: e.scalar_tensor_tensor(
                out=mean[:, hh * 4:(hh + 1) * 4, :].rearrange("p h t -> p (h t)"), in0=pt[:64, :], scalar=-1.0 / 64,
                in1=yf[:, hh * 4:(hh + 1) * 4, :].rearrange("p h t -> p (h t)"), op0=ALU.mult, op1=ALU.add),
                reads=[pb, yfbuf], writes=[meanb])
        sq, sqb = k.pool("gn_sq", [64, 8, TRO], F32, 1)
        k.op("act", lambda e: e.activation(out=sq[:], in_=mean[:], func=AF.Square), reads=[meanb], writes=[sqb])
        rstd, rstdb = k.pool("gn_r", [64, 8, TRO], F32, 1)
        for hh in range(2):
            pt, pb = k.psum()
            k.op("pe", lambda e, pt=pt, hh=hh: e.matmul(
                pt[:64, :], lhsT=k.ones_f[0:64, 0:64], rhs=sq[:, hh * 4:(hh + 1) * 4, :].rearrange("p h t -> p (h t)"),
                start=True, stop=True), reads=[sqb, k.ones_fb], writes=[pb])
            k.op("act", lambda e, pt=pt, hh=hh: e.activation(
                out=rstd[:, hh * 4:(hh + 1) * 4, :].rearrange("p h t -> p (h t)"), in_=pt[:64, :], func=AF.Sqrt,
                scale=1.0 / 64, bias=k.epsc[0:64, 1:2]), reads=[pb, k.epsb], writes=[rstdb])
        k.op("dve", lambda e: e.reciprocal(out=rstd[:], in_=rstd[:]), reads=[rstdb], writes=[rstdb])
        k.op("dve", lambda e: e.tensor_tensor(out=mean[:], in0=mean[:], in1=rstd[:], op=ALU.mult),
             reads=[meanb, rstdb], writes=[meanb])
        k.op("pool", lambda e: e.tensor_tensor(
            out=mean[:], in0=mean[:], in1=rwc[:, RC_LG:RC_LG + 8].unsqueeze(2).to_broadcast([64, 8, TRO]), op=ALU.mult),
            reads=[meanb, rwcb], writes=[meanb])
        k.op("pool", lambda e: e.tensor_tensor(
            out=mean[:], in0=mean[:], in1=rwc[:, RC_LB:RC_LB + 8].unsqueeze(2).to_broadcast([64, 8, TRO]), op=ALU.add),
            reads=[meanb, rwcb], writes=[meanb])
        k.op("pool", lambda e: e.tensor_tensor(out=mean[:], in0=mean[:], in1=bf_[:], op=ALU.add),
             reads=[meanb, bfbuf], writes=[meanb])
        gs, gsb = k.pool("gsig", [128, TRO], F32, 1)
        k.dma("sp", gs[:], k.P128[0, :, t0:t0 + TRO], reads=k.P128b[0], writes=[gsb])
        k.op("act", lambda e: e.activation(out=gs[:], in_=gs[:], func=AF.Sigmoid), reads=[gsb], writes=[gsb])
        ot, otb = k.pool("rw_out", [64, 8, TRO], BF16, 1)
        for hh in range(2):
            pt, pb = k.psum()
            for h4 in range(4):
                h = hh * 4 + h4
                k.op("pe", lambda e, pt=pt, h=h, h4=h4: e.matmul(
                    pt[:64, h4 * TRO:(h4 + 1) * TRO], lhsT=gup[:, h * 64:(h + 1) * 64], rhs=gs[:],
                    start=True, stop=True), reads=[gupb, gsb], writes=[pb])
            k.op("dve", lambda e, pt=pt, hh=hh: e.tensor_tensor(
                out=ot[:, hh * 4:(hh + 1) * 4, :].rearrange("p h t -> p (h t)"), in0=pt[:64, :],
                in1=mean[:, hh * 4:(hh + 1) * 4, :].rearrange("p h t -> p (h t)"), op=ALU.mult),
                reads=[pb, meanb], writes=[otb])
        k.dma("sp", k.FEATS[4:8, :, t0:t0 + TRO].rearrange("c (two p) t -> p c two t", two=2),
              ot[:].rearrange("p (c two) t -> p c two t", two=2), reads=[otb], writes=[k.FEATSb[1][bi]])


def rwkv_host_prep(inputs):
    f32 = np.float32
    cols = np.zeros((DEPTH, 64, RC_N), f32)
    for l in range(DEPTH):
        for d in range(2):
            mu = np.asarray(inputs["rwkv_mu"], f32)[l, d]
            cols[l, :, RC_MU + d * 26:RC_MU + (d + 1) * 26] = mu.reshape(26, 64).T
            cols[l, :, RC_W0 + d * 8:RC_W0 + (d + 1) * 8] = np.asarray(inputs["rwkv_w0"], f32)[l, d].reshape(8, 64).T
            cols[l, :, RC_A0 + d * 8:RC_A0 + (d + 1) * 8] = np.asarray(inputs["rwkv_a0"], f32)[l, d].reshape(8, 64).T
            cols[l, :, RC_RK + d * 8:RC_RK + (d + 1) * 8] = np.asarray(inputs["rwkv_r_k"], f32)[l, d].T
        cols[l, :, RC_KK:RC_KK + 8] = np.asarray(inputs["rwkv_k_k"], f32)[l].reshape(8, 64).T
        cols[l, :, RC_KA:RC_KA + 8] = np.asarray(inputs["rwkv_k_a"], f32)[l].reshape(8, 64).T
        cols[l, :, RC_LG:RC_LG + 8] = np.asarray(inputs["rwkv_lnx_g"], f32)[l].reshape(8, 64).T
        cols[l, :, RC_LB:RC_LB + 8] = np.asarray(inputs["rwkv_lnx_b"], f32)[l].reshape(8, 64).T
    return {"rwc": cols, "wup": np.ascontiguousarray(np.asarray(inputs["rwkv_w_up"], f32)),
            "aup": np.ascontiguousarray(np.asarray(inputs["rwkv_a_up"], f32)),
            "gup": np.ascontiguousarray(np.asarray(inputs["rwkv_g_up"], f32)),
            "rconst": rwkv_consts()}


NEG = -1e30


def att_consts():
    t = np.arange(SEQ)
    row = (t // 64).astype(np.float32)
    col = (t % 64).astype(np.float32)
    inv = (10000.0 ** (-np.arange(16, dtype=np.float32) / 16)).astype(np.float32)
    tab = np.zeros((64, 2, SEQ), np.float32)
    for dd in range(64):
        half = dd // 32
        i = dd % 32
        pos = row if half == 0 else col
        fi = i % 16
        ang = (pos * inv[fi]).astype(np.float32)
        tab[dd, 0] = np.cos(ang)
        tab[dd, 1] = (-np.sin(ang)) if i < 16 else np.sin(ang)
    i = np.arange(128)[:, None]
    j = np.arange(384)[None, :]
    band = np.where((j >= i) & (j <= i + 256), 0.0, NEG).astype(np.float32)
    return tab, band


class MKA(MKR):
    def declare_att(self):
        self.rope_d = self.dram("rope", [64, 2, SEQ], F32, kind="ExternalInput")
        self.band_d = self.dram("band", [128, 384], F32, kind="ExternalInput")
        self.sink_d = self.dram("sinkb", [DEPTH, 128, 8], F32, kind="ExternalInput")

    def attention(self, l):
        k = self
        scale = 0.125
        Qb = k.sb("Qb", [64, 8, T], BF16)
        Kb = k.sb("Kb", [64, 2, T], BF16)
        Vr = k.sb("Vr", [128, T // 128, 128], BF16)
        Qbb = [Buf() for _ in TBS]
        Kbb = [Buf() for _ in TBS]
        Vrb = Buf()
        k.dma("sp", Vr[:], k.VTM[:, :].rearrange("(n p) c -> p n c", p=128), reads=k.VTMb, writes=[Vrb])
        band = k.sb("band_t", [128, 384], F32)
        bandb = Buf()
        k.dma("sp", band[:], k.band_d[:], writes=[bandb])
        sink = k.sb("sink_t", [128, 8], F32)
        sinkb = Buf()
        k.dma("sp", sink[:], k.sink_d[l], writes=[sinkb])
        identb = k.sb("identb", [128, 128], BF16)
        identbb = Buf()
        k.op("dve", lambda e: e.tensor_copy(out=identb[:], in_=k.ident), reads=[k.rconstb], writes=[identbb])
        for bi, (t0, tb) in enumerate(TBS):
            for (dst, dstb, c0, nh) in [(Qb, Qbb, 28, 8), (Kb, Kbb, 44, 2)]:
                raw, rawb = k.pool(f"araw{nh}", [64, nh, 512], F32, 1)
                k.dma("sp", raw[:, :, :tb], k.P64[c0:c0 + nh, :, t0:t0 + tb].rearrange("c p t -> p c t"),
                      reads=[b_ for c_ in range(c0, c0 + nh) for b_ in k.P64b[c_]], writes=[rawb])
                if bi == 0:
                    k.op("act", lambda e, raw=raw, dst=dst: e.copy(out=dst[:, :, t0:t0 + tb], in_=raw[:, :, :tb]),
                         reads=[rawb], writes=[dstb[bi]])
                    continue
                sw, swb = k.pool(f"asw{nh}", [64, nh, 512], F32, 1)
                k.dma("sp", sw[:, :, :tb], k.P64[c0 + nh:c0 + 2 * nh, :, t0:t0 + tb].rearrange("c p t -> p c t"),
                      reads=[b_ for c_ in range(c0 + nh, c0 + 2 * nh) for b_ in k.P64b[c_]], writes=[swb])
                tab, tabb = k.pool("ropetab", [64, 2, 512], F32, 2)
                k.dma("sp", tab[:, :, :tb], k.rope_d[:, :, t0 - CTX:t0 - CTX + tb], writes=[tabb])
                k.op("dve", lambda e, raw=raw, tab=tab, nh=nh: e.tensor_tensor(
                    out=raw[:, :, :tb], in0=raw[:, :, :tb], in1=tab[:, 0:1, :tb].to_broadcast([64, nh, tb]), op=ALU.mult),
                    reads=[rawb, tabb], writes=[rawb])
                k.op("pool", lambda e, sw=sw, tab=tab, nh=nh: e.tensor_tensor(
                    out=sw[:, :, :tb], in0=sw[:, :, :tb], in1=tab[:, 1:2, :tb].to_broadcast([64, nh, tb]), op=ALU.mult),
                    reads=[swb, tabb], writes=[swb])
                k.op("dve", lambda e, raw=raw, sw=sw, dst=dst: e.tensor_tensor(
                    out=dst[:, :, t0:t0 + tb], in0=raw[:, :, :tb], in1=sw[:, :, :tb], op=ALU.add),
                    reads=[rawb, swb], writes=[dstb[bi]])
        allq = Qbb + Kbb
        yield
        for qt in range(T // 128):
            yield
            q0 = qt * 128
            is_ctx = qt < 2
            if is_ctx:
                lat_tiles = []
            else:
                n = qt - 2
                lat_tiles = [tt for tt in (n - 1, n, n + 1) if 0 <= tt < 16]
            jlo = 0
            if not is_ctx:
                jlo = (lat_tiles[0] - (n - 1)) * 128
            nb = len(lat_tiles) * 128
            key_tiles = [2 + tt for tt in lat_tiles] + [0, 1]
            ob_t, ob_b = k.psum_hold()
            rinv, rinvb = k.pool("a_rinv", [128, 8], F32, 2)
            for h in range(8):
                g = h // 4
                s, sb_ = k.pool("a_s", [128, 640], F32, 2)
                if nb < 384:
                    k.op("pool", lambda e, s=s: e.memset(s[:, 0:384], NEG), writes=[sb_])
                if nb > 0:
                    pa, pab = k.psum()
                    k0 = CTX + lat_tiles[0] * 128
                    k.op("pe", lambda e, pa=pa, h=h, g=g, k0=k0: e.matmul(
                        pa[:, :nb], lhsT=Qb[:, h, q0:q0 + 128], rhs=Kb[:, g, k0:k0 + nb], start=True, stop=True),
                        reads=allq, writes=[pab])
                    k.op("dve", lambda e, pa=pa, s=s: e.tensor_tensor(
                        out=s[:, jlo:jlo + nb], in0=pa[:, :nb], in1=band[:, jlo:jlo + nb], op=ALU.add),
                        reads=[pab, bandb], writes=[sb_])
                pc_, pcb = k.psum()
                k.op("pe", lambda e, pc_=pc_, h=h, g=g: e.matmul(
                    pc_[:, :256], lhsT=Qb[:, h, q0:q0 + 128], rhs=Kb[:, g, 0:256], start=True, stop=True),
                    reads=allq, writes=[pcb])
                k.op("act", lambda e, pc_=pc_, s=s: e.copy(out=s[:, 384:640], in_=pc_[:, :256]),
                     reads=[pcb], writes=[sb_])
                st, stb = k.pool("a_stat", [128, 4], F32, 4)
                k.op("dve", lambda e, s=s, st=st: e.reduce_max(out=st[:, 0:1], in_=s[:, :], axis=AX.X),
                     reads=[sb_], writes=[stb])
                k.op("dve", lambda e, st=st: e.tensor_scalar(out=st[:, 1:2], in0=st[:, 0:1], scalar1=-scale, scalar2=None,
                                                           op0=ALU.mult), reads=[stb], writes=[stb])
                P, Pb_ = k.pool("a_P", [128, 640], BF16, 2)
                k.op("act", lambda e, s=s, st=st, P=P: e.activation(
                    out=P[:], in_=s[:], func=AF.Exp, bias=st[:, 1:2], scale=scale, accum_out=st[:, 2:3]),
                    reads=[sb_, stb], writes=[Pb_, stb])
                k.op("act", lambda e, st=st, h=h: e.activation(
                    out=st[:, 3:4], in_=st[:, 1:2], func=AF.Exp, bias=sink[:, h:h + 1], scale=1.0),
                    reads=[stb, sinkb], writes=[stb])
                k.op("dve", lambda e, st=st: e.tensor_tensor(out=st[:, 2:3], in0=st[:, 2:3], in1=st[:, 3:4], op=ALU.add),
                     reads=[stb], writes=[stb])
                k.op("dve", lambda e, st=st, h=h: e.reciprocal(out=rinv[:, h:h + 1], in_=st[:, 2:3]),
                     reads=[stb], writes=[rinvb])
                pt_, ptb = k.psum()
                ptv = pt_[:].bitcast(BF16)
                cols = [jlo // 128 + i for i in range(len(lat_tiles))] + [3, 4]
                for ci, cb in enumerate(cols):
                    k.op("pe", lambda e, ptv=ptv, P=P, ci=ci, cb=cb: e.transpose(
                        ptv[:, ci * 128:(ci + 1) * 128], P[:, cb * 128:(cb + 1) * 128], identb[:]),
                        reads=[Pb_, identbb], writes=[ptb])
                nk = len(cols)
                PT, PTb = k.pool("a_PT", [128, 5, 128], BF16, 2)
                k.op("act" if h % 2 == 0 else "dve",
                     (lambda e, ptv=ptv, PT=PT, nk=nk: e.copy(out=PT[:, :nk, :].rearrange("p a b -> p (a b)"), in_=ptv[:, :nk * 128]))
                     if h % 2 == 0 else
                     (lambda e, ptv=ptv, PT=PT, nk=nk: e.tensor_copy(out=PT[:, :nk, :].rearrange("p a b -> p (a b)"), in_=ptv[:, :nk * 128])),
                     reads=[ptb], writes=[PTb])
                for ci, kt in enumerate(key_tiles):
                    k.op("pe", lambda e, PT=PT, ci=ci, kt=kt, h=h, g=g: e.matmul(
                        ob_t[:, h * 64:(h + 1) * 64], lhsT=PT[:, ci, :], rhs=Vr[:, kt, g * 64:(g + 1) * 64],
                        start=(ci == 0), stop=(ci == nk - 1)), reads=[PTb, Vrb], writes=[ob_b])
            o_tm, o_tmb = k.pool("a_otm", [128, 8, 64], BF16, 2)
            k.op("dve", lambda e, o_tm=o_tm, rinv=rinv: e.tensor_tensor(
                out=o_tm[:], in0=ob_t[:, :].rearrange("p (h d) -> p h d", d=64),
                in1=rinv[:, :].unsqueeze(2).to_broadcast([128, 8, 64]), op=ALU.mult),
                reads=[ob_b, rinvb], writes=[o_tmb])
            pt_, ptb = k.psum()
            ptv = pt_[:].bitcast(BF16)
            for c4 in range(4):
                k.op("pe", lambda e, ptv=ptv, o_tm=o_tm, c4=c4: e.transpose(
                    ptv[:, c4 * 128:(c4 + 1) * 128],
                    o_tm[:, 2 * c4:2 * c4 + 2, :].rearrange("p h d -> p (h d)"), identb[:]),
                    reads=[o_tmb, identbb], writes=[ptb])
            o_fm, o_fmb = k.pool("a_ofm", [128, 4, 128], BF16, 2)
            k.op("act", lambda e, ptv=ptv, o_fm=o_fm: e.copy(out=o_fm[:].rearrange("p a b -> p (a b)"), in_=ptv[:, :512]),
                 reads=[ptb], writes=[o_fmb])
            k.dma("sp", k.FEATS[8:12, :, q0:q0 + 128].rearrange("c p t -> p c t"), o_fm[:],
                  reads=[o_fmb], writes=[k.FEATSb[2][qt]])


def att_host_prep(inputs):
    tab, band = att_consts()
    sink = np.asarray(inputs["att_sink"], np.float32)
    return {"rope": tab, "band": band,
            "sinkb": np.ascontiguousarray(np.broadcast_to(sink[:, None, :], (DEPTH, 128, 8)))}


def fno_consts():
    c = np.arange(128)
    ang = 2 * np.pi * np.outer(c, c) / 128.0
    c128 = np.concatenate([np.cos(ang), np.sin(ang)], axis=1) / np.sqrt(128.0)
    tabs = {}
    for L in (SEQ, CTX):
        l = np.arange(L, dtype=np.int64)
        m = np.outer(l, l) % L
        a = 2 * np.pi * m / L
        tabs[L] = (np.stack([np.cos(a), -np.sin(a)], axis=1) / np.sqrt(L))
    return (c128.astype(ml_dtypes.bfloat16), tabs[SEQ].astype(ml_dtypes.bfloat16), tabs[CTX].astype(ml_dtypes.bfloat16))


class MKC(MKA):
    def declare_cf(self):
        self.convp_d = self.dram("convp", [DEPTH, 128, 4, 34], F32, kind="ExternalInput")
        self.c128_d = self.dram("c128", [128, 256], BF16, kind="ExternalInput")
        self.fL_d = self.dram("fL", [SEQ, 2, SEQ], BF16, kind="ExternalInput")
        self.fC_d = self.dram("fC", [CTX, 2, CTX], BF16, kind="ExternalInput")

    def conv(self, l):
        k = self
        cp = k.sb("convp_t", [128, 4, 34], F32)
        cpb = Buf()
        k.dma("sp", cp[:], k.convp_d[l], writes=[cpb])
        for (s0, L) in [(0, CTX), (CTX, SEQ)]:
            co = k.sb(f"convo{L}", [128, 4, L], F32)
            cob = [Buf() for _ in range(4)]
            for j in range(4):
                yield
                at, ab = k.pool(f"cv_a{L}", [128, L], F32, 1)
                bt, bb = k.pool(f"cv_b{L}", [128, L], F32, 1)
                k.dma("sp", at[:], k.P128[5 + j, :, s0:s0 + L], reads=k.P128b[5 + j], writes=[ab])
                k.dma("sp", bt[:], k.P128[9 + j, :, s0:s0 + L], reads=k.P128b[9 + j], writes=[bb])
                k.op("act", lambda e, bt=bt: e.activation(out=bt[:], in_=bt[:], func=AF.Sigmoid), reads=[bb], writes=[bb])
                hp, hpb = k.pool(f"cv_h{L}", [128, L + 30], F32, 2)
                k.op("pool", lambda e, hp=hp: e.memset(hp[:, 0:15], 0.0), writes=[hpb])
                k.op("pool", lambda e, hp=hp: e.memset(hp[:, L + 15:L + 30], 0.0), writes=[hpb])
                k.op("pool", lambda e, hp=hp, at=at, bt=bt: e.tensor_tensor(out=hp[:, 15:15 + L], in0=at[:], in1=bt[:], op=ALU.mult),
                     reads=[ab, bb], writes=[hpb])
                k.op("dve", lambda e, hp=hp, j=j: e.tensor_scalar(
                    out=co[:, j, :], in0=hp[:, 0:L], scalar1=cp[:, j, 0:1], scalar2=cp[:, j, 31:32],
                    op0=ALU.mult, op1=ALU.add), reads=[hpb, cpb], writes=[cob[j]])
                for tap in range(1, 31):
                    k.op("dve", lambda e, hp=hp, j=j, tap=tap: e.scalar_tensor_tensor(
                        out=co[:, j, :], in0=hp[:, tap:tap + L], scalar=cp[:, j, tap:tap + 1], in1=co[:, j, :],
                        op0=ALU.mult, op1=ALU.add), reads=[hpb, cpb, cob[j]], writes=[cob[j]])
            for t0 in range(0, L, 512):
                yield
                tb = min(512, L - t0)
                pm, pmb = k.psum()
                for j in range(4):
                    k.op("pe", lambda e, pm=pm, j=j: e.matmul(pm[:, :tb], lhsT=k.ones_f[:], rhs=co[:, j, t0:t0 + tb],
                                                             start=(j == 0), stop=(j == 3)), reads=[cob[j], k.ones_fb], writes=[pmb])
                for j in range(4):
                    k.op("dve", lambda e, pm=pm, j=j: e.scalar_tensor_tensor(
                        out=co[:, j, t0:t0 + tb], in0=pm[:, :tb], scalar=-1.0 / 512, in1=co[:, j, t0:t0 + tb],
                        op0=ALU.mult, op1=ALU.add), reads=[pmb, cob[j]], writes=[cob[j]])
                pv, pvb = k.psum()
                for j in range(4):
                    sq, sqb = k.pool("cv_sq", [128, 512], F32, 2)
                    k.op("act", lambda e, sq=sq, j=j: e.activation(out=sq[:, :tb], in_=co[:, j, t0:t0 + tb], func=AF.Square),
                         reads=[cob[j]], writes=[sqb])
                    k.op("pe", lambda e, pv=pv, sq=sq, j=j: e.matmul(pv[:, :tb], lhsT=k.ones_f[:], rhs=sq[:, :tb],
                                                                    start=(j == 0), stop=(j == 3)), reads=[sqb, k.ones_fb], writes=[pvb])
                rs, rsb = k.pool("cv_rs", [128, 512], F32, 2)
                k.op("act", lambda e, rs=rs, pv=pv: e.activation(out=rs[:, :tb], in_=pv[:, :tb], func=AF.Sqrt,
                                                                scale=1.0 / 512, bias=k.epsc[:, 2:3]), reads=[pvb, k.epsb], writes=[rsb])
                k.op("dve", lambda e, rs=rs: e.reciprocal(out=rs[:, :tb], in_=rs[:, :tb]), reads=[rsb], writes=[rsb])
                for j in range(4):
                    y, yb = k.pool("cv_y", [128, 512], F32, 2)
                    k.op("dve", lambda e, y=y, rs=rs, j=j: e.tensor_tensor(out=y[:, :tb], in0=co[:, j, t0:t0 + tb], in1=rs[:, :tb],
                                                                          op=ALU.mult), reads=[cob[j], rsb], writes=[yb])
                    k.op("act", lambda e, y=y, j=j: e.activation(out=y[:, :tb], in_=y[:, :tb], func=AF.Identity,
                                                                 scale=cp[:, j, 32:33], bias=cp[:, j, 33:34]), reads=[yb, cpb], writes=[yb])
                    sg, sgb = k.pool("cv_sg", [128, 512], F32, 2)
                    k.op("act", lambda e, y=y, sg=sg: e.activation(out=sg[:, :tb], in_=y[:, :tb], func=AF.Sigmoid),
                         reads=[yb], writes=[sgb])
                    o, ob = k.pool("cv_o", [128, 512], BF16, 2)
                    k.op("pool", lambda e, y=y, sg=sg, o=o: e.tensor_tensor(out=o[:, :tb], in0=y[:, :tb], in1=sg[:, :tb], op=ALU.mult),
                         reads=[yb, sgb], writes=[ob])
                    tt0 = s0 + t0
                    k.dma("sp", k.FEATS[12 + j, :, tt0:tt0 + tb], o[:, :tb], reads=[ob],
                          writes=[k.FEATSb[3][(tt0 // 128) + i] for i in range(tb // 128)])

    def fno(self, l, with_ctx=True):
        k = self
        c128 = k.sb("c128_t", [128, 256], BF16)
        c128b = Buf()
        k.dma("sp", c128[:], k.c128_d[:], writes=[c128b])
        for (s0, L, ftab) in [(0, CTX, k.fC_d), (CTX, SEQ, k.fL_d)]:
            if L == CTX and not with_ctx:
                continue
            nt = L // 128
            A = k.sb(f"fnoA{L}", [128, nt, 4, 256], BF16)
            Ab = Buf()
            for g in range(4):
                yield
                u, ub = k.pool(f"fno_u{L}", [128, L], F32, 1)
                k.dma("sp", u[:], k.P128[1 + g, :, s0:s0 + L], reads=k.P128b[1 + g], writes=[ub])
                ubf, ubfb = k.pool(f"fno_ub{L}", [128, L], BF16, 2)
                k.op("act", lambda e, u=u, ubf=ubf: e.copy(out=ubf[:], in_=u[:]), reads=[ub], writes=[ubfb])
                for ti in range(nt):
                    pt, pb = k.psum()
                    k.op("pe", lambda e, pt=pt, ubf=ubf, ti=ti: e.matmul(
                        pt[:, :256], lhsT=ubf[:, ti * 128:(ti + 1) * 128], rhs=c128[:], start=True, stop=True),
                        reads=[ubfb, c128b], writes=[pb])
                    if ti % 2 == 0:
                        k.op("act", lambda e, pt=pt, ti=ti, g=g: e.copy(out=A[:, ti, g, :], in_=pt[:, :256]), reads=[pb], writes=[Ab])
                    else:
                        k.op("dve", lambda e, pt=pt, ti=ti, g=g: e.tensor_copy(out=A[:, ti, g, :], in_=pt[:, :256]), reads=[pb], writes=[Ab])
            for m0 in range(0, L, 512):
                mb = min(512, L - m0)
                F, Fb = k.pool(f"fno_F{L}", [128, nt, 2, mb], BF16, 1)
                for hlf in range(2):
                    n0, n1 = (hlf * nt) // 2, ((hlf + 1) * nt) // 2
                    for s in range(2):
                        k.dma("sp", F[:, n0:n1, s, :], ftab[n0 * 128:n1 * 128, s, m0:m0 + mb].rearrange("(n p) m -> p n m", p=128),
                              writes=[Fb])
                for g in range(4):
                    yield
                    pt, pb = k.psum()
                    for ti in range(nt):
                        for s in range(2):
                            k.op("pe", lambda e, pt=pt, F=F, ti=ti, s=s, g=g: e.matmul(
                                pt[:, :mb], lhsT=A[:, ti, g, s * 128:(s + 1) * 128], rhs=F[:, ti, s, :],
                                start=(ti == 0 and s == 0), stop=(ti == nt - 1 and s == 1)), reads=[Ab, Fb], writes=[pb])
                    o, ob = k.pool("fno_o", [128, 512], BF16, 2)
                    k.op("act" if g % 2 == 0 else "dve",
                         (lambda e, pt=pt, o=o: e.copy(out=o[:, :mb], in_=pt[:, :mb])) if g % 2 == 0 else
                         (lambda e, pt=pt, o=o: e.tensor_copy(out=o[:, :mb], in_=pt[:, :mb])), reads=[pb], writes=[ob])
                    tt0 = s0 + m0
                    k.dma("sp", k.FEATS[g, :, tt0:tt0 + mb], o[:, :mb], reads=[ob],
                          writes=[k.FEATSb[0][(tt0 // 128) + i] for i in range(mb // 128)])


def cf_host_prep(inputs):
    f32 = np.float32
    cp = np.zeros((DEPTH, 128, 4, 34), f32)
    for l in range(DEPTH):
        dw = np.asarray(inputs["conv_dw"], f32)[l]
        cp[l, :, :, 0:31] = dw.T.reshape(4, 128, 31).transpose(1, 0, 2)
        cp[l, :, :, 31] = np.asarray(inputs["conv_dw_b"], f32)[l].reshape(4, 128).T
        cp[l, :, :, 32] = np.asarray(inputs["conv_ln_g"], f32)[l].reshape(4, 128).T
        cp[l, :, :, 33] = np.asarray(inputs["conv_ln_b"], f32)[l].reshape(4, 128).T
    c128, fL, fC = fno_consts()
    return {"convp": cp, "c128": c128, "fL": fL, "fC": fC}


class MKB(MKC):
    def declare_b(self):
        self.wg_d = self.dram("wg", [DEPTH, 4, 16, 128, KC, 128], F32, kind="ExternalInput")
        self.wb_d = self.dram("wbr", [DEPTH, 4, 16, 128, 4, 128], F32, kind="ExternalInput")
        self.wo_d = self.dram("wo", [DEPTH, 16, 128, KC, 128], F32, kind="ExternalInput")
        self.w1_d = self.dram("w1", [DEPTH, 64, 128, KC, 128], F32, kind="ExternalInput")
        self.w2_d = self.dram("w2", [DEPTH, 16, 4, 128, KC, 128], F32, kind="ExternalInput")
        self.XS = self.dram("XS", [KC, 128, T], F32)
        self.XSb = [Buf() for _ in TBS]
        self.yout = self.dram("yout", [KC, 128, SEQ], F32, kind="ExternalOutput")
        self.youtb = Buf()

    def wload(self, src, kcn=KC):
        wt, wb = self.pool(f"wB{kcn}", [128, kcn, 128], BF16, 4)
        self.dma("pool", wt[:], src, writes=[wb])
        return wt, wb

    def phase_b(self, l, xsrc, xsrc_bufs, last):
        k = self
        mod = k.mod[l]
        import os
        only = os.environ.get('PB_ONLY')
        for bi, (t0, tb) in enumerate(TBS):
            if bi == 0 and last:
                continue
            if only is not None and bi != int(only):
                continue
            ci = 1 if bi == 0 else 0
            xt, xb = k.pool("xblk", [128, KC, 512], F32, 1)
            for q in range(4):
                k.dma("sp", xt[:, 4 * q:4 * q + 4, :tb],
                      xsrc[4 * q:4 * q + 4, :, t0:t0 + tb].rearrange("k p t -> p k t"),
                      reads=[xsrc_bufs[bi]], writes=[xb])
            hx, hxb = k.pool("b_hx", [128, KC, 512], BF16, 1)
            ft, ftb = k.pool("b_ft", [128, KC, 512], BF16, 1)
            k.dma("sp", hx[:, :, :tb], k.HX[:, :, t0:t0 + tb].rearrange("k p t -> p k t"), reads=[k.HXb[bi]], writes=[hxb])
            fr = [b_ for br in range(4) for b_ in k.FEATSb[br][t0 // 128:(t0 + tb) // 128]]
            for q in range(4):
                k.dma("sp", ft[:, 4 * q:4 * q + 4, :tb], k.FEATS[4 * q:4 * q + 4, :, t0:t0 + tb].rearrange("k p t -> p k t"),
                      reads=fr, writes=[ftb])
            mt, mb_ = k.pool("b_m", [128, KC, 512], BF16, 1)
            for dc in range(16):
                macc, maccb = k.pool("b_macc", [128, 512], F32, 2)
                for i in range(4):
                    wg, wgb = k.wload(k.wg_d[l, i, dc])
                    wbt, wbb = k.wload(k.wb_d[l, i, dc], 4)
                    pg, pgb = k.psum()
                    for kc in range(KC):
                        k.op("pe", lambda e, pg=pg, wg=wg, kc=kc: e.matmul(pg[:, :tb], lhsT=wg[:, kc, :], rhs=hx[:, kc, :tb],
                                                                          start=(kc == 0), stop=(kc == KC - 1)),
                             reads=[wgb, hxb], writes=[pgb])
                    pp, ppb = k.psum()
                    for kc in range(4):
                        k.op("pe", lambda e, pp=pp, wbt=wbt, kc=kc, i=i: e.matmul(pp[:, :tb], lhsT=wbt[:, kc, :], rhs=ft[:, 4 * i + kc, :tb],
                                                                                 start=(kc == 0), stop=(kc == 3)),
                             reads=[wbb, ftb], writes=[ppb])
                    sg, sgb = k.pool("b_sig", [128, 512], F32, 2)
                    k.op("act", lambda e, pg=pg, sg=sg: e.activation(out=sg[:, :tb], in_=pg[:, :tb], func=AF.Sigmoid),
                         reads=[pgb], writes=[sgb])
                    if i == 0:
                        k.op("dve", lambda e, pp=pp, sg=sg, macc=macc: e.tensor_tensor(out=macc[:, :tb], in0=pp[:, :tb], in1=sg[:, :tb], op=ALU.mult),
                             reads=[ppb, sgb], writes=[maccb])
                    else:
                        k.op("dve", lambda e, pp=pp, sg=sg: e.tensor_tensor(out=sg[:, :tb], in0=pp[:, :tb], in1=sg[:, :tb], op=ALU.mult),
                             reads=[ppb, sgb], writes=[sgb])
                        if i < 3:
                            k.op("dve", lambda e, sg=sg, macc=macc: e.tensor_tensor(out=macc[:, :tb], in0=macc[:, :tb], in1=sg[:, :tb], op=ALU.add),
                                 reads=[maccb, sgb], writes=[maccb])
                        else:
                            k.op("dve", lambda e, sg=sg, macc=macc, dc=dc: e.tensor_tensor(out=mt[:, dc, :tb], in0=macc[:, :tb], in1=sg[:, :tb], op=ALU.add),
                                 reads=[maccb, sgb], writes=[mb_])
            for dc in range(16):
                wo, wob = k.wload(k.wo_d[l, dc])
                py, pyb = k.psum()
                for kc in range(KC):
                    k.op("pe", lambda e, py=py, wo=wo, kc=kc: e.matmul(py[:, :tb], lhsT=wo[:, kc, :], rhs=mt[:, kc, :tb],
                                                                      start=(kc == 0), stop=(kc == KC - 1)),
                         reads=[wob, mb_], writes=[pyb])
                k.op("dve", lambda e, py=py, dc=dc: e.scalar_tensor_tensor(
                    out=xt[:, dc, :tb], in0=py[:, :tb], scalar=mod[:, 2 * KC + dc, ci:ci + 1], in1=xt[:, dc, :tb],
                    op0=ALU.mult, op1=ALU.add), reads=[pyb, xb, k.modb[l]], writes=[xb])
            k.norm_block(l, 1, xt, xb, t0, tb, bi == 0, hx, hxb, sq_tile=(ft, ftb))
            hm, hmb = k.pool("b_hmid", [128, 64, 512], BF16, 1)
            for fc in range(64):
                w1, w1b = k.wload(k.w1_d[l, fc])
                ph, phb = k.psum()
                for kc in range(KC):
                    k.op("pe", lambda e, ph=ph, w1=w1, kc=kc: e.matmul(ph[:, :tb], lhsT=w1[:, kc, :], rhs=hx[:, kc, :tb],
                                                                      start=(kc == 0), stop=(kc == KC - 1)),
                         reads=[w1b, hxb], writes=[phb])
                rl, rlb = k.pool("b_relu", [128, 512], F32, 2)
                k.op("act", lambda e, ph=ph, rl=rl: e.activation(out=rl[:, :tb], in_=ph[:, :tb], func=AF.Relu),
                     reads=[phb], writes=[rlb])
                k.op("dve", lambda e, rl=rl, fc=fc: e.tensor_tensor(out=hm[:, fc, :tb], in0=rl[:, :tb], in1=rl[:, :tb], op=ALU.mult),
                     reads=[rlb], writes=[hmb])
            for dc in range(16):
                py, pyb = k.psum()
                for q4 in range(4):
                    w2, w2b = k.wload(k.w2_d[l, dc, q4])
                    for kc in range(KC):
                        k.op("pe", lambda e, py=py, w2=w2, kc=kc, q4=q4: e.matmul(
                            py[:, :tb], lhsT=w2[:, kc, :], rhs=hm[:, q4 * KC + kc, :tb],
                            start=(q4 == 0 and kc == 0), stop=(q4 == 3 and kc == KC - 1)),
                            reads=[w2b, hmb], writes=[pyb])
                k.op("dve", lambda e, py=py, dc=dc: e.scalar_tensor_tensor(
                    out=xt[:, dc, :tb], in0=py[:, :tb], scalar=mod[:, 5 * KC + dc, ci:ci + 1], in1=xt[:, dc, :tb],
                    op0=ALU.mult, op1=ALU.add), reads=[pyb, xb, k.modb[l]], writes=[xb])
            if not last:
                for q in range(4):
                    k.dma("sp", k.XS[4 * q:4 * q + 4, :, t0:t0 + tb].rearrange("k p t -> p k t"), xt[:, 4 * q:4 * q + 4, :tb],
                          reads=[xb], writes=[k.XSb[bi]])
            else:
                k.final_norm(xt, xb, t0, tb, sq_tile=(ft, ftb))

    def final_norm(self, xt, xb, t0, tb, sq_tile=None):
        k = self
        sq, sqb = sq_tile if sq_tile is not None else k.pool("sq", [128, KC, 512], BF16, 1)
        k.op("act", lambda e: e.activation(out=sq[:, :, :tb], in_=xt[:, :, :tb], func=AF.Square), reads=[xb], writes=[sqb])
        pt, pb = k.psum()
        for kc in range(KC):
            k.op("pe", lambda e, kc=kc: e.matmul(pt[:, :tb], lhsT=k.ones_bf[:], rhs=sq[:, kc, :tb],
                                                 start=(kc == 0), stop=(kc == KC - 1)), reads=[sqb, k.ones_b], writes=[pb])
        rs, rsb = k.pool("rstd", [128, 512], F32, 2)
        k.op("act", lambda e: e.activation(out=rs[:, :tb], in_=pt[:, :tb], func=AF.Sqrt, scale=1.0 / D, bias=k.epsc[:, 0:1]),
             reads=[pb, k.epsb], writes=[rsb])
        k.op("dve", lambda e: e.reciprocal(out=rs[:, :tb], in_=rs[:, :tb]), reads=[rsb], writes=[rsb])
        for kc in range(KC):
            k.op("dve", lambda e, kc=kc: e.scalar_tensor_tensor(
                out=xt[:, kc, :tb], in0=xt[:, kc, :tb], scalar=k.ngt[:, 4, kc:kc + 1], in1=rs[:, :tb],
                op0=ALU.mult, op1=ALU.mult), reads=[xb, rsb, k.ngb], writes=[xb])
        for q in range(4):
            k.dma("sp", k.yout[4 * q:4 * q + 4, :, t0 - CTX:t0 - CTX + tb].rearrange("k p t -> p k t"), xt[:, 4 * q:4 * q + 4, :tb],
                  reads=[xb], writes=[k.youtb])

    def build_all(self):
        k = self
        k.declare_inputs()
        k.declare_rwkv()
        k.declare_att()
        k.declare_cf()
        k.declare_b()
        k.setup_eps()
        k.setup_consts()
        k.rwkv_setup()
        xsrc, xbufs = k.xin, [Buf() for _ in TBS]
        for l in range(DEPTH):
            last = (l == DEPTH - 1)
            with k.scope():
                k.phase_mod(l)
            with k.scope():
                k.phase_a(l, xsrc, xbufs)
            with k.scope():
                k.rwkv(l)
            with k.scope():
                k.run_gens([k.rwkv_readout_gen(l), k.attention(l)])
            with k.scope():
                k.run_gens([k.conv(l), k.fno(l, with_ctx=not last)])
            with k.scope():
                k.phase_b(l, xsrc, xbufs, last)
            xsrc, xbufs = k.XS, k.XSb
        k.finish([k.youtb])


def b_host_prep(inputs):
    f32 = np.float32
    w_in = np.asarray(inputs["w_in"], f32)
    wbr = np.asarray(inputs["w_branch"], f32)
    wo = np.asarray(inputs["w_out"], f32)
    w1 = np.asarray(inputs["w_mlp1"], f32)
    w2 = np.asarray(inputs["w_mlp2"], f32)
    out = {}
    g = w_in[:, :, O_GATE:].reshape(DEPTH, KC, 128, 4, 16, 128)
    out["wg"] = np.ascontiguousarray(g.transpose(0, 3, 4, 2, 1, 5))
    b = wbr.reshape(DEPTH, 4, 4, 128, 16, 128)
    out["wbr"] = np.ascontiguousarray(b.transpose(0, 1, 4, 3, 2, 5))
    o = wo.reshape(DEPTH, KC, 128, 16, 128)
    out["wo"] = np.ascontiguousarray(o.transpose(0, 3, 2, 1, 4))
    a = w1.reshape(DEPTH, KC, 128, 64, 128)
    out["w1"] = np.ascontiguousarray(a.transpose(0, 3, 2, 1, 4))
    c = w2.reshape(DEPTH, 4, KC, 128, 16, 128)
    out["w2"] = np.ascontiguousarray(c.transpose(0, 4, 1, 3, 2, 5))
    return out


def full_host_prep(inputs):
    shared, per_core = host_prep(inputs)
    shared.update(rwkv_host_prep(inputs))
    shared.update(att_host_prep(inputs))
    shared.update(cf_host_prep(inputs))
    shared.update(b_host_prep(inputs))
    return shared, per_core


def kernel(**inputs):
    shared, per_core = full_host_prep(inputs)
    k = MKB()
    k.build_all()
    n = 8
    in_maps = []
    for i in range(n):
        m = dict(shared)
        m.update(per_core[i % len(per_core)])
        in_maps.append(m)
    res = run_bass_kernel_spmd(k.nc, in_maps, core_ids=list(range(n)))
    B = len(per_core)
    out = np.stack([np.asarray(res.results[b]["yout"]).reshape(D, SEQ).T for b in range(B)])
    return np.ascontiguousarray(out.astype(np.float32))
```

```python
import os
import ml_dtypes
from concourse.bass_utils import run_bass_kernel_spmd
import contextlib
import numpy as np
import concourse.bass as bass
import concourse.mybir as mybir

F32 = mybir.dt.float32
BF16 = mybir.dt.bfloat16
I32 = mybir.dt.int32
AF = mybir.ActivationFunctionType
ALU = mybir.AluOpType
AX = mybir.AxisListType

NDMA = 64


class Buf:
    __slots__ = ("name", "w", "r")

    def __init__(self, name=""):
        self.name = name
        self.w = None
        self.r = {}


class Ctx:
    def __init__(self):
        self.nc = bass.Bass("TRN2", target_bir_lowering=False)
        nc = self.nc
        self.stack = contextlib.ExitStack()
        self.eng = {"pe": nc.tensor, "act": nc.scalar, "dve": nc.vector, "pool": nc.gpsimd, "sp": nc.sync}
        self.semh = {}
        for e in ["pe", "act", "dve", "pool"]:
            self.semh[e] = self.stack.enter_context(nc.semaphore("s_" + e))
        self.cnt = {e: 0 for e in ["pe", "act", "dve", "pool"]}
        self.seen = {e: {} for e in self.eng}
        self.dcnt = [0] * NDMA
        for i in range(NDMA):
            self.semh[("d", i)] = self.stack.enter_context(nc.semaphore(f"s_d{i}"))
        self.dnext = 0
        self.dnext_sw = 0
        self.n_ops = 0
        self.n_waits = 0
        self.out_tokens = []

    def sb(self, name, shape, dtype=F32):
        self._uid = getattr(self, "_uid", 0) + 1
        nm = f"{name}_{self._uid}"
        st = getattr(self, "cur_stack", None)
        if st is not None:
            return st.enter_context(self.nc.sbuf_tensor(nm, list(shape), dtype))
        return self.nc.alloc_sbuf_tensor(nm, list(shape), dtype)

    def barrier(self):
        for e in ["pe", "act", "dve", "pool", "sp"]:
            deps = {}
            for o in ["pe", "act", "dve", "pool"]:
                if self.cnt[o] > 0:
                    deps[o] = self.cnt[o]
            for i in range(NDMA):
                if self.dcnt[i] > 0:
                    deps[("d", i)] = 16 * self.dcnt[i]
            if getattr(self, "cccnt", 0) > 0:
                deps["cc"] = 16 * self.cccnt
            eng = self.eng[e]
            for k, v in deps.items():
                if self.seen[e].get(k, 0) < v:
                    eng.wait_ge(self.semh[k], v)
                    self.seen[e][k] = v
                    self.n_waits += 1

    @contextlib.contextmanager
    def scope(self):
        st = contextlib.ExitStack()
        prev = getattr(self, "cur_stack", None)
        saved = dict(self.pools) if hasattr(self, "pools") else None
        self.cur_stack = st
        try:
            yield
        finally:
            self.barrier()
            st.close()
            self.cur_stack = prev
            if saved is not None:
                self.pools = saved

    def ps(self, name, shape, dtype=F32):
        return self.nc.alloc_psum_tensor(name, list(shape), dtype)

    def dram(self, name, shape, dtype=F32, kind=None):
        if kind is None:
            return self.nc.dram_tensor(name, list(shape), dtype)
        return self.nc.dram_tensor(name, list(shape), dtype, kind=kind)

    def _deps(self, reads, writes):
        deps = {}

        def add(tok):
            if tok is None:
                return
            k, v = tok
            if deps.get(k, 0) < v:
                deps[k] = v

        for b in reads:
            add(b.w)
        for b in writes:
            add(b.w)
            for k, v in b.r.items():
                add((k, v))
        return deps

    def _wait(self, e, deps):
        eng = self.eng[e]
        for k, v in deps.items():
            if e == "pe" and k == "pe":
                continue
            if self.seen[e].get(k, 0) < v:
                eng.wait_ge(self.semh[k], v)
                self.seen[e][k] = v
                self.n_waits += 1

    def _mark(self, tok, reads, writes):
        k, v = tok
        for b in reads:
            if b.r.get(k, 0) < v:
                b.r[k] = v
        for b in writes:
            b.w = tok
            b.r = {}

    def op(self, e, fn, reads=(), writes=()):
        self._wait(e, self._deps(reads, writes))
        ins = fn(self.eng[e])
        self.cnt[e] += 1
        ins.then_inc(self.semh[e], 1)
        self._mark((e, self.cnt[e]), reads, writes)
        self.n_ops += 1
        return ins

    def dma(self, q, out, in_, reads=(), writes=(), **kw):
        if q == "pool":
            i = NDMA // 2 + self.dnext_sw
            self.dnext_sw = (self.dnext_sw + 1) % (NDMA // 2)
        else:
            i = self.dnext
            self.dnext = (i + 1) % (NDMA // 2)
        key = ("d", i)
        deps = self._deps(reads, writes)
        if self.dcnt[i] > 0:
            v = 16 * self.dcnt[i]
            if deps.get(key, 0) < v:
                deps[key] = v
        self._wait(q, deps)
        ins = self.eng[q].dma_start(out=out, in_=in_, **kw)
        self.dcnt[i] += 1
        ins.then_inc(self.semh[key], 16)
        tok = (key, 16 * self.dcnt[i])
        self._mark(tok, reads, writes)
        self.n_ops += 1
        return tok

    def collective(self, kind, in_ap, out_ap, reads=(), writes=()):
        if not hasattr(self, "ccsem"):
            self.semh["cc"] = self.stack.enter_context(self.nc.semaphore("s_cc"))
            self.cccnt = 0
        deps = self._deps(reads, writes)
        self._wait("pool", deps)
        ins = self.eng["pool"].collective_compute(kind, ALU.bypass, replica_groups=[[0, 1], [2, 3], [4, 5], [6, 7]],
                                                  ins=[in_ap], outs=[out_ap])
        self.cccnt += 1
        ins.then_inc(self.semh["cc"], 16)
        self.ccsem = True
        tok = ("cc", 16 * self.cccnt)
        self._mark(tok, reads, writes)
        return tok

    def finish(self, bufs):
        deps = self._deps(bufs, ())
        self._wait("sp", deps)
        for e in ["pe", "act", "dve", "pool"]:
            if self.cnt[e] > 0 and self.seen["sp"].get(e, 0) < self.cnt[e]:
                self.eng["sp"].wait_ge(self.semh[e], self.cnt[e])
        for i in range(NDMA):
            if self.dcnt[i] > 0 and self.seen["sp"].get(("d", i), 0) < 16 * self.dcnt[i]:
                self.eng["sp"].wait_ge(self.semh[("d", i)], 16 * self.dcnt[i])
        if getattr(self, "cccnt", 0) > 0:
            self.eng["sp"].wait_ge(self.semh["cc"], 16 * self.cccnt)


D = 2048
KC = 16
SEQ = 2048
CTX = 256
T = SEQ + CTX
DEPTH = 2
O_RKV = 0
O_LORA = 1536
O_KV = 1792
O_G = 2048
O_Q = 2176
O_FNO = 2688
O_CONV = 3200
O_GATE = 4224
IN_W = 12416
N64 = 48
N128 = 13
TBS = [(0, 256), (256, 512), (768, 512), (1280, 512), (1792, 512)]
NORM_EPS = 1e-6


def rope_partner():
    p = np.zeros(64, np.int64)
    for d in range(64):
        half = (d // 32) * 32
        i = d - half
        p[d] = half + (i + 16 if i < 16 else i - 16)
    return p


def cols64(l=None):
    cols = []
    for h in range(8):
        cols.append(np.arange(O_RKV + h * 64, O_RKV + (h + 1) * 64))
    for h in range(8):
        cols.append(np.arange(512 + h * 64, 512 + (h + 1) * 64))
    for h in range(8):
        cols.append(np.arange(1024 + h * 64, 1024 + (h + 1) * 64))
    for d in range(2):
        lo = O_LORA + d * 128
        cols.append(np.arange(lo, lo + 64))
        cols.append(np.arange(lo + 64, lo + 128))
    pr = rope_partner()
    for h in range(8):
        cols.append(np.arange(O_Q + h * 64, O_Q + (h + 1) * 64))
    for h in range(8):
        cols.append(O_Q + h * 64 + pr)
    for g in range(2):
        cols.append(np.arange(O_KV + g * 64, O_KV + (g + 1) * 64))
    for g in range(2):
        cols.append(O_KV + g * 64 + pr)
    assert len(cols) == N64
    return cols


def cols128():
    cols = [np.arange(O_G, O_G + 128)]
    for j in range(4):
        cols.append(np.arange(O_FNO + j * 128, O_FNO + (j + 1) * 128))
    for j in range(8):
        cols.append(np.arange(O_CONV + j * 128, O_CONV + (j + 1) * 128))
    assert len(cols) == N128
    return cols


def wlayout(w):
    M = w.shape[1]
    return np.ascontiguousarray(w.reshape(KC, 128, M).transpose(1, 0, 2))


def colvec(v):
    return np.ascontiguousarray(v.reshape(-1, 128).T)


class MK(Ctx):
    def __init__(self, debug=()):
        super().__init__()
        self.debug = set(debug)
        self.dbg_out = {}
        self.psb = [(self.ps(f"psb{i}", [128, 512], F32), Buf(f"psb{i}")) for i in range(8)]
        self.psi = 0
        self.pools = {}

    def psum(self):
        t, b = self.psb[self.psi]
        self.psi = (self.psi + 1) % 6
        return t, b

    def run_gens(self, gens):
        gens = [g for g in gens if g is not None]
        while gens:
            for g_ in list(gens):
                try:
                    next(g_)
                except StopIteration:
                    gens.remove(g_)

    def psum_hold(self):
        self.psh = getattr(self, "psh", 0)
        t, b = self.psb[6 + self.psh]
        self.psh = 1 - self.psh
        return t, b

    def pool(self, name, shape, dtype, n):
        if name not in self.pools:
            self.pools[name] = [[(self.sb(f"{name}{i}", shape, dtype), Buf(f"{name}{i}")) for i in range(n)], 0]
        p = self.pools[name]
        t, b = p[0][p[1]]
        p[1] = (p[1] + 1) % len(p[0])
        return t, b

    def tap(self, name, src_ap, src_buf, shape, dtype=F32):
        if name not in self.debug:
            return
        if name not in self.dbg_out:
            self.dbg_out[name] = (self.dram("dbg_" + name, shape, dtype, kind="ExternalOutput"), Buf("dbg_" + name))
        return self.dbg_out[name]

    def declare_inputs(self):
        self.xin = self.dram("xin", [KC, 128, T], F32, kind="ExternalInput")
        self.ccol = self.dram("ccol", [128, KC, 2], F32, kind="ExternalInput")
        self.adaw = self.dram("adaw", [DEPTH, 96, 128, KC, 128], F32, kind="ExternalInput")
        self.adab = self.dram("adab", [DEPTH, 128, 96], F32, kind="ExternalInput")
        self.ng = self.dram("ng", [128, 5, KC], F32, kind="ExternalInput")
        self.wa64 = self.dram("wa64", [DEPTH, N64, 128, KC, 64], F32, kind="ExternalInput")
        self.wa128 = self.dram("wa128", [DEPTH, N128, 128, KC, 128], F32, kind="ExternalInput")
        self.wav = self.dram("wav", [DEPTH, 128, KC, 128], F32, kind="ExternalInput")
        self.HX = self.dram("HX", [KC, 128, T], BF16)
        self.HXb = [Buf(f"HX{i}") for i in range(len(TBS))]
        self.P64 = self.dram("P64", [N64, 64, T], F32)
        self.P64b = [[Buf() for _ in TBS] for _ in range(N64)]
        self.P128 = self.dram("P128", [N128, 128, T], F32)
        self.P128b = [[Buf() for _ in TBS] for _ in range(N128)]
        self.VTM = self.dram("VTM", [T, 128], BF16)
        self.VTMb = [Buf() for _ in range(T // 128)]

    def setup_consts(self):
        self.ones_bf = self.sb("ones_bf", [128, 128], BF16)
        self.ones_b = Buf("ones_bf")
        self.op("pool", lambda e: e.memset(self.ones_bf[:], 1.0), writes=[self.ones_b])
        self.ones_f = self.sb("ones_f", [128, 128], F32)
        self.ones_fb = Buf("ones_f")
        self.op("pool", lambda e: e.memset(self.ones_f[:], 1.0), writes=[self.ones_fb])
        self.ngt = self.sb("ngt", [128, 5, KC], F32)
        self.ngb = Buf("ng")
        self.dma("sp", self.ngt[:], self.ng[:], writes=[self.ngb])
        self.sc = self.sb("sc", [128, KC, 2], F32)
        self.scb = Buf("sc")
        self.dma("sp", self.sc[:], self.ccol[:], writes=[self.scb])
        self.op("act", lambda e: e.activation(out=self.sc[:], in_=self.sc[:], func=AF.Silu),
                reads=[self.scb], writes=[self.scb])
        self.mod = [self.sb(f"mod{l}", [128, 96, 2], F32) for l in range(DEPTH)]
        self.modb = [Buf(f"mod{l}") for l in range(DEPTH)]
        self.geff = [[self.sb(f"geff{l}_{n}", [128, KC, 2], F32) for n in range(2)] for l in range(DEPTH)]
        self.geffb = [[Buf() for n in range(2)] for l in range(DEPTH)]

    def phase_mod(self, l):
        pt, pb = self.psum()
        for fc in range(96):
            wt, wb = self.pool("adaw", [128, KC, 128], F32, 3)
            self.dma("sp", wt[:], self.adaw[l, fc], writes=[wb])
            for kc in range(KC):
                self.op("pe", lambda e, kc=kc, fc=fc, wt=wt: e.matmul(
                    pt[:, 2 * fc:2 * fc + 2], lhsT=wt[:, kc, :], rhs=self.sc[:, kc, :],
                    start=(kc == 0), stop=(kc == KC - 1)),
                    reads=[wb, self.scb], writes=[pb])
        bt, bb = self.pool("adab", [128, 96], F32, 1)
        self.dma("sp", bt[:], self.adab[l], writes=[bb])
        mod = self.mod[l]
        self.op("dve", lambda e: e.tensor_tensor(
            out=mod[:], in0=pt[:, 0:192].rearrange("p (f c) -> p f c", c=2),
            in1=bt[:].unsqueeze(2).to_broadcast([128, 96, 2]), op=ALU.add),
            reads=[pb, bb], writes=[self.modb[l]])
        self.phase_geff(l)

    def phase_geff(self, l):
        mod = self.mod[l]
        for n, (gi, sci) in enumerate([(l, 1), (2 + l, 4)]):
            ge = self.geff[l][n]
            self.op("dve", lambda e, ge=ge, gi=gi, sci=sci: e.scalar_tensor_tensor(
                out=ge[:], in0=mod[:, sci * KC:(sci + 1) * KC, :], scalar=1.0,
                in1=self.ngt[:, gi, :].unsqueeze(2).to_broadcast([128, KC, 2]),
                op0=ALU.add, op1=ALU.mult),
                reads=[self.modb[l], self.ngb], writes=[self.geffb[l][n]])

    def norm_block(self, l, n, xt, xb, t0, tb, is_ctx, out_t, out_b, sq_tile=None):
        ci = 1 if is_ctx else 0
        shi = 0 if n == 0 else 3
        sq, sqb = sq_tile if sq_tile is not None else self.pool("sq", [128, KC, 512], BF16, 1)
        self.op("act", lambda e: e.activation(out=sq[:, :, :tb], in_=xt[:, :, :tb], func=AF.Square),
                reads=[xb], writes=[sqb])
        pt, pb = self.psum()
        for kc in range(KC):
            self.op("pe", lambda e, kc=kc: e.matmul(pt[:, :tb], lhsT=self.ones_bf[:], rhs=sq[:, kc, :tb],
                                                   start=(kc == 0), stop=(kc == KC - 1)),
                    reads=[sqb, self.ones_b], writes=[pb])
        rs, rsb = self.pool("rstd", [128, 512], F32, 2)
        self.op("act", lambda e: e.activation(out=rs[:, :tb], in_=pt[:, :tb], func=AF.Sqrt,
                                              scale=1.0 / D, bias=self.epsc[:, 0:1]),
                reads=[pb, self.epsb], writes=[rsb])
        self.op("dve", lambda e: e.reciprocal(out=rs[:, :tb], in_=rs[:, :tb]), reads=[rsb], writes=[rsb])
        ge = self.geff[l][n]
        mod = self.mod[l]
        for kc in range(KC):
            tmp, tmpb = self.pool("ntmp", [128, 512], F32, 3)
            self.op("dve", lambda e, kc=kc, tmp=tmp: e.scalar_tensor_tensor(
                out=tmp[:, :tb], in0=xt[:, kc, :tb], scalar=ge[:, kc, ci:ci + 1], in1=rs[:, :tb],
                op0=ALU.mult, op1=ALU.mult),
                reads=[xb, rsb, self.geffb[l][n]], writes=[tmpb])
            self.op("act", lambda e, kc=kc, tmp=tmp: e.activation(
                out=out_t[:, kc, :tb], in_=tmp[:, :tb], func=AF.Identity,
                bias=mod[:, shi * KC + kc, ci:ci + 1], scale=1.0),
                reads=[tmpb, self.modb[l]], writes=[out_b])

    def setup_eps(self):
        self.epsc = self.sb("epsc", [128, 4], F32)
        self.epsb = Buf("eps")
        self.op("pool", lambda e: e.memset(self.epsc[:, 0:1], NORM_EPS), writes=[self.epsb])
        self.op("pool", lambda e: e.memset(self.epsc[:, 1:2], 64e-5), writes=[self.epsb])
        self.op("pool", lambda e: e.memset(self.epsc[:, 2:3], 1e-5), writes=[self.epsb])
        self.op("pool", lambda e: e.memset(self.epsc[:, 3:4], 0.0), writes=[self.epsb])

    def phase_a(self, l, xsrc, xsrc_bufs):
        hx = self.sb(f"hx_res{l}", [128, KC, T], BF16)
        hxb = [Buf() for _ in TBS]
        for bi, (t0, tb) in enumerate(TBS):
            xt, xb = self.pool("xblk", [128, KC, 512], F32, 1)
            for q in range(4):
                self.dma("sp", xt[:, 4 * q:4 * q + 4, :tb],
                         xsrc[4 * q:4 * q + 4, :, t0:t0 + tb].rearrange("k p t -> p k t"),
                         reads=[xsrc_bufs[bi]], writes=[xb])
            self.norm_block(l, 0, xt, xb, t0, tb, bi == 0, hx[:, :, t0:t0 + tb], hxb[bi])
            self.dma("sp", self.HX[:, :, t0:t0 + tb].rearrange("k p t -> p k t"), hx[:, :, t0:t0 + tb],
                     reads=[hxb[bi]], writes=[self.HXb[bi]])
        for grp, n, wsrc, dst, dstb in [(64, N64, self.wa64, self.P64, self.P64b),
                                        (128, N128, self.wa128, self.P128, self.P128b)]:
            nj = n // 2 if grp == 64 else n
            for c in range(nj):
                wt, wb = self.pool("wa128", [128, KC, 128], BF16, 3)
                if grp == 64:
                    self.dma("pool", wt[:, :, 0:64], wsrc[l, 2 * c], writes=[wb])
                    self.dma("pool", wt[:, :, 64:128], wsrc[l, 2 * c + 1], writes=[wb])
                else:
                    self.dma("pool", wt[:], wsrc[l, c], writes=[wb])
                for bi, (t0, tb) in enumerate(TBS):
                    pt, pb = self.psum()
                    for kc in range(KC):
                        self.op("pe", lambda e, kc=kc, wt=wt, pt=pt: e.matmul(
                            pt[:, :tb], lhsT=wt[:, kc, :], rhs=hx[:, kc, t0:t0 + tb],
                            start=(kc == 0), stop=(kc == KC - 1)),
                            reads=[wb, hxb[bi]], writes=[pb])
                    ot, ob = self.pool(f"pa_o", [128, 512], F32, 4)
                    if bi % 2 == 0:
                        self.op("act", lambda e, ot=ot, pt=pt: e.copy(out=ot[:, :tb], in_=pt[:, :tb]),
                                reads=[pb], writes=[ob])
                    else:
                        self.op("dve", lambda e, ot=ot, pt=pt: e.tensor_copy(out=ot[:, :tb], in_=pt[:, :tb]),
                                reads=[pb], writes=[ob])
                    if grp == 64:
                        self.dma("sp", dst[2 * c, :, t0:t0 + tb], ot[0:64, :tb], reads=[ob], writes=[dstb[2 * c][bi]])
                        self.dma("sp", dst[2 * c + 1, :, t0:t0 + tb], ot[64:128, :tb], reads=[ob], writes=[dstb[2 * c + 1][bi]])
                    else:
                        self.dma("sp", dst[c, :, t0:t0 + tb], ot[:, :tb], reads=[ob], writes=[dstb[c][bi]])
        wt, wb = self.pool("wa128", [128, KC, 128], BF16, 3)
        self.dma("pool", wt[:], self.wav[l], writes=[wb])
        for ti in range(T // 128):
            bi = 0 if ti < 2 else 1 + (ti - 2) // 4
            pt, pb = self.psum()
            for kc in range(KC):
                self.op("pe", lambda e, kc=kc, pt=pt: e.matmul(
                    pt[:, :128], lhsT=hx[:, kc, ti * 128:(ti + 1) * 128], rhs=wt[:, kc, :],
                    start=(kc == 0), stop=(kc == KC - 1)),
                    reads=[wb, hxb[bi]], writes=[pb])
            ot, ob = self.pool("pav_o", [128, 128], BF16, 3)
            self.op("act", lambda e, ot=ot, pt=pt: e.copy(out=ot[:], in_=pt[:, :128]), reads=[pb], writes=[ob])
            self.dma("sp", self.VTM[ti * 128:(ti + 1) * 128, :], ot[:], reads=[ob], writes=[self.VTMb[ti]])
        return hx


def host_prep(inputs):
    f32 = np.float32
    x = np.asarray(inputs["x"], f32)
    ctx = np.asarray(inputs["ctx"], f32)
    c = np.asarray(inputs["c"], f32)
    c_ctx = np.asarray(inputs["c_ctx"], f32)
    B = x.shape[0]
    shared = {}
    ada_w = np.asarray(inputs["ada_w"], f32)
    shared["adaw"] = np.ascontiguousarray(
        ada_w.reshape(DEPTH, KC, 128, 96, 128).transpose(0, 3, 2, 1, 4))
    ada_b = np.asarray(inputs["ada_b"], f32)
    shared["adab"] = np.ascontiguousarray(ada_b.reshape(DEPTH, 96, 128).transpose(0, 2, 1))
    ng = np.stack([inputs["norm1_g"][0], inputs["norm1_g"][1], inputs["norm2_g"][0], inputs["norm2_g"][1],
                   inputs["final_g"]]).astype(f32)
    shared["ng"] = np.ascontiguousarray(ng.reshape(5, KC, 128).transpose(2, 0, 1))
    w_in = np.asarray(inputs["w_in"], f32)
    c64 = cols64()
    c128 = cols128()
    shared["wa64"] = np.stack([np.stack([wlayout(w_in[l][:, cc]) for cc in c64]) for l in range(DEPTH)])
    shared["wa128"] = np.stack([np.stack([wlayout(w_in[l][:, cc]) for cc in c128]) for l in range(DEPTH)])
    shared["wav"] = np.stack([wlayout(w_in[l][:, O_KV + 128:O_KV + 256]) for l in range(DEPTH)])
    per_core = []
    for b in range(B):
        xin = np.concatenate([ctx[b].T, x[b].T], axis=1)
        m = {"xin": np.ascontiguousarray(xin.reshape(KC, 128, T))}
        cc = np.stack([c[b], c_ctx], axis=-1)
        m["ccol"] = np.ascontiguousarray(cc.reshape(KC, 128, 2).transpose(1, 0, 2))
        per_core.append(m)
    return shared, per_core


TBR = 64
TRO = 128
DECAY_C = float(np.exp(-0.5))
RC_MU = 0
RC_W0 = 52
RC_A0 = 68
RC_KK = 84
RC_KA = 92
RC_RK = 100
RC_LG = 116
RC_LB = 124
RC_N = 132


def rwkv_consts():
    c = np.zeros((128, 1024), np.float32)
    s = np.arange(64)[:, None]
    t = np.arange(64)[None, :]
    for d in range(2):
        before = ((s < t) if d == 0 else (s > t)).astype(np.float32)
        beq = ((s <= t) if d == 0 else (s >= t)).astype(np.float32)
        o = d * 256
        c[0:64, o + 0:o + 64] = -before
        c[64:128, o + 0:o + 64] = before
        c[0:64, o + 64:o + 128] = beq
        c[64:128, o + 64:o + 128] = beq
        c[0:64, o + 128:o + 192] = -(before.T)
    c[:, 512:640] = np.eye(128, dtype=np.float32)
    m = np.ones(128, np.float32)
    m[0] = 0
    m[64] = 0
    c[:, 640:768] = m[None, :]
    return c


class MKR(MK):
    def declare_rwkv(self):
        self.rwc_d = self.dram("rwc", [DEPTH, 64, RC_N], F32, kind="ExternalInput")
        self.wup_d = self.dram("wup", [DEPTH, 2, 64, 512], F32, kind="ExternalInput")
        self.aup_d = self.dram("aup", [DEPTH, 2, 64, 512], F32, kind="ExternalInput")
        self.gup_d = self.dram("gup", [DEPTH, 128, 512], F32, kind="ExternalInput")
        self.rconst_d = self.dram("rconst", [128, 1024], F32, kind="ExternalInput")
        self.YD = self.dram("YD", [2, 8, 64, T], F32)
        self.BOND = self.dram("BOND", [2, 8, 64, T], F32)
        self.FEATS = self.dram("FEATS", [16, 128, T], BF16)
        self.FEATSb = [[Buf() for _ in range(T // 128)] for _ in range(4)]

    def rwkv_setup(self):
        self.rconst = self.sb("rconst_t", [128, 1024], F32)
        self.rconstb = Buf("rconst")
        self.dma("sp", self.rconst[:], self.rconst_d[:], writes=[self.rconstb])
        self.ident = self.rconst[:, 512:640]

    def rwkv(self, l):
        k = self
        nblk = T // TBR
        rwc = k.sb(f"rwc{l}", [64, RC_N + 8], F32)
        rwcb = Buf("rwc")
        k.dma("sp", rwc[:, :RC_N], k.rwc_d[l], writes=[rwcb])
        k.op("dve", lambda e: e.tensor_scalar(out=rwc[:, RC_N:RC_N + 8], in0=rwc[:, RC_KA:RC_KA + 8],
                                              scalar1=-1.0, scalar2=1.0, op0=ALU.mult, op1=ALU.add),
             reads=[rwcb], writes=[rwcb])
        wup = k.sb(f"wup{l}", [64, 2, 512], F32)
        aup = k.sb(f"aup{l}", [64, 2, 512], F32)
        gup = k.sb(f"gup{l}", [128, 512], F32)
        wb_ = Buf("wupaup")
        k.dma("sp", wup[:], k.wup_d[l].rearrange("d j c -> j d c"), writes=[wb_])
        k.dma("sp", aup[:], k.aup_d[l].rearrange("d j c -> j d c"), writes=[wb_])
        k.dma("sp", gup[:], k.gup_d[l], writes=[wb_])
        YDb = [[[Buf() for _ in range(nblk)] for _ in range(2)] for _ in range(2)]
        k._uvub = [Buf(), Buf()]
        CB = [rwcb, k.rconstb]

        def bc(ap2, n):
            return ap2.unsqueeze(2).to_broadcast([64, 8, n])

        import os as _os
        FR = mybir.dt.float32r if _os.environ.get('RW_F32R', '1') == '1' else F32
        identr = k.sb("identr", [64, 64], FR)
        identrb = Buf()
        k.op("dve", lambda e: e.tensor_copy(out=identr[:], in_=k.ident[0:64, 0:64]), reads=[k.rconstb], writes=[identrb])
        def run_dir(d, h0, NH):
            kp = lambda name, shape, dt, n: k.pool(f"{name}_d{d}h{h0}", shape, dt, n)
            bcl = lambda col, n: rwc[:, col + h0:col + h0 + NH].unsqueeze(2).to_broadcast([64, NH, n])
            UVub_s = Buf()
            SV = [k.sb(f"SV{l}{d}{h0}{i}", [128, NH, 64], FR) for i in range(2)]
            SVs = [Buf(), Buf()]
            SVv = [Buf(), Buf()]
            k.op("dve", lambda e: e.tensor_scalar(out=SV[0][0:64].rearrange("p h x -> p (h x)"), in0=k.rconst[0:64, 0:NH * 64], scalar1=0.0, scalar2=None, op0=ALU.mult), reads=[k.rconstb], writes=[SVs[0]])
            cidx = 0
            order = list(range(nblk))
            if d == 1:
                nc_ = CTX // TBR
                order = list(range(nc_ - 1, -1, -1)) + list(range(nblk - 1, nc_ - 1, -1))
            nb_done = 0
            for bi in order:
                t0 = bi * TBR
                is_ctx = t0 < CTX
                seq_lo, seq_hi = (0, CTX) if is_ctx else (CTX, T)
                mixed = []
                for g in range(4):
                    ng = NH if g < 3 else 2
                    c0 = g * 8 + h0 if g < 3 else 24 + 2 * d
                    raw, rawb = kp(f"raw{min(g,3)}", [64, ng, TBR + 1], F32, 2)
                    if d == 0:
                        lo = t0 - 1
                        if lo < seq_lo:
                            k.op("pool", lambda e, raw=raw: e.memset(raw[:, :, 0:1], 0.0), writes=[rawb])
                            k.dma("sp", raw[:, :, 1:TBR + 1],
                                  k.P64[c0:c0 + ng, :, t0:t0 + TBR].rearrange("c p t -> p c t"),
                                  reads=[b_ for c_ in range(c0, c0 + ng) for b_ in k.P64b[c_]], writes=[rawb])
                        else:
                            k.dma("sp", raw[:, :, 0:TBR + 1],
                                  k.P64[c0:c0 + ng, :, lo:t0 + TBR].rearrange("c p t -> p c t"),
                                  reads=[b_ for c_ in range(c0, c0 + ng) for b_ in k.P64b[c_]], writes=[rawb])
                        cur = raw[:, :, 1:TBR + 1]
                        prev = raw[:, :, 0:TBR]
                    else:
                        hi = t0 + TBR + 1
                        if hi > seq_hi:
                            k.op("pool", lambda e, raw=raw: e.memset(raw[:, :, TBR:TBR + 1], 0.0), writes=[rawb])
                            k.dma("sp", raw[:, :, 0:TBR],
                                  k.P64[c0:c0 + ng, :, t0:t0 + TBR].rearrange("c p t -> p c t"),
                                  reads=[b_ for c_ in range(c0, c0 + ng) for b_ in k.P64b[c_]], writes=[rawb])
                        else:
                            k.dma("sp", raw[:, :, 0:TBR + 1],
                                  k.P64[c0:c0 + ng, :, t0:hi].rearrange("c p t -> p c t"),
                                  reads=[b_ for c_ in range(c0, c0 + ng) for b_ in k.P64b[c_]], writes=[rawb])
                        cur = raw[:, :, 0:TBR]
                        prev = raw[:, :, 1:TBR + 1]
                    if g == 2:
                        mx, mxb = kp("vpad", [64, NH, 64 + TBR], F32, 1)
                        mxv = mx[:, :, 64:64 + TBR]
                    else:
                        mx, mxb = kp(f"mix{g}", [64, ng, TBR], F32, 1)
                        mxv = mx[:, :, :]
                    mcol = RC_MU + d * 26 + (g * 8 + h0 if g < 3 else 24)
                    mub = rwc[:, mcol:mcol + ng].unsqueeze(2).to_broadcast([64, ng, TBR])
                    eng = "pool" if g % 2 == 0 else "dve"
                    k.op(eng, lambda e, mxv=mxv, prev=prev, cur=cur: e.tensor_tensor(
                        out=mxv, in0=prev, in1=cur, op=ALU.subtract), reads=[rawb], writes=[mxb])
                    k.op(eng, lambda e, mxv=mxv, mub=mub: e.tensor_tensor(
                        out=mxv, in0=mxv, in1=mub, op=ALU.mult), reads=[mxb, rwcb], writes=[mxb])
                    k.op(eng, lambda e, mxv=mxv, cur=cur: e.tensor_tensor(
                        out=mxv, in0=mxv, in1=cur, op=ALU.add), reads=[mxb, rawb], writes=[mxb])
                    mixed.append((mx, mxv, mxb))
                (rm, rmv, rmb), (km, kmv, kmb), (vp, vmv, vpb), (lm, lmv, lmb) = mixed
                if nb_done == 0:
                    k.op("pool", lambda e, vp=vp: e.memset(vp[:, :, 0:64], 0.0), writes=[vpb])
                yield
                tw, twb = kp("tw", [64, TBR], F32, 1)
                k.op("act", lambda e: e.activation(out=tw[:], in_=lm[:, 0, :], func=AF.Tanh),
                     reads=[lmb], writes=[twb])
                sg, sgb = kp("sg", [64, NH, TBR], F32, 1)
                at, ab = kp("a", [64, NH, TBR], F32, 1)
                for (dst, dstb, up, rhs, rhsb, bcol) in [(sg, sgb, wup, tw[:], twb, RC_W0),
                                                         (at, ab, aup, lm[:, 1, :], lmb, RC_A0)]:
                    for hh in range(NH // 4):
                        pt, pb = k.psum()
                        for h4 in range(4):
                            h = hh * 4 + h4
                            k.op("pe", lambda e, pt=pt, h=h, h4=h4, up=up, rhs=rhs: e.matmul(
                                pt[:64, h4 * TBR:(h4 + 1) * TBR], lhsT=up[:, d, (h0 + h) * 64:(h0 + h + 1) * 64], rhs=rhs,
                                start=True, stop=True), reads=[wb_, rhsb], writes=[pb])
                        for h4 in range(4):
                            h = hh * 4 + h4
                            k.op("act", lambda e, pt=pt, h=h, h4=h4, dst=dst, bcol=bcol: e.activation(
                                out=dst[:, h, :], in_=pt[:64, h4 * TBR:(h4 + 1) * TBR], func=AF.Sigmoid,
                                bias=rwc[:, bcol + d * 8 + h0 + h:bcol + d * 8 + h0 + h + 1], scale=1.0),
                                reads=[pb, rwcb], writes=[dstb])
                yield
                cs, csb = kp("cs", [64, NH, TBR], F32, 1)
                for h in range(NH):
                    k.op("dve", lambda e, h=h: e.tensor_tensor_scan(
                        out=cs[:, h, :], data0=k.rconst[0:64, 640:640 + TBR], data1=sg[:, h, :], initial=0.0,
                        op0=ALU.mult, op1=ALU.add), reads=[sgb, k.rconstb], writes=[csb])
                cs4 = cs[:].rearrange("p h (c t) -> p (h c) t", t=64)
                sg4 = sg[:].rearrange("p h (c t) -> p (h c) t", t=64)
                if d == 1:
                    tot, totb = kp("tot", [64, NH * (TBR // 64), 1], F32, 1)
                    k.op("pool", lambda e: e.tensor_copy(out=tot[:], in_=cs4[:, :, 63:64]), reads=[csb], writes=[totb])
                    k.op("dve", lambda e: e.tensor_tensor(out=cs4, in0=sg4, in1=cs4, op=ALU.subtract),
                         reads=[sgb, csb], writes=[csb])
                    k.op("dve", lambda e: e.tensor_tensor(out=cs4, in0=cs4, in1=tot[:].to_broadcast([64, NH * (TBR // 64), 64]),
                                                          op=ALU.add), reads=[csb, totb], writes=[csb])
                Pt, Pb = kp("P", [64, NH, TBR], F32, 1)
                Pi, Pib = kp("Pi", [64, NH, TBR], F32, 1)
                Pp, Ppb = kp("Pp", [64, NH, TBR], F32, 1)
                k.op("act", lambda e: e.activation(out=Pt[:], in_=cs[:], func=AF.Exp, scale=-DECAY_C),
                     reads=[csb], writes=[Pb])
                k.op("act", lambda e: e.activation(out=Pi[:], in_=cs[:], func=AF.Exp, scale=DECAY_C),
                     reads=[csb], writes=[Pib])
                k.op("pool", lambda e: e.tensor_tensor(out=Pp[:], in0=cs[:], in1=sg[:], op=ALU.subtract),
                     reads=[csb, sgb], writes=[Ppb])
                k.op("act", lambda e: e.activation(out=Pp[:], in_=Pp[:], func=AF.Exp, scale=-DECAY_C),
                     reads=[Ppb], writes=[Ppb])
                yield
                kap, kapb = kp("kap", [64, NH, TBR], F32, 1)
                k.op("dve", lambda e: e.tensor_tensor(out=kap[:], in0=kmv, in1=bcl(RC_KK, TBR),
                                                      op=ALU.mult), reads=[kmb, rwcb], writes=[kapb])
                sq, sqb = kp("rsq", [64, NH, TBR], F32, 1)
                k.op("act", lambda e: e.activation(out=sq[:], in_=kap[:], func=AF.Square), reads=[kapb], writes=[sqb])
                rin, rinb = kp("rin", [64, NH, TBR], F32, 1)
                for hh in range(NH // 4):
                    pt, pb = k.psum()
                    k.op("pe", lambda e, pt=pt, hh=hh: e.matmul(
                        pt[:64, :4 * TBR], lhsT=k.ones_f[0:64, 0:64],
                        rhs=sq[:, hh * 4:(hh + 1) * 4, :].rearrange("p h t -> p (h t)"),
                        start=True, stop=True), reads=[sqb, k.ones_fb], writes=[pb])
                    k.op("act", lambda e, pt=pt, hh=hh: e.activation(
                        out=rin[:, hh * 4:(hh + 1) * 4, :].rearrange("p h t -> p (h t)"), in_=pt[:64, :4 * TBR],
                        func=AF.Sqrt), reads=[pb], writes=[rinb])
                k.op("dve", lambda e: e.tensor_scalar(out=rin[:], in0=rin[:], scalar1=1e-12, scalar2=None,
                                                      op0=ALU.max), reads=[rinb], writes=[rinb])
                k.op("dve", lambda e: e.reciprocal(out=rin[:], in_=rin[:]), reads=[rinb], writes=[rinb])
                k.op("dve", lambda e: e.tensor_tensor(out=kap[:], in0=kap[:], in1=rin[:], op=ALU.mult),
                     reads=[kapb, rinb], writes=[kapb])
                yield
                kr, krb = kp("krep", [64, NH, TBR], F32, 1)
                k.op("pool", lambda e: e.tensor_tensor(out=kr[:], in0=at[:], in1=bcl(RC_KA, TBR),
                                                       op=ALU.mult), reads=[ab, rwcb], writes=[krb])
                k.op("pool", lambda e: e.tensor_tensor(out=kr[:], in0=kr[:], in1=bcl(RC_N, TBR),
                                                       op=ALU.add), reads=[krb, rwcb], writes=[krb])
                k.op("pool", lambda e: e.tensor_tensor(out=kr[:], in0=kr[:], in1=kmv, op=ALU.mult),
                     reads=[krb, kmb], writes=[krb])
                bt_, btb = kp("bb", [64, NH, TBR], F32, 1)
                k.op("pool", lambda e: e.tensor_tensor(out=bt_[:], in0=kap[:], in1=at[:], op=ALU.mult),
                     reads=[kapb, ab], writes=[btb])
                yield
                bon, bonb = kp("bon", [64, NH, TBR], F32, 1)
                k.op("dve", lambda e: e.tensor_tensor(out=bon[:], in0=rmv, in1=kr[:], op=ALU.mult),
                     reads=[rmb, krb], writes=[bonb])
                k.op("dve", lambda e: e.tensor_tensor(out=bon[:], in0=bon[:],
                                                      in1=bcl(RC_RK + d * 8, TBR), op=ALU.mult),
                     reads=[bonb, rwcb], writes=[bonb])
                for hh in range(NH // 4):
                    pt, pb = k.psum()
                    k.op("pe", lambda e, pt=pt, hh=hh: e.matmul(
                        pt[:64, :4 * TBR], lhsT=k.ones_f[0:64, 0:64],
                        rhs=bon[:, hh * 4:(hh + 1) * 4, :].rearrange("p h t -> p (h t)"),
                        start=True, stop=True), reads=[bonb, k.ones_fb], writes=[pb])
                    k.op("dve", lambda e, pt=pt, hh=hh: e.tensor_tensor(
                        out=bon[:, hh * 4:(hh + 1) * 4, :],
                        in0=pt[:64, :4 * TBR].rearrange("p (h t) -> p h t", t=TBR),
                        in1=vmv[:, hh * 4:(hh + 1) * 4, :], op=ALU.mult), reads=[pb, vpb, bonb], writes=[bonb])
                yield
                QR, QRb = kp("QR", [64, NH, TBR // 64, 128], FR, 1)
                BK, BKb = kp("BK", [64, NH, TBR // 64, 128], FR, 1)

                def v4(ap):
                    return ap.rearrange("p h (c t) -> p h c t", t=64)
                k.op("dve", lambda e: e.tensor_tensor(out=QR[:, :, :, 0:64], in0=v4(kap[:]), in1=v4(Pp[:]), op=ALU.mult),
                     reads=[kapb, Ppb], writes=[QRb])
                k.op("dve", lambda e: e.tensor_tensor(out=QR[:, :, :, 64:128], in0=v4(rmv), in1=v4(Pt[:]), op=ALU.mult),
                     reads=[rmb, Pb], writes=[QRb])
                k.op("dve", lambda e: e.tensor_tensor(out=BK[:, :, :, 0:64], in0=v4(bt_[:]), in1=v4(Pi[:]), op=ALU.mult),
                     reads=[btb, Pib], writes=[BKb])
                k.op("dve", lambda e: e.tensor_tensor(out=BK[:, :, :, 64:128], in0=v4(kr[:]), in1=v4(Pi[:]), op=ALU.mult),
                     reads=[krb, Pib], writes=[BKb])
                yield
                Yt, Yb = kp("Yt", [64, NH, TBR], F32, 1)
                mo = d * 256
                for c in (list(range(TBR // 64)) if d == 0 else list(range(TBR // 64 - 1, -1, -1))):
                    cur_sv = cidx % 2
                    nxt_sv = (cidx + 1) % 2
                    QB, QBb = kp("QB", [128, NH, 64], FR, 1)
                    AMR, AMRb = kp("AMR", [128, NH, 64], FR, 1)
                    XSa, XSab = kp("XS", [64, NH, 128], FR, 2)
                    XTa, XTab = kp("XT", [64, NH, 64], FR, 2)
                    Tm, Tmb = kp("Tm", [64, NH, 64], FR, 1)
                    k.op("act", lambda e, QB=QB, c=c: e.copy(out=QB[0:64, :, :], in_=QR[:, :, c, 0:64]),
                         reads=[QRb], writes=[QBb])
                    for hh in range(NH // 4):
                        pt, pb = k.psum()
                        for h4 in range(4):
                            h = hh * 4 + h4
                            k.op("pe", lambda e, pt=pt, h=h, h4=h4, c=c: e.matmul(
                                pt[:, h4 * 128:(h4 + 1) * 128], lhsT=BK[:, h, c, :], rhs=QR[:, h, c, :],
                                start=True, stop=True), reads=[BKb, QRb], writes=[pb])
                        p4 = pt[:, :].rearrange("p (h x) -> p h x", x=128)
                        hs = slice(hh * 4, hh * 4 + 4)
                        k.op("dve", lambda e, p4=p4, hs=hs, XSa=XSa: e.tensor_tensor(
                            out=XSa[:, hs, 0:64], in0=p4[0:64, :, 0:64],
                            in1=k.rconst[0:64, mo:mo + 64].unsqueeze(1).to_broadcast([64, 4, 64]), op=ALU.mult),
                            reads=[pb, k.rconstb], writes=[XSab])
                        k.op("dve", lambda e, p4=p4, hs=hs, QB=QB: e.tensor_tensor(
                            out=QB[64:128, hs, :], in0=p4[64:128, :, 0:64],
                            in1=k.rconst[64:128, mo:mo + 64].unsqueeze(1).to_broadcast([64, 4, 64]), op=ALU.mult),
                            reads=[pb, k.rconstb], writes=[QBb])
                        k.op("dve", lambda e, p4=p4, hs=hs, AMR=AMR: e.tensor_tensor(
                            out=AMR[:, hs, :], in0=p4[:, :, 64:128],
                            in1=k.rconst[:, mo + 64:mo + 128].unsqueeze(1).to_broadcast([128, 4, 64]), op=ALU.mult),
                            reads=[pb, k.rconstb], writes=[AMRb])
                    yield
                    pt, pb = k.psum()
                    for h in range(NH):
                        k.op("pe", lambda e, pt=pt, h=h, c=c: e.matmul(
                            pt[:64, h * 64:(h + 1) * 64], lhsT=QR[:, h, c, 0:64], rhs=BK[:, h, c, 0:64],
                            start=True, stop=True), reads=[BKb, QRb], writes=[pb])
                    k.op("dve", lambda e, pt=pt, XTa=XTa: e.tensor_tensor(
                        out=XTa[:], in0=pt[:64, :NH * 64].rearrange("p (h x) -> p h x", x=64),
                        in1=k.rconst[0:64, mo + 128:mo + 192].unsqueeze(1).to_broadcast([64, NH, 64]), op=ALU.mult),
                        reads=[pb, k.rconstb], writes=[XTab])
                    k.op("dve", lambda e, XSa=XSa: e.tensor_copy(
                        out=XSa[:, :, 64:128], in_=k.ident[0:64, 0:64].unsqueeze(1).to_broadcast([64, NH, 64])),
                        reads=[k.rconstb], writes=[XSab])
                    yield
                    for j in range(6):
                        if j < 5:
                            XSn, XSnb = kp("XS", [64, NH, 128], FR, 2)
                            for hh in range(NH // 4):
                                pt, pb = k.psum()
                                for h4 in range(4):
                                    h = hh * 4 + h4
                                    k.op("pe", lambda e, pt=pt, h=h, h4=h4, XSa=XSa, XTa=XTa: e.matmul(
                                        pt[:64, h4 * 128:(h4 + 1) * 128], lhsT=XTa[:, h, :], rhs=XSa[:, h, :],
                                        start=True, stop=True), reads=[XSab, XTab], writes=[pb])
                                p4 = pt[:64, :].rearrange("p (h x) -> p h x", x=128)
                                hs = slice(hh * 4, hh * 4 + 4)
                                k.op("act", lambda e, p4=p4, hs=hs, XSn=XSn: e.copy(
                                    out=XSn[:, hs, 0:64], in_=p4[:, :, 0:64]), reads=[pb], writes=[XSnb])
                                k.op("dve", lambda e, p4=p4, hs=hs, XSn=XSn, XSa=XSa: e.tensor_tensor(
                                    out=XSn[:, hs, 64:128], in0=p4[:, :, 64:128], in1=XSa[:, hs, 64:128], op=ALU.add),
                                    reads=[pb, XSab], writes=[XSnb])
                            XTn, XTnb = kp("XT", [64, NH, 64], FR, 2)
                            pt2, pb2 = k.psum()
                            for h in range(NH):
                                k.op("pe", lambda e, pt2=pt2, h=h, XSa=XSa, XTa=XTa: e.matmul(
                                    pt2[:64, h * 64:(h + 1) * 64], lhsT=XSa[:, h, 0:64], rhs=XTa[:, h, :],
                                    start=True, stop=True), reads=[XSab, XTab], writes=[pb2])
                            k.op("act", lambda e, pt2=pt2, XTn=XTn: e.copy(
                                out=XTn[:].rearrange("p h x -> p (h x)"), in_=pt2[:64, :NH * 64]), reads=[pb2], writes=[XTnb])
                            XSa, XSab = XSn, XSnb
                            XTa, XTab = XTn, XTnb
                        else:
                            pt3, pb3 = k.psum()
                            for h in range(NH):
                                k.op("pe", lambda e, pt3=pt3, h=h, XSa=XSa, XTa=XTa: e.matmul(
                                    pt3[:64, h * 64:(h + 1) * 64], lhsT=XTa[:, h, :], rhs=XSa[:, h, 64:128],
                                    start=True, stop=True), reads=[XSab, XTab], writes=[pb3])
                            k.op("dve", lambda e, pt3=pt3, Tm=Tm, XSa=XSa: e.tensor_tensor(
                                out=Tm[:], in0=pt3[:64, :NH * 64].rearrange("p (h x) -> p h x", x=64),
                                in1=XSa[:, :, 64:128], op=ALU.add), reads=[pb3, XSab], writes=[Tmb])
                        yield
                    BKT, BKTb = kp("BKT", [128, NH, 64], FR, 1)
                    pt, pb = k.psum()
                    for h in range(NH):
                        k.op("pe", lambda e, pt=pt, h=h, c=c: e.transpose(
                            pt[:, h * 64:(h + 1) * 64], BK[:, h, c, :].bitcast(F32), k.ident[0:64, 0:64]),
                            reads=[BKb, k.rconstb], writes=[pb])
                    k.op("act", lambda e, pt=pt, BKT=BKT: e.copy(
                        out=BKT[:].rearrange("p h x -> p (h x)"), in_=pt[:, :NH * 64]), reads=[pb], writes=[BKTb])
                    yield
                    UV, UVvb = kp("UV", [128, NH, 64], FR, 1)
                    UVub = UVub_s
                    pt, pb = k.psum()
                    for h in range(NH):
                        k.op("pe", lambda e, pt=pt, h=h, c=c: e.transpose(
                            pt[:, h * 64:(h + 1) * 64], vp[:, h, c * 64:c * 64 + 128], k.ident[0:64, 0:64]),
                            reads=[vpb, k.rconstb], writes=[pb])
                    k.op("dve", lambda e, pt=pt, UV=UV: e.tensor_copy(
                        out=UV[64:128].rearrange("p h x -> p (h x)"), in_=pt[64:128, :NH * 64]), reads=[pb], writes=[UVvb])
                    k.op("dve", lambda e, pt=pt, cur_sv=cur_sv: e.tensor_copy(
                        out=SV[cur_sv][64:128].rearrange("p h x -> p (h x)"), in_=pt[64:128, :NH * 64]),
                        reads=[pb], writes=[SVv[cur_sv]])
                    yield
                    WT, WTb = kp("WT", [64, NH, 64], FR, 1)
                    pt, pb = k.psum()
                    for h in range(NH):
                        k.op("pe", lambda e, pt=pt, h=h, QB=QB, cur_sv=cur_sv: e.matmul(
                            pt[:64, h * 64:(h + 1) * 64], lhsT=QB[:, h, :], rhs=SV[cur_sv][:, h, :],
                            start=True, stop=True), reads=[QBb, SVs[cur_sv], SVv[cur_sv]], writes=[pb])
                    k.op("act", lambda e, pt=pt, WT=WT: e.copy(
                        out=WT[:].rearrange("p h x -> p (h x)"), in_=pt[:64, :NH * 64]), reads=[pb], writes=[WTb])
                    yield
                    pt, pb = k.psum()
                    for h in range(NH):
                        k.op("pe", lambda e, pt=pt, h=h, Tm=Tm, WT=WT: e.matmul(
                            pt[:64, h * 64:(h + 1) * 64], lhsT=Tm[:, h, :], rhs=WT[:, h, :],
                            start=True, stop=True), reads=[Tmb, WTb], writes=[pb])
                    k.op("act", lambda e, pt=pt, UV=UV: e.mul(
                        out=UV[0:64].rearrange("p h x -> p (h x)"), in_=pt[:64, :NH * 64], mul=-1.0), reads=[pb], writes=[UVub])
                    yield
                    pt, pb = k.psum()
                    for h in range(NH):
                        k.op("pe", lambda e, pt=pt, h=h, c=c, cur_sv=cur_sv: e.matmul(
                            pt[:64, h * 64:(h + 1) * 64], lhsT=SV[cur_sv][0:64, h, :], rhs=QR[:, h, c, 64:128],
                            start=True, stop=False), reads=[SVs[cur_sv], QRb], writes=[pb])
                        k.op("pe", lambda e, pt=pt, h=h, UV=UV, AMR=AMR: e.matmul(
                            pt[:64, h * 64:(h + 1) * 64], lhsT=UV[:, h, :], rhs=AMR[:, h, :],
                            start=False, stop=True), reads=[UVub, UVvb, AMRb], writes=[pb])
                    k.op("act", lambda e, pt=pt, c=c: e.copy(
                        out=Yt[:, :, c * 64:(c + 1) * 64], in_=pt[:64, :NH * 64].rearrange("p (h x) -> p h x", x=64)),
                        reads=[pb], writes=[Yb])
                    yield
                    pt, pb = k.psum()
                    for h in range(NH):
                        k.op("pe", lambda e, pt=pt, h=h, cur_sv=cur_sv: e.matmul(
                            pt[:64, h * 64:(h + 1) * 64], lhsT=identr[:], rhs=SV[cur_sv][0:64, h, :],
                            start=True, stop=False), reads=[SVs[cur_sv], identrb], writes=[pb])
                        k.op("pe", lambda e, pt=pt, h=h, BKT=BKT, UV=UV: e.matmul(
                            pt[:64, h * 64:(h + 1) * 64], lhsT=BKT[:, h, :], rhs=UV[:, h, :],
                            start=False, stop=True), reads=[BKTb, UVub, UVvb], writes=[pb])
                    pcol = (c * 64 + 63) if d == 0 else (c * 64)
                    k.op("dve", lambda e, pt=pt, nxt_sv=nxt_sv, pcol=pcol: e.tensor_tensor(
                        out=SV[nxt_sv][0:64], in0=pt[:64, :NH * 64].rearrange("p (h x) -> p h x", x=64),
                        in1=Pt[:, :, pcol:pcol + 1].to_broadcast([64, NH, 64]), op=ALU.mult),
                        reads=[pb, Pb], writes=[SVs[nxt_sv]])
                    cidx += 1
                k.dma("sp", k.YD[d, h0:h0 + NH, :, t0:t0 + TBR].rearrange("h p t -> p h t"), Yt[:], reads=[Yb], writes=[YDb[d][h0 // NH][bi]])
                k.dma("sp", k.BOND[d, h0:h0 + NH, :, t0:t0 + TBR].rearrange("h p t -> p h t"), bon[:], reads=[bonb], writes=[YDb[d][h0 // NH][bi]])
                nb_done += 1
                yield

        with k.scope():
            NHG = int(_os.environ.get('RW_NH', '8'))
            gens = [run_dir(d_, h0_, NHG) for h0_ in range(0, 8, NHG) for d_ in range(2)]
            while gens:
                for g_ in list(gens):
                    try:
                        next(g_)
                    except StopIteration:
                        gens.remove(g_)
        k._ydb = YDb

    def rwkv_readout_gen(self, l):
        k = self
        YDb = k._ydb
        rwc = k.sb(f"rwc_ro{l}", [64, RC_N + 8], F32)
        rwcb = Buf("rwc_ro")
        k.dma("sp", rwc[:, :RC_N], k.rwc_d[l], writes=[rwcb])
        gup = k.sb(f"gup_ro{l}", [128, 512], F32)
        wb_ = Buf("gup_ro")
        k.dma("sp", gup[:], k.gup_d[l], writes=[wb_])
        for ro in range(T // TRO):
            yield
            k.rwkv_readout(l, ro, ro * TRO, [b_ for d_ in range(2) for g_ in range(2) for b_ in YDb[d_][g_][ro * (TRO // TBR):(ro + 1) * (TRO // TBR)]],
                           rwc, rwcb, gup, wb_)

    def rwkv_readout(self, l, bi, t0, ydbufs, rwc, rwcb, gup, gupb):
        k = self
        yf, yfbuf = k.pool("yf", [64, 8, TRO], F32, 1)
        bf_, bfbuf = k.pool("bonf", [64, 8, TRO], F32, 1)
        yb_, ybbuf = k.pool("yb", [64, 8, TRO], F32, 1)
        bb_, bbbuf = k.pool("bonb", [64, 8, TRO], F32, 1)
        for (dst, dstb, src, dd) in [(yf, yfbuf, k.YD, 0), (yb_, ybbuf, k.YD, 1), (bf_, bfbuf, k.BOND, 0), (bb_, bbbuf, k.BOND, 1)]:
            k.dma("sp", dst[:], src[dd, :, :, t0:t0 + TRO].rearrange("h p t -> p h t"), reads=ydbufs, writes=[dstb])
        k.op("pool", lambda e: e.tensor_tensor(out=yf[:], in0=yf[:], in1=yb_[:], op=ALU.add),
             reads=[yfbuf, ybbuf], writes=[yfbuf])
        k.op("pool", lambda e: e.tensor_tensor(out=bf_[:], in0=bf_[:], in1=bb_[:], op=ALU.add),
             reads=[bfbuf, bbbuf], writes=[bfbuf])
        if "rwkv_y" in k.debug and l == 0:
            if "o_y" not in k.dbg_out:
                k.dbg_out["o_y"] = k.dram("o_y", [8, 64, T], F32, kind="ExternalOutput")
            ob = Buf()
            k.dma("sp", k.dbg_out["o_y"][:, :, t0:t0 + TRO].rearrange("h p t -> p h t"), yf[:], reads=[yfbuf], writes=[ob])
            k.out_tokens.append(ob)
        mean, meanb = k.pool("gn_m", [64, 8, TRO], F32, 1)
        for hh in range(2):
            pt, pb = k.psum()
            k.op("pe", lambda e, pt=pt, hh=hh: e.matmul(
                pt[:64, :], lhsT=k.ones_f[0:64, 0:64], rhs=yf[:, hh * 4:(hh + 1) * 4, :].rearrange("p h t -> p (h t)"),
                start=True, stop=True), reads=[yfbuf, k.ones_fb], writes=[pb])
            k.op("dve", lambda e, pt=pt, hh=hh: e.scalar_tensor_tensor(
                out=mean[:, hh * 4:(hh + 1) * 4, :].rearrange("p h t -> p (h t)"), in0=pt[:64, :], scalar=-1.0 / 64,
                in1=yf[:, hh * 4:(hh + 1) * 4, :].rearrange("p h t -> p (h t)"), op0=ALU.mult, op1=ALU.add),
                reads=[pb, yfbuf], writes=[meanb])
        sq, sqb = k.pool("gn_sq", [64, 8, TRO], F32, 1)
        k.op("act", lambda e: e.activation(out=sq[:], in_=mean[:], func=AF.Square), reads=[meanb], writes=[sqb])
        rstd, rstdb = k.pool("gn_r", [64, 8, TRO], F32, 1)
        for hh in range(2):
            pt, pb = k.psum()
            k.op("pe", lambda e, pt=pt, hh=hh: e.matmul(
                pt[:64, :], lhsT=k.ones_f[0:64, 0:64], rhs=sq[:, hh * 4:(hh + 1) * 4, :].rearrange("p h t -> p (h t)"),
                start=True, stop=True), reads=[sqb, k.ones_fb], writes=[pb])
            k.op("act", lambda e, pt=pt, hh=hh: e.activation(
                out=rstd[:, hh * 4:(hh + 1) * 4, :].rearrange("p h t -> p (h t)"), in_=pt[:64, :], func=AF.Sqrt,
                scale=1.0 / 64, bias=k.epsc[0:64, 1:2]), reads=[pb, k.epsb], writes=[rstdb])
        k.op("dve", lambda e: e.reciprocal(out=rstd[:], in_=rstd[:]), reads=[rstdb], writes=[rstdb])
        k.op("dve", lambda e: e.tensor_tensor(out=mean[:], in0=mean[:], in1=rstd[:], op=ALU.mult),
             reads=[meanb, rstdb], writes=[meanb])
        k.op("pool", lambda e: e.tensor_tensor(
            out=mean[:], in0=mean[:], in1=rwc[:, RC_LG:RC_LG + 8].unsqueeze(2).to_broadcast([64, 8, TRO]), op=ALU.mult),
            reads=[meanb, rwcb], writes=[meanb])
        k.op("pool", lambda e: e.tensor_tensor(
            out=mean[:], in0=mean[:], in1=rwc[:, RC_LB:RC_LB + 8].unsqueeze(2).to_broadcast([64, 8, TRO]), op=ALU.add),
            reads=[meanb, rwcb], writes=[meanb])
        k.op("pool", lambda e: e.tensor_tensor(out=mean[:], in0=mean[:], in1=bf_[:], op=ALU.add),
             reads=[meanb, bfbuf], writes=[meanb])
        gs, gsb = k.pool("gsig", [128, TRO], F32, 1)
        k.dma("sp", gs[:], k.P128[0, :, t0:t0 + TRO], reads=k.P128b[0], writes=[gsb])
        k.op("act", lambda e: e.activation(out=gs[:], in_=gs[:], func=AF.Sigmoid), reads=[gsb], writes=[gsb])
        ot, otb = k.pool("rw_out", [64, 8, TRO], BF16, 1)
        for hh in range(2):
            pt, pb = k.psum()
            for h4 in range(4):
                h = hh * 4 + h4
                k.op("pe", lambda e, pt=pt, h=h, h4=h4: e.matmul(
                    pt[:64, h4 * TRO:(h4 + 1) * TRO], lhsT=gup[:, h * 64:(h + 1) * 64], rhs=gs[:],
                    start=True, stop=True), reads=[gupb, gsb], writes=[pb])
            k.op("dve", lambda e, pt=pt, hh=hh: e.tensor_tensor(
                out=ot[:, hh * 4:(hh + 1) * 4, :].rearrange("p h t -> p (h t)"), in0=pt[:64, :],
                in1=mean[:, hh * 4:(hh + 1) * 4, :].rearrange("p h t -> p (h t)"), op=ALU.mult),
                reads=[pb, meanb], writes=[otb])
        k.dma("sp", k.FEATS[4:8, :, t0:t0 + TRO].rearrange("c (two p) t -> p c two t", two=2),
              ot[:].rearrange("p (c two) t -> p c two t", two=2), reads=[otb], writes=[k.FEATSb[1][bi]])


def rwkv_host_prep(inputs):
    f32 = np.float32
    cols = np.zeros((DEPTH, 64, RC_N), f32)
    for l in range(DEPTH):
        for d in range(2):
            mu = np.asarray(inputs["rwkv_mu"], f32)[l, d]
            cols[l, :, RC_MU + d * 26:RC_MU + (d + 1) * 26] = mu.reshape(26, 64).T
            cols[l, :, RC_W0 + d * 8:RC_W0 + (d + 1) * 8] = np.asarray(inputs["rwkv_w0"], f32)[l, d].reshape(8, 64).T
            cols[l, :, RC_A0 + d * 8:RC_A0 + (d + 1) * 8] = np.asarray(inputs["rwkv_a0"], f32)[l, d].reshape(8, 64).T
            cols[l, :, RC_RK + d * 8:RC_RK + (d + 1) * 8] = np.asarray(inputs["rwkv_r_k"], f32)[l, d].T
        cols[l, :, RC_KK:RC_KK + 8] = np.asarray(inputs["rwkv_k_k"], f32)[l].reshape(8, 64).T
        cols[l, :, RC_KA:RC_KA + 8] = np.asarray(inputs["rwkv_k_a"], f32)[l].reshape(8, 64).T
        cols[l, :, RC_LG:RC_LG + 8] = np.asarray(inputs["rwkv_lnx_g"], f32)[l].reshape(8, 64).T
        cols[l, :, RC_LB:RC_LB + 8] = np.asarray(inputs["rwkv_lnx_b"], f32)[l].reshape(8, 64).T
    return {"rwc": cols, "wup": np.ascontiguousarray(np.asarray(inputs["rwkv_w_up"], f32)),
            "aup": np.ascontiguousarray(np.asarray(inputs["rwkv_a_up"], f32)),
            "gup": np.ascontiguousarray(np.asarray(inputs["rwkv_g_up"], f32)),
            "rconst": rwkv_consts()}


NEG = -1e30


def att_consts():
    t = np.arange(SEQ)
    row = (t // 64).astype(np.float32)
    col = (t % 64).astype(np.float32)
    inv = (10000.0 ** (-np.arange(16, dtype=np.float32) / 16)).astype(np.float32)
    tab = np.zeros((64, 2, SEQ), np.float32)
    for dd in range(64):
        half = dd // 32
        i = dd % 32
        pos = row if half == 0 else col
        fi = i % 16
        ang = (pos * inv[fi]).astype(np.float32)
        tab[dd, 0] = np.cos(ang)
        tab[dd, 1] = (-np.sin(ang)) if i < 16 else np.sin(ang)
    i = np.arange(128)[:, None]
    j = np.arange(384)[None, :]
    band = np.where((j >= i) & (j <= i + 256), 0.0, NEG).astype(np.float32)
    return tab, band


class MKA(MKR):
    def declare_att(self):
        self.rope_d = self.dram("rope", [64, 2, SEQ], F32, kind="ExternalInput")
        self.band_d = self.dram("band", [128, 384], F32, kind="ExternalInput")
        self.sink_d = self.dram("sinkb", [DEPTH, 128, 8], F32, kind="ExternalInput")

    def attention(self, l):
        k = self
        scale = 0.125
        Qb = k.sb("Qb", [64, 8, T], BF16)
        Kb = k.sb("Kb", [64, 2, T], BF16)
        Vr = k.sb("Vr", [128, T // 128, 128], BF16)
        Qbb = [Buf() for _ in TBS]
        Kbb = [Buf() for _ in TBS]
        Vrb = Buf()
        k.dma("sp", Vr[:], k.VTM[:, :].rearrange("(n p) c -> p n c", p=128), reads=k.VTMb, writes=[Vrb])
        band = k.sb("band_t", [128, 384], F32)
        bandb = Buf()
        k.dma("sp", band[:], k.band_d[:], writes=[bandb])
        sink = k.sb("sink_t", [128, 8], F32)
        sinkb = Buf()
        k.dma("sp", sink[:], k.sink_d[l], writes=[sinkb])
        identb = k.sb("identb", [128, 128], BF16)
        identbb = Buf()
        k.op("dve", lambda e: e.tensor_copy(out=identb[:], in_=k.ident), reads=[k.rconstb], writes=[identbb])
        for bi, (t0, tb) in enumerate(TBS):
            for (dst, dstb, c0, nh) in [(Qb, Qbb, 28, 8), (Kb, Kbb, 44, 2)]:
                raw, rawb = k.pool(f"araw{nh}", [64, nh, 512], F32, 1)
                k.dma("sp", raw[:, :, :tb], k.P64[c0:c0 + nh, :, t0:t0 + tb].rearrange("c p t -> p c t"),
                      reads=[b_ for c_ in range(c0, c0 + nh) for b_ in k.P64b[c_]], writes=[rawb])
                if bi == 0:
                    k.op("act", lambda e, raw=raw, dst=dst: e.copy(out=dst[:, :, t0:t0 + tb], in_=raw[:, :, :tb]),
                         reads=[rawb], writes=[dstb[bi]])
                    continue
                sw, swb = k.pool(f"asw{nh}", [64, nh, 512], F32, 1)
                k.dma("sp", sw[:, :, :tb], k.P64[c0 + nh:c0 + 2 * nh, :, t0:t0 + tb].rearrange("c p t -> p c t"),
                      reads=[b_ for c_ in range(c0 + nh, c0 + 2 * nh) for b_ in k.P64b[c_]], writes=[swb])
                tab, tabb = k.pool("ropetab", [64, 2, 512], F32, 2)
                k.dma("sp", tab[:, :, :tb], k.rope_d[:, :, t0 - CTX:t0 - CTX + tb], writes=[tabb])
                k.op("dve", lambda e, raw=raw, tab=tab, nh=nh: e.tensor_tensor(
                    out=raw[:, :, :tb], in0=raw[:, :, :tb], in1=tab[:, 0:1, :tb].to_broadcast([64, nh, tb]), op=ALU.mult),
                    reads=[rawb, tabb], writes=[rawb])
                k.op("pool", lambda e, sw=sw, tab=tab, nh=nh: e.tensor_tensor(
                    out=sw[:, :, :tb], in0=sw[:, :, :tb], in1=tab[:, 1:2, :tb].to_broadcast([64, nh, tb]), op=ALU.mult),
                    reads=[swb, tabb], writes=[swb])
                k.op("dve", lambda e, raw=raw, sw=sw, dst=dst: e.tensor_tensor(
                    out=dst[:, :, t0:t0 + tb], in0=raw[:, :, :tb], in1=sw[:, :, :tb], op=ALU.add),
                    reads=[rawb, swb], writes=[dstb[bi]])
        allq = Qbb + Kbb
        yield
        for qt in range(T // 128):
            yield
            q0 = qt * 128
            is_ctx = qt < 2
            if is_ctx:
                lat_tiles = []
            else:
                n = qt - 2
                lat_tiles = [tt for tt in (n - 1, n, n + 1) if 0 <= tt < 16]
            jlo = 0
            if not is_ctx:
                jlo = (lat_tiles[0] - (n - 1)) * 128
            nb = len(lat_tiles) * 128
            key_tiles = [2 + tt for tt in lat_tiles] + [0, 1]
            ob_t, ob_b = k.psum_hold()
            rinv, rinvb = k.pool("a_rinv", [128, 8], F32, 2)
            for h in range(8):
                g = h // 4
                s, sb_ = k.pool("a_s", [128, 640], F32, 2)
                if nb < 384:
                    k.op("pool", lambda e, s=s: e.memset(s[:, 0:384], NEG), writes=[sb_])
                if nb > 0:
                    pa, pab = k.psum()
                    k0 = CTX + lat_tiles[0] * 128
                    k.op("pe", lambda e, pa=pa, h=h, g=g, k0=k0: e.matmul(
                        pa[:, :nb], lhsT=Qb[:, h, q0:q0 + 128], rhs=Kb[:, g, k0:k0 + nb], start=True, stop=True),
                        reads=allq, writes=[pab])
                    k.op("dve", lambda e, pa=pa, s=s: e.tensor_tensor(
                        out=s[:, jlo:jlo + nb], in0=pa[:, :nb], in1=band[:, jlo:jlo + nb], op=ALU.add),
                        reads=[pab, bandb], writes=[sb_])
                pc_, pcb = k.psum()
                k.op("pe", lambda e, pc_=pc_, h=h, g=g: e.matmul(
                    pc_[:, :256], lhsT=Qb[:, h, q0:q0 + 128], rhs=Kb[:, g, 0:256], start=True, stop=True),
                    reads=allq, writes=[pcb])
                k.op("act", lambda e, pc_=pc_, s=s: e.copy(out=s[:, 384:640], in_=pc_[:, :256]),
                     reads=[pcb], writes=[sb_])
                st, stb = k.pool("a_stat", [128, 4], F32, 4)
                k.op("dve", lambda e, s=s, st=st: e.reduce_max(out=st[:, 0:1], in_=s[:, :], axis=AX.X),
                     reads=[sb_], writes=[stb])
                k.op("dve", lambda e, st=st: e.tensor_scalar(out=st[:, 1:2], in0=st[:, 0:1], scalar1=-scale, scalar2=None,
                                                           op0=ALU.mult), reads=[stb], writes=[stb])
                P, Pb_ = k.pool("a_P", [128, 640], BF16, 2)
                k.op("act", lambda e, s=s, st=st, P=P: e.activation(
                    out=P[:], in_=s[:], func=AF.Exp, bias=st[:, 1:2], scale=scale, accum_out=st[:, 2:3]),
                    reads=[sb_, stb], writes=[Pb_, stb])
                k.op("act", lambda e, st=st, h=h: e.activation(
                    out=st[:, 3:4], in_=st[:, 1:2], func=AF.Exp, bias=sink[:, h:h + 1], scale=1.0),
                    reads=[stb, sinkb], writes=[stb])
                k.op("dve", lambda e, st=st: e.tensor_tensor(out=st[:, 2:3], in0=st[:, 2:3], in1=st[:, 3:4], op=ALU.add),
                     reads=[stb], writes=[stb])
                k.op("dve", lambda e, st=st, h=h: e.reciprocal(out=rinv[:, h:h + 1], in_=st[:, 2:3]),
                     reads=[stb], writes=[rinvb])
                pt_, ptb = k.psum()
                ptv = pt_[:].bitcast(BF16)
                cols = [jlo // 128 + i for i in range(len(lat_tiles))] + [3, 4]
                for ci, cb in enumerate(cols):
                    k.op("pe", lambda e, ptv=ptv, P=P, ci=ci, cb=cb: e.transpose(
                        ptv[:, ci * 128:(ci + 1) * 128], P[:, cb * 128:(cb + 1) * 128], identb[:]),
                        reads=[Pb_, identbb], writes=[ptb])
                nk = len(cols)
                PT, PTb = k.pool("a_PT", [128, 5, 128], BF16, 2)
                k.op("act" if h % 2 == 0 else "dve",
                     (lambda e, ptv=ptv, PT=PT, nk=nk: e.copy(out=PT[:, :nk, :].rearrange("p a b -> p (a b)"), in_=ptv[:, :nk * 128]))
                     if h % 2 == 0 else
                     (lambda e, ptv=ptv, PT=PT, nk=nk: e.tensor_copy(out=PT[:, :nk, :].rearrange("p a b -> p (a b)"), in_=ptv[:, :nk * 128])),
                     reads=[ptb], writes=[PTb])
                for ci, kt in enumerate(key_tiles):
                    k.op("pe", lambda e, PT=PT, ci=ci, kt=kt, h=h, g=g: e.matmul(
                        ob_t[:, h * 64:(h + 1) * 64], lhsT=PT[:, ci, :], rhs=Vr[:, kt, g * 64:(g + 1) * 64],
                        start=(ci == 0), stop=(ci == nk - 1)), reads=[PTb, Vrb], writes=[ob_b])
            o_tm, o_tmb = k.pool("a_otm", [128, 8, 64], BF16, 2)
            k.op("dve", lambda e, o_tm=o_tm, rinv=rinv: e.tensor_tensor(
                out=o_tm[:], in0=ob_t[:, :].rearrange("p (h d) -> p h d", d=64),
                in1=rinv[:, :].unsqueeze(2).to_broadcast([128, 8, 64]), op=ALU.mult),
                reads=[ob_b, rinvb], writes=[o_tmb])
            pt_, ptb = k.psum()
            ptv = pt_[:].bitcast(BF16)
            for c4 in range(4):
                k.op("pe", lambda e, ptv=ptv, o_tm=o_tm, c4=c4: e.transpose(
                    ptv[:, c4 * 128:(c4 + 1) * 128],
                    o_tm[:, 2 * c4:2 * c4 + 2, :].rearrange("p h d -> p (h d)"), identb[:]),
                    reads=[o_tmb, identbb], writes=[ptb])
            o_fm, o_fmb = k.pool("a_ofm", [128, 4, 128], BF16, 2)
            k.op("act", lambda e, ptv=ptv, o_fm=o_fm: e.copy(out=o_fm[:].rearrange("p a b -> p (a b)"), in_=ptv[:, :512]),
                 reads=[ptb], writes=[o_fmb])
            k.dma("sp", k.FEATS[8:12, :, q0:q0 + 128].rearrange("c p t -> p c t"), o_fm[:],
                  reads=[o_fmb], writes=[k.FEATSb[2][qt]])


def att_host_prep(inputs):
    tab, band = att_consts()
    sink = np.asarray(inputs["att_sink"], np.float32)
    return {"rope": tab, "band": band,
            "sinkb": np.ascontiguousarray(np.broadcast_to(sink[:, None, :], (DEPTH, 128, 8)))}


def fno_consts():
    c = np.arange(128)
    ang = 2 * np.pi * np.outer(c, c) / 128.0
    c128 = np.concatenate([np.cos(ang), np.sin(ang)], axis=1) / np.sqrt(128.0)
    tabs = {}
    for L in (SEQ, CTX):
        l = np.arange(L, dtype=np.int64)
        m = np.outer(l, l) % L
        a = 2 * np.pi * m / L
        tabs[L] = (np.stack([np.cos(a), -np.sin(a)], axis=1) / np.sqrt(L))
    return (c128.astype(ml_dtypes.bfloat16), tabs[SEQ].astype(ml_dtypes.bfloat16), tabs[CTX].astype(ml_dtypes.bfloat16))


class MKC(MKA):
    def declare_cf(self):
        self.convp_d = self.dram("convp", [DEPTH, 128, 4, 34], F32, kind="ExternalInput")
        self.c128_d = self.dram("c128", [128, 256], BF16, kind="ExternalInput")
        self.fL_d = self.dram("fL", [SEQ, 2, SEQ], BF16, kind="ExternalInput")
        self.fC_d = self.dram("fC", [CTX, 2, CTX], BF16, kind="ExternalInput")

    def conv(self, l):
        k = self
        cp = k.sb("convp_t", [128, 4, 34], F32)
        cpb = Buf()
        k.dma("sp", cp[:], k.convp_d[l], writes=[cpb])
        for (s0, L) in [(0, CTX), (CTX, SEQ)]:
            co = k.sb(f"convo{L}", [128, 4, L], F32)
            cob = [Buf() for _ in range(4)]
            for j in range(4):
                yield
                at, ab = k.pool(f"cv_a{L}", [128, L], F32, 1)
                bt, bb = k.pool(f"cv_b{L}", [128, L], F32, 1)
                k.dma("sp", at[:], k.P128[5 + j, :, s0:s0 + L], reads=k.P128b[5 + j], writes=[ab])
                k.dma("sp", bt[:], k.P128[9 + j, :, s0:s0 + L], reads=k.P128b[9 + j], writes=[bb])
                k.op("act", lambda e, bt=bt: e.activation(out=bt[:], in_=bt[:], func=AF.Sigmoid), reads=[bb], writes=[bb])
                hp, hpb = k.pool(f"cv_h{L}", [128, L + 30], F32, 2)
                k.op("pool", lambda e, hp=hp: e.memset(hp[:, 0:15], 0.0), writes=[hpb])
                k.op("pool", lambda e, hp=hp: e.memset(hp[:, L + 15:L + 30], 0.0), writes=[hpb])
                k.op("pool", lambda e, hp=hp, at=at, bt=bt: e.tensor_tensor(out=hp[:, 15:15 + L], in0=at[:], in1=bt[:], op=ALU.mult),
                     reads=[ab, bb], writes=[hpb])
                k.op("dve", lambda e, hp=hp, j=j: e.tensor_scalar(
                    out=co[:, j, :], in0=hp[:, 0:L], scalar1=cp[:, j, 0:1], scalar2=cp[:, j, 31:32],
                    op0=ALU.mult, op1=ALU.add), reads=[hpb, cpb], writes=[cob[j]])
                for tap in range(1, 31):
                    k.op("dve", lambda e, hp=hp, j=j, tap=tap: e.scalar_tensor_tensor(
                        out=co[:, j, :], in0=hp[:, tap:tap + L], scalar=cp[:, j, tap:tap + 1], in1=co[:, j, :],
                        op0=ALU.mult, op1=ALU.add), reads=[hpb, cpb, cob[j]], writes=[cob[j]])
            for t0 in range(0, L, 512):
                yield
                tb = min(512, L - t0)
                pm, pmb = k.psum()
                for j in range(4):
                    k.op("pe", lambda e, pm=pm, j=j: e.matmul(pm[:, :tb], lhsT=k.ones_f[:], rhs=co[:, j, t0:t0 + tb],
                                                             start=(j == 0), stop=(j == 3)), reads=[cob[j], k.ones_fb], writes=[pmb])
                for j in range(4):
                    k.op("dve", lambda e, pm=pm, j=j: e.scalar_tensor_tensor(
                        out=co[:, j, t0:t0 + tb], in0=pm[:, :tb], scalar=-1.0 / 512, in1=co[:, j, t0:t0 + tb],
                        op0=ALU.mult, op1=ALU.add), reads=[pmb, cob[j]], writes=[cob[j]])
                pv, pvb = k.psum()
                for j in range(4):
                    sq, sqb = k.pool("cv_sq", [128, 512], F32, 2)
                    k.op("act", lambda e, sq=sq, j=j: e.activation(out=sq[:, :tb], in_=co[:, j, t0:t0 + tb], func=AF.Square),
                         reads=[cob[j]], writes=[sqb])
                    k.op("pe", lambda e, pv=pv, sq=sq, j=j: e.matmul(pv[:, :tb], lhsT=k.ones_f[:], rhs=sq[:, :tb],
                                                                    start=(j == 0), stop=(j == 3)), reads=[sqb, k.ones_fb], writes=[pvb])
                rs, rsb = k.pool("cv_rs", [128, 512], F32, 2)
                k.op("act", lambda e, rs=rs, pv=pv: e.activation(out=rs[:, :tb], in_=pv[:, :tb], func=AF.Sqrt,
                                                                scale=1.0 / 512, bias=k.epsc[:, 2:3]), reads=[pvb, k.epsb], writes=[rsb])
                k.op("dve", lambda e, rs=rs: e.reciprocal(out=rs[:, :tb], in_=rs[:, :tb]), reads=[rsb], writes=[rsb])
                for j in range(4):
                    y, yb = k.pool("cv_y", [128, 512], F32, 2)
                    k.op("dve", lambda e, y=y, rs=rs, j=j: e.tensor_tensor(out=y[:, :tb], in0=co[:, j, t0:t0 + tb], in1=rs[:, :tb],
                                                                          op=ALU.mult), reads=[cob[j], rsb], writes=[yb])
                    k.op("act", lambda e, y=y, j=j: e.activation(out=y[:, :tb], in_=y[:, :tb], func=AF.Identity,
                                                                 scale=cp[:, j, 32:33], bias=cp[:, j, 33:34]), reads=[yb, cpb], writes=[yb])
                    sg, sgb = k.pool("cv_sg", [128, 512], F32, 2)
                    k.op("act", lambda e, y=y, sg=sg: e.activation(out=sg[:, :tb], in_=y[:, :tb], func=AF.Sigmoid),
                         reads=[yb], writes=[sgb])
                    o, ob = k.pool("cv_o", [128, 512], BF16, 2)
                    k.op("pool", lambda e, y=y, sg=sg, o=o: e.tensor_tensor(out=o[:, :tb], in0=y[:, :tb], in1=sg[:, :tb], op=ALU.mult),
                         reads=[yb, sgb], writes=[ob])
                    tt0 = s0 + t0
                    k.dma("sp", k.FEATS[12 + j, :, tt0:tt0 + tb], o[:, :tb], reads=[ob],
                          writes=[k.FEATSb[3][(tt0 // 128) + i] for i in range(tb // 128)])

    def fno(self, l, with_ctx=True):
        k = self
        c128 = k.sb("c128_t", [128, 256], BF16)
        c128b = Buf()
        k.dma("sp", c128[:], k.c128_d[:], writes=[c128b])
        for (s0, L, ftab) in [(0, CTX, k.fC_d), (CTX, SEQ, k.fL_d)]:
            if L == CTX and not with_ctx:
                continue
            nt = L // 128
            A = k.sb(f"fnoA{L}", [128, nt, 4, 256], BF16)
            Ab = Buf()
            for g in range(4):
                yield
                u, ub = k.pool(f"fno_u{L}", [128, L], F32, 1)
                k.dma("sp", u[:], k.P128[1 + g, :, s0:s0 + L], reads=k.P128b[1 + g], writes=[ub])
                ubf, ubfb = k.pool(f"fno_ub{L}", [128, L], BF16, 2)
                k.op("act", lambda e, u=u, ubf=ubf: e.copy(out=ubf[:], in_=u[:]), reads=[ub], writes=[ubfb])
                for ti in range(nt):
                    pt, pb = k.psum()
                    k.op("pe", lambda e, pt=pt, ubf=ubf, ti=ti: e.matmul(
                        pt[:, :256], lhsT=ubf[:, ti * 128:(ti + 1) * 128], rhs=c128[:], start=True, stop=True),
                        reads=[ubfb, c128b], writes=[pb])
                    if ti % 2 == 0:
                        k.op("act", lambda e, pt=pt, ti=ti, g=g: e.copy(out=A[:, ti, g, :], in_=pt[:, :256]), reads=[pb], writes=[Ab])
                    else:
                        k.op("dve", lambda e, pt=pt, ti=ti, g=g: e.tensor_copy(out=A[:, ti, g, :], in_=pt[:, :256]), reads=[pb], writes=[Ab])
            for m0 in range(0, L, 512):
                mb = min(512, L - m0)
                F, Fb = k.pool(f"fno_F{L}", [128, nt, 2, mb], BF16, 1)
                for hlf in range(2):
                    n0, n1 = (hlf * nt) // 2, ((hlf + 1) * nt) // 2
                    for s in range(2):
                        k.dma("sp", F[:, n0:n1, s, :], ftab[n0 * 128:n1 * 128, s, m0:m0 + mb].rearrange("(n p) m -> p n m", p=128),
                              writes=[Fb])
                for g in range(4):
                    yield
                    pt, pb = k.psum()
                    for ti in range(nt):
                        for s in range(2):
                            k.op("pe", lambda e, pt=pt, F=F, ti=ti, s=s, g=g: e.matmul(
                                pt[:, :mb], lhsT=A[:, ti, g, s * 128:(s + 1) * 128], rhs=F[:, ti, s, :],
                                start=(ti == 0 and s == 0), stop=(ti == nt - 1 and s == 1)), reads=[Ab, Fb], writes=[pb])
                    o, ob = k.pool("fno_o", [128, 512], BF16, 2)
                    k.op("act" if g % 2 == 0 else "dve",
                         (lambda e, pt=pt, o=o: e.copy(out=o[:, :mb], in_=pt[:, :mb])) if g % 2 == 0 else
                         (lambda e, pt=pt, o=o: e.tensor_copy(out=o[:, :mb], in_=pt[:, :mb])), reads=[pb], writes=[ob])
                    tt0 = s0 + m0
                    k.dma("sp", k.FEATS[g, :, tt0:tt0 + mb], o[:, :mb], reads=[ob],
                          writes=[k.FEATSb[0][(tt0 // 128) + i] for i in range(mb // 128)])


def cf_host_prep(inputs):
    f32 = np.float32
    cp = np.zeros((DEPTH, 128, 4, 34), f32)
    for l in range(DEPTH):
        dw = np.asarray(inputs["conv_dw"], f32)[l]
        cp[l, :, :, 0:31] = dw.T.reshape(4, 128, 31).transpose(1, 0, 2)
        cp[l, :, :, 31] = np.asarray(inputs["conv_dw_b"], f32)[l].reshape(4, 128).T
        cp[l, :, :, 32] = np.asarray(inputs["conv_ln_g"], f32)[l].reshape(4, 128).T
        cp[l, :, :, 33] = np.asarray(inputs["conv_ln_b"], f32)[l].reshape(4, 128).T
    c128, fL, fC = fno_consts()
    return {"convp": cp, "c128": c128, "fL": fL, "fC": fC}


class MKB(MKC):
    def declare_b(self):
        self.wg_d = self.dram("wg", [DEPTH, 4, 16, 128, KC, 128], F32, kind="ExternalInput")
        self.wb_d = self.dram("wbr", [DEPTH, 4, 16, 128, 4, 128], F32, kind="ExternalInput")
        self.wo_d = self.dram("wo", [DEPTH, 16, 128, KC, 128], F32, kind="ExternalInput")
        self.w1_d = self.dram("w1", [DEPTH, 64, 128, KC, 128], F32, kind="ExternalInput")
        self.w2_d = self.dram("w2", [DEPTH, 16, 4, 128, KC, 128], F32, kind="ExternalInput")
        self.XS = self.dram("XS", [KC, 128, T], F32)
        self.XSb = [Buf() for _ in TBS]
        self.yout = self.dram("yout", [KC, 128, SEQ], F32, kind="ExternalOutput")
        self.youtb = Buf()

    def wload(self, src, kcn=KC):
        wt, wb = self.pool(f"wB{kcn}", [128, kcn, 128], BF16, 4)
        self.dma("pool", wt[:], src, writes=[wb])
        return wt, wb

    def phase_b(self, l, xsrc, xsrc_bufs, last):
        k = self
        mod = k.mod[l]
        import os
        only = os.environ.get('PB_ONLY')
        for bi, (t0, tb) in enumerate(TBS):
            if bi == 0 and last:
                continue
            if only is not None and bi != int(only):
                continue
            ci = 1 if bi == 0 else 0
            xt, xb = k.pool("xblk", [128, KC, 512], F32, 1)
            for q in range(4):
                k.dma("sp", xt[:, 4 * q:4 * q + 4, :tb],
                      xsrc[4 * q:4 * q + 4, :, t0:t0 + tb].rearrange("k p t -> p k t"),
                      reads=[xsrc_bufs[bi]], writes=[xb])
            hx, hxb = k.pool("b_hx", [128, KC, 512], BF16, 1)
            ft, ftb = k.pool("b_ft", [128, KC, 512], BF16, 1)
            k.dma("sp", hx[:, :, :tb], k.HX[:, :, t0:t0 + tb].rearrange("k p t -> p k t"), reads=[k.HXb[bi]], writes=[hxb])
            fr = [b_ for br in range(4) for b_ in k.FEATSb[br][t0 // 128:(t0 + tb) // 128]]
            for q in range(4):
                k.dma("sp", ft[:, 4 * q:4 * q + 4, :tb], k.FEATS[4 * q:4 * q + 4, :, t0:t0 + tb].rearrange("k p t -> p k t"),
                      reads=fr, writes=[ftb])
            mt, mb_ = k.pool("b_m", [128, KC, 512], BF16, 1)
            for dc in range(16):
                macc, maccb = k.pool("b_macc", [128, 512], F32, 2)
                for i in range(4):
                    wg, wgb = k.wload(k.wg_d[l, i, dc])
                    wbt, wbb = k.wload(k.wb_d[l, i, dc], 4)
                    pg, pgb = k.psum()
                    for kc in range(KC):
                        k.op("pe", lambda e, pg=pg, wg=wg, kc=kc: e.matmul(pg[:, :tb], lhsT=wg[:, kc, :], rhs=hx[:, kc, :tb],
                                                                          start=(kc == 0), stop=(kc == KC - 1)),
                             reads=[wgb, hxb], writes=[pgb])
                    pp, ppb = k.psum()
                    for kc in range(4):
                        k.op("pe", lambda e, pp=pp, wbt=wbt, kc=kc, i=i: e.matmul(pp[:, :tb], lhsT=wbt[:, kc, :], rhs=ft[:, 4 * i + kc, :tb],
                                                                                 start=(kc == 0), stop=(kc == 3)),
                             reads=[wbb, ftb], writes=[ppb])
                    sg, sgb = k.pool("b_sig", [128, 512], F32, 2)
                    k.op("act", lambda e, pg=pg, sg=sg: e.activation(out=sg[:, :tb], in_=pg[:, :tb], func=AF.Sigmoid),
                         reads=[pgb], writes=[sgb])
                    if i == 0:
                        k.op("dve", lambda e, pp=pp, sg=sg, macc=macc: e.tensor_tensor(out=macc[:, :tb], in0=pp[:, :tb], in1=sg[:, :tb], op=ALU.mult),
                             reads=[ppb, sgb], writes=[maccb])
                    else:
                        k.op("dve", lambda e, pp=pp, sg=sg: e.tensor_tensor(out=sg[:, :tb], in0=pp[:, :tb], in1=sg[:, :tb], op=ALU.mult),
                             reads=[ppb, sgb], writes=[sgb])
                        if i < 3:
                            k.op("dve", lambda e, sg=sg, macc=macc: e.tensor_tensor(out=macc[:, :tb], in0=macc[:, :tb], in1=sg[:, :tb], op=ALU.add),
                                 reads=[maccb, sgb], writes=[maccb])
                        else:
                            k.op("dve", lambda e, sg=sg, macc=macc, dc=dc: e.tensor_tensor(out=mt[:, dc, :tb], in0=macc[:, :tb], in1=sg[:, :tb], op=ALU.add),
                                 reads=[maccb, sgb], writes=[mb_])
            for dc in range(16):
                wo, wob = k.wload(k.wo_d[l, dc])
                py, pyb = k.psum()
                for kc in range(KC):
                    k.op("pe", lambda e, py=py, wo=wo, kc=kc: e.matmul(py[:, :tb], lhsT=wo[:, kc, :], rhs=mt[:, kc, :tb],
                                                                      start=(kc == 0), stop=(kc == KC - 1)),
                         reads=[wob, mb_], writes=[pyb])
                k.op("dve", lambda e, py=py, dc=dc: e.scalar_tensor_tensor(
                    out=xt[:, dc, :tb], in0=py[:, :tb], scalar=mod[:, 2 * KC + dc, ci:ci + 1], in1=xt[:, dc, :tb],
                    op0=ALU.mult, op1=ALU.add), reads=[pyb, xb, k.modb[l]], writes=[xb])
            k.norm_block(l, 1, xt, xb, t0, tb, bi == 0, hx, hxb, sq_tile=(ft, ftb))
            hm, hmb = k.pool("b_hmid", [128, 64, 512], BF16, 1)
            for fc in range(64):
                w1, w1b = k.wload(k.w1_d[l, fc])
                ph, phb = k.psum()
                for kc in range(KC):
                    k.op("pe", lambda e, ph=ph, w1=w1, kc=kc: e.matmul(ph[:, :tb], lhsT=w1[:, kc, :], rhs=hx[:, kc, :tb],
                                                                      start=(kc == 0), stop=(kc == KC - 1)),
                         reads=[w1b, hxb], writes=[phb])
                rl, rlb = k.pool("b_relu", [128, 512], F32, 2)
                k.op("act", lambda e, ph=ph, rl=rl: e.activation(out=rl[:, :tb], in_=ph[:, :tb], func=AF.Relu),
                     reads=[phb], writes=[rlb])
                k.op("dve", lambda e, rl=rl, fc=fc: e.tensor_tensor(out=hm[:, fc, :tb], in0=rl[:, :tb], in1=rl[:, :tb], op=ALU.mult),
                     reads=[rlb], writes=[hmb])
            for dc in range(16):
                py, pyb = k.psum()
                for q4 in range(4):
                    w2, w2b = k.wload(k.w2_d[l, dc, q4])
                    for kc in range(KC):
                        k.op("pe", lambda e, py=py, w2=w2, kc=kc, q4=q4: e.matmul(
                            py[:, :tb], lhsT=w2[:, kc, :], rhs=hm[:, q4 * KC + kc, :tb],
                            start=(q4 == 0 and kc == 0), stop=(q4 == 3 and kc == KC - 1)),
                            reads=[w2b, hmb], writes=[pyb])
                k.op("dve", lambda e, py=py, dc=dc: e.scalar_tensor_tensor(
                    out=xt[:, dc, :tb], in0=py[:, :tb], scalar=mod[:, 5 * KC + dc, ci:ci + 1], in1=xt[:, dc, :tb],
                    op0=ALU.mult, op1=ALU.add), reads=[pyb, xb, k.modb[l]], writes=[xb])
            if not last:
                for q in range(4):
                    k.dma("sp", k.XS[4 * q:4 * q + 4, :, t0:t0 + tb].rearrange("k p t -> p k t"), xt[:, 4 * q:4 * q + 4, :tb],
                          reads=[xb], writes=[k.XSb[bi]])
            else:
                k.final_norm(xt, xb, t0, tb, sq_tile=(ft, ftb))

    def final_norm(self, xt, xb, t0, tb, sq_tile=None):
        k = self
        sq, sqb = sq_tile if sq_tile is not None else k.pool("sq", [128, KC, 512], BF16, 1)
        k.op("act", lambda e: e.activation(out=sq[:, :, :tb], in_=xt[:, :, :tb], func=AF.Square), reads=[xb], writes=[sqb])
        pt, pb = k.psum()
        for kc in range(KC):
            k.op("pe", lambda e, kc=kc: e.matmul(pt[:, :tb], lhsT=k.ones_bf[:], rhs=sq[:, kc, :tb],
                                                 start=(kc == 0), stop=(kc == KC - 1)), reads=[sqb, k.ones_b], writes=[pb])
        rs, rsb = k.pool("rstd", [128, 512], F32, 2)
        k.op("act", lambda e: e.activation(out=rs[:, :tb], in_=pt[:, :tb], func=AF.Sqrt, scale=1.0 / D, bias=k.epsc[:, 0:1]),
             reads=[pb, k.epsb], writes=[rsb])
        k.op("dve", lambda e: e.reciprocal(out=rs[:, :tb], in_=rs[:, :tb]), reads=[rsb], writes=[rsb])
        for kc in range(KC):
            k.op("dve", lambda e, kc=kc: e.scalar_tensor_tensor(
                out=xt[:, kc, :tb], in0=xt[:, kc, :tb], scalar=k.ngt[:, 4, kc:kc + 1], in1=rs[:, :tb],
                op0=ALU.mult, op1=ALU.mult), reads=[xb, rsb, k.ngb], writes=[xb])
        for q in range(4):
            k.dma("sp", k.yout[4 * q:4 * q + 4, :, t0 - CTX:t0 - CTX + tb].rearrange("k p t -> p k t"), xt[:, 4 * q:4 * q + 4, :tb],
                  reads=[xb], writes=[k.youtb])

    def build_all(self):
        k = self
        k.declare_inputs()
        k.declare_rwkv()
        k.declare_att()
        k.declare_cf()
        k.declare_b()
        k.setup_eps()
        k.setup_consts()
        k.rwkv_setup()
        xsrc, xbufs = k.xin, [Buf() for _ in TBS]
        for l in range(DEPTH):
            last = (l == DEPTH - 1)
            with k.scope():
                k.phase_mod(l)
            with k.scope():
                k.phase_a(l, xsrc, xbufs)
            with k.scope():
                k.rwkv(l)
            with k.scope():
                k.run_gens([k.rwkv_readout_gen(l), k.attention(l)])
            with k.scope():
                k.run_gens([k.conv(l), k.fno(l, with_ctx=not last)])
            with k.scope():
                k.phase_b(l, xsrc, xbufs, last)
            xsrc, xbufs = k.XS, k.XSb
        k.finish([k.youtb])


def b_host_prep(inputs):
    f32 = np.float32
    w_in = np.asarray(inputs["w_in"], f32)
    wbr = np.asarray(inputs["w_branch"], f32)
    wo = np.asarray(inputs["w_out"], f32)
    w1 = np.asarray(inputs["w_mlp1"], f32)
    w2 = np.asarray(inputs["w_mlp2"], f32)
    out = {}
    g = w_in[:, :, O_GATE:].reshape(DEPTH, KC, 128, 4, 16, 128)
    out["wg"] = np.ascontiguousarray(g.transpose(0, 3, 4, 2, 1, 5))
    b = wbr.reshape(DEPTH, 4, 4, 128, 16, 128)
    out["wbr"] = np.ascontiguousarray(b.transpose(0, 1, 4, 3, 2, 5))
    o = wo.reshape(DEPTH, KC, 128, 16, 128)
    out["wo"] = np.ascontiguousarray(o.transpose(0, 3, 2, 1, 4))
    a = w1.reshape(DEPTH, KC, 128, 64, 128)
    out["w1"] = np.ascontiguousarray(a.transpose(0, 3, 2, 1, 4))
    c = w2.reshape(DEPTH, 4, KC, 128, 16, 128)
    out["w2"] = np.ascontiguousarray(c.transpose(0, 4, 1, 3, 2, 5))
    return out


def full_host_prep(inputs):
    shared, per_core = host_prep(inputs)
    shared.update(rwkv_host_prep(inputs))
    shared.update(att_host_prep(inputs))
    shared.update(cf_host_prep(inputs))
    shared.update(b_host_prep(inputs))
    return shared, per_core


def kernel(**inputs):
    shared, per_core = full_host_prep(inputs)
    k = MKB()
    k.build_all()
    n = 8
    in_maps = []
    for i in range(n):
        m = dict(shared)
        m.update(per_core[i % len(per_core)])
        in_maps.append(m)
    res = run_bass_kernel_spmd(k.nc, in_maps, core_ids=list(range(n)))
    B = len(per_core)
    out = np.stack([np.asarray(res.results[b]["yout"]).reshape(D, SEQ).T for b in range(B)])
    return np.ascontiguousarray(out.astype(np.float32))
```

```python
import os
import ml_dtypes
from concourse.bass_utils import run_bass_kernel_spmd
import contextlib
import numpy as np
import concourse.bass as bass
import concourse.mybir as mybir

F32 = mybir.dt.float32
BF16 = mybir.dt.bfloat16
I32 = mybir.dt.int32
AF = mybir.ActivationFunctionType
ALU = mybir.AluOpType
AX = mybir.AxisListType

NDMA = 64


class Buf:
    __slots__ = ("name", "w", "r")

    def __init__(self, name=""):
        self.name = name
        self.w = None
        self.r = {}


class Ctx:
    def __init__(self):
        self.nc = bass.Bass("TRN2", target_bir_lowering=False)
        nc = self.nc
        self.stack = contextlib.ExitStack()
        self.eng = {"pe": nc.tensor, "act": nc.scalar, "dve": nc.vector, "pool": nc.gpsimd, "sp": nc.sync}
        self.semh = {}
        for e in ["pe", "act", "dve", "pool"]:
            self.semh[e] = self.stack.enter_context(nc.semaphore("s_" + e))
        self.cnt = {e: 0 for e in ["pe", "act", "dve", "pool"]}
        self.seen = {e: {} for e in self.eng}
        self.dcnt = [0] * NDMA
        for i in range(NDMA):
            self.semh[("d", i)] = self.stack.enter_context(nc.semaphore(f"s_d{i}"))
        self.dnext = 0
        self.dnext_sw = 0
        self.n_ops = 0
        self.n_waits = 0
        self.out_tokens = []

    def sb(self, name, shape, dtype=F32):
        self._uid = getattr(self, "_uid", 0) + 1
        nm = f"{name}_{self._uid}"
        st = getattr(self, "cur_stack", None)
        if st is not None:
            return st.enter_context(self.nc.sbuf_tensor(nm, list(shape), dtype))
        return self.nc.alloc_sbuf_tensor(nm, list(shape), dtype)

    def barrier(self):
        for e in ["pe", "act", "dve", "pool", "sp"]:
            deps = {}
            for o in ["pe", "act", "dve", "pool"]:
                if self.cnt[o] > 0:
                    deps[o] = self.cnt[o]
            for i in range(NDMA):
                if self.dcnt[i] > 0:
                    deps[("d", i)] = 16 * self.dcnt[i]
            if getattr(self, "cccnt", 0) > 0:
                deps["cc"] = 16 * self.cccnt
            eng = self.eng[e]
            for k, v in deps.items():
                if self.seen[e].get(k, 0) < v:
                    eng.wait_ge(self.semh[k], v)
                    self.seen[e][k] = v
                    self.n_waits += 1

    @contextlib.contextmanager
    def scope(self):
        st = contextlib.ExitStack()
        prev = getattr(self, "cur_stack", None)
        saved = dict(self.pools) if hasattr(self, "pools") else None
        self.cur_stack = st
        try:
            yield
        finally:
            self.barrier()
            st.close()
            self.cur_stack = prev
            if saved is not None:
                self.pools = saved

    def ps(self, name, shape, dtype=F32):
        return self.nc.alloc_psum_tensor(name, list(shape), dtype)

    def dram(self, name, shape, dtype=F32, kind=None):
        if kind is None:
            return self.nc.dram_tensor(name, list(shape), dtype)
        return self.nc.dram_tensor(name, list(shape), dtype, kind=kind)

    def _deps(self, reads, writes):
        deps = {}

        def add(tok):
            if tok is None:
                return
            k, v = tok
            if deps.get(k, 0) < v:
                deps[k] = v

        for b in reads:
            add(b.w)
        for b in writes:
            add(b.w)
            for k, v in b.r.items():
                add((k, v))
        return deps

    def _wait(self, e, deps):
        eng = self.eng[e]
        for k, v in deps.items():
            if e == "pe" and k == "pe":
                continue
            if self.seen[e].get(k, 0) < v:
                eng.wait_ge(self.semh[k], v)
                self.seen[e][k] = v
                self.n_waits += 1

    def _mark(self, tok, reads, writes):
        k, v = tok
        for b in reads:
            if b.r.get(k, 0) < v:
                b.r[k] = v
        for b in writes:
            b.w = tok
            b.r = {}

    def op(self, e, fn, reads=(), writes=()):
        self._wait(e, self._deps(reads, writes))
        ins = fn(self.eng[e])
        self.cnt[e] += 1
        ins.then_inc(self.semh[e], 1)
        self._mark((e, self.cnt[e]), reads, writes)
        self.n_ops += 1
        return ins

    def dma(self, q, out, in_, reads=(), writes=(), **kw):
        if q == "pool":
            i = NDMA // 2 + self.dnext_sw
            self.dnext_sw = (self.dnext_sw + 1) % (NDMA // 2)
        else:
            i = self.dnext
            self.dnext = (i + 1) % (NDMA // 2)
        key = ("d", i)
        deps = self._deps(reads, writes)
        if self.dcnt[i] > 0:
            v = 16 * self.dcnt[i]
            if deps.get(key, 0) < v:
                deps[key] = v
        self._wait(q, deps)
        ins = self.eng[q].dma_start(out=out, in_=in_, **kw)
        self.dcnt[i] += 1
        ins.then_inc(self.semh[key], 16)
        tok = (key, 16 * self.dcnt[i])
        self._mark(tok, reads, writes)
        self.n_ops += 1
        return tok

    def collective(self, kind, in_ap, out_ap, reads=(), writes=()):
        if not hasattr(self, "ccsem"):
            self.semh["cc"] = self.stack.enter_context(self.nc.semaphore("s_cc"))
            self.cccnt = 0
        deps = self._deps(reads, writes)
        self._wait("pool", deps)
        ins = self.eng["pool"].collective_compute(kind, ALU.bypass, replica_groups=[[0, 1], [2, 3], [4, 5], [6, 7]],
                                                  ins=[in_ap], outs=[out_ap])
        self.cccnt += 1
        ins.then_inc(self.semh["cc"], 16)
        self.ccsem = True
        tok = ("cc", 16 * self.cccnt)
        self._mark(tok, reads, writes)
        return tok

    def finish(self, bufs):
        deps = self._deps(bufs, ())
        self._wait("sp", deps)
        for e in ["pe", "act", "dve", "pool"]:
            if self.cnt[e] > 0 and self.seen["sp"].get(e, 0) < self.cnt[e]:
                self.eng["sp"].wait_ge(self.semh[e], self.cnt[e])
        for i in range(NDMA):
            if self.dcnt[i] > 0 and self.seen["sp"].get(("d", i), 0) < 16 * self.dcnt[i]:
                self.eng["sp"].wait_ge(self.semh[("d", i)], 16 * self.dcnt[i])
        if getattr(self, "cccnt", 0) > 0:
            self.eng["sp"].wait_ge(self.semh["cc"], 16 * self.cccnt)


D = 2048
KC = 16
SEQ = 2048
CTX = 256
T = SEQ + CTX
DEPTH = 2
O_RKV = 0
O_LORA = 1536
O_KV = 1792
O_G = 2048
O_Q = 2176
O_FNO = 2688
O_CONV = 3200
O_GATE = 4224
IN_W = 12416
N64 = 48
N128 = 13
TBS = [(0, 256), (256, 512), (768, 512), (1280, 512), (1792, 512)]
NORM_EPS = 1e-6


def rope_partner():
    p = np.zeros(64, np.int64)
    for d in range(64):
        half = (d // 32) * 32
        i = d - half
        p[d] = half + (i + 16 if i < 16 else i - 16)
    return p


def cols64(l=None):
    cols = []
    for h in range(8):
        cols.append(np.arange(O_RKV + h * 64, O_RKV + (h + 1) * 64))
    for h in range(8):
        cols.append(np.arange(512 + h * 64, 512 + (h + 1) * 64))
    for h in range(8):
        cols.append(np.arange(1024 + h * 64, 1024 + (h + 1) * 64))
    for d in range(2):
        lo = O_LORA + d * 128
        cols.append(np.arange(lo, lo + 64))
        cols.append(np.arange(lo + 64, lo + 128))
    pr = rope_partner()
    for h in range(8):
        cols.append(np.arange(O_Q + h * 64, O_Q + (h + 1) * 64))
    for h in range(8):
        cols.append(O_Q + h * 64 + pr)
    for g in range(2):
        cols.append(np.arange(O_KV + g * 64, O_KV + (g + 1) * 64))
    for g in range(2):
        cols.append(O_KV + g * 64 + pr)
    assert len(cols) == N64
    return cols


def cols128():
    cols = [np.arange(O_G, O_G + 128)]
    for j in range(4):
        cols.append(np.arange(O_FNO + j * 128, O_FNO + (j + 1) * 128))
    for j in range(8):
        cols.append(np.arange(O_CONV + j * 128, O_CONV + (j + 1) * 128))
    assert len(cols) == N128
    return cols


def wlayout(w):
    M = w.shape[1]
    return np.ascontiguousarray(w.reshape(KC, 128, M).transpose(1, 0, 2))


def colvec(v):
    return np.ascontiguousarray(v.reshape(-1, 128).T)


class MK(Ctx):
    def __init__(self, debug=()):
        super().__init__()
        self.debug = set(debug)
        self.dbg_out = {}
        self.psb = [(self.ps(f"psb{i}", [128, 512], F32), Buf(f"psb{i}")) for i in range(8)]
        self.psi = 0
        self.pools = {}

    def psum(self):
        t, b = self.psb[self.psi]
        self.psi = (self.psi + 1) % 6
        return t, b

    def run_gens(self, gens):
        gens = [g for g in gens if g is not None]
        while gens:
            for g_ in list(gens):
                try:
                    next(g_)
                except StopIteration:
                    gens.remove(g_)

    def psum_hold(self):
        self.psh = getattr(self, "psh", 0)
        t, b = self.psb[6 + self.psh]
        self.psh = 1 - self.psh
        return t, b

    def pool(self, name, shape, dtype, n):
        if name not in self.pools:
            self.pools[name] = [[(self.sb(f"{name}{i}", shape, dtype), Buf(f"{name}{i}")) for i in range(n)], 0]
        p = self.pools[name]
        t, b = p[0][p[1]]
        p[1] = (p[1] + 1) % len(p[0])
        return t, b

    def tap(self, name, src_ap, src_buf, shape, dtype=F32):
        if name not in self.debug:
            return
        if name not in self.dbg_out:
            self.dbg_out[name] = (self.dram("dbg_" + name, shape, dtype, kind="ExternalOutput"), Buf("dbg_" + name))
        return self.dbg_out[name]

    def declare_inputs(self):
        self.xin = self.dram("xin", [KC, 128, T], F32, kind="ExternalInput")
        self.ccol = self.dram("ccol", [128, KC, 2], F32, kind="ExternalInput")
        self.adaw = self.dram("adaw", [DEPTH, 96, 128, KC, 128], F32, kind="ExternalInput")
        self.adab = self.dram("adab", [DEPTH, 128, 96], F32, kind="ExternalInput")
        self.ng = self.dram("ng", [128, 5, KC], F32, kind="ExternalInput")
        self.wa64 = self.dram("wa64", [DEPTH, N64, 128, KC, 64], F32, kind="ExternalInput")
        self.wa128 = self.dram("wa128", [DEPTH, N128, 128, KC, 128], F32, kind="ExternalInput")
        self.wav = self.dram("wav", [DEPTH, 128, KC, 128], F32, kind="ExternalInput")
        self.HX = self.dram("HX", [KC, 128, T], BF16)
        self.HXb = [Buf(f"HX{i}") for i in range(len(TBS))]
        self.P64 = self.dram("P64", [N64, 64, T], F32)
        self.P64b = [[Buf() for _ in TBS] for _ in range(N64)]
        self.P128 = self.dram("P128", [N128, 128, T], F32)
        self.P128b = [[Buf() for _ in TBS] for _ in range(N128)]
        self.VTM = self.dram("VTM", [T, 128], BF16)
        self.VTMb = [Buf() for _ in range(T // 128)]

    def setup_consts(self):
        self.ones_bf = self.sb("ones_bf", [128, 128], BF16)
        self.ones_b = Buf("ones_bf")
        self.op("pool", lambda e: e.memset(self.ones_bf[:], 1.0), writes=[self.ones_b])
        self.ones_f = self.sb("ones_f", [128, 128], F32)
        self.ones_fb = Buf("ones_f")
        self.op("pool", lambda e: e.memset(self.ones_f[:], 1.0), writes=[self.ones_fb])
        self.ngt = self.sb("ngt", [128, 5, KC], F32)
        self.ngb = Buf("ng")
        self.dma("sp", self.ngt[:], self.ng[:], writes=[self.ngb])
        self.sc = self.sb("sc", [128, KC, 2], F32)
        self.scb = Buf("sc")
        self.dma("sp", self.sc[:], self.ccol[:], writes=[self.scb])
        self.op("act", lambda e: e.activation(out=self.sc[:], in_=self.sc[:], func=AF.Silu),
                reads=[self.scb], writes=[self.scb])
        self.mod = [self.sb(f"mod{l}", [128, 96, 2], F32) for l in range(DEPTH)]
        self.modb = [Buf(f"mod{l}") for l in range(DEPTH)]
        self.geff = [[self.sb(f"geff{l}_{n}", [128, KC, 2], F32) for n in range(2)] for l in range(DEPTH)]
        self.geffb = [[Buf() for n in range(2)] for l in range(DEPTH)]

    def phase_mod(self, l):
        pt, pb = self.psum_hold()
        tiles = []
        for fc in range(2):
            wt, wb = self.pool("adaw", [128, KC, 128], F32, 3)
            self.dma("sp", wt[:], self.adaw[l, fc], writes=[wb])
            tiles.append((wt, wb))
        for fc in range(96):
            if fc + 2 < 96:
                wt, wb = self.pool("adaw", [128, KC, 128], F32, 3)
                self.dma("sp", wt[:], self.adaw[l, fc + 2], writes=[wb])
                tiles.append((wt, wb))
            yield
            yield
            wt, wb = tiles[fc]
            for kc in range(KC):
                self.op("pe", lambda e, kc=kc, fc=fc, wt=wt: e.matmul(
                    pt[:, 2 * fc:2 * fc + 2], lhsT=wt[:, kc, :], rhs=self.sc[:, kc, :],
                    start=(kc == 0), stop=(kc == KC - 1)),
                    reads=[wb, self.scb], writes=[pb])
        bt, bb = self.pool("adab", [128, 96], F32, 1)
        self.dma("sp", bt[:], self.adab[l], writes=[bb])
        mod = self.mod[l]
        self.op("dve", lambda e: e.tensor_tensor(
            out=mod[:], in0=pt[:, 0:192].rearrange("p (f c) -> p f c", c=2),
            in1=bt[:].unsqueeze(2).to_broadcast([128, 96, 2]), op=ALU.add),
            reads=[pb, bb], writes=[self.modb[l]])
        self.phase_geff(l)

    def phase_geff(self, l):
        mod = self.mod[l]
        for n, (gi, sci) in enumerate([(l, 1), (2 + l, 4)]):
            ge = self.geff[l][n]
            self.op("dve", lambda e, ge=ge, gi=gi, sci=sci: e.scalar_tensor_tensor(
                out=ge[:], in0=mod[:, sci * KC:(sci + 1) * KC, :], scalar=1.0,
                in1=self.ngt[:, gi, :].unsqueeze(2).to_broadcast([128, KC, 2]),
                op0=ALU.add, op1=ALU.mult),
                reads=[self.modb[l], self.ngb], writes=[self.geffb[l][n]])

    def norm_block(self, l, n, xt, xb, t0, tb, is_ctx, out_t, out_b, sq_tile=None):
        ci = 1 if is_ctx else 0
        shi = 0 if n == 0 else 3
        sq, sqb = sq_tile if sq_tile is not None else self.pool("sq", [128, KC, 512], BF16, 1)
        self.op("act", lambda e: e.activation(out=sq[:, :, :tb], in_=xt[:, :, :tb], func=AF.Square),
                reads=[xb], writes=[sqb])
        pt, pb = self.psum()
        for kc in range(KC):
            self.op("pe", lambda e, kc=kc: e.matmul(pt[:, :tb], lhsT=self.ones_bf[:], rhs=sq[:, kc, :tb],
                                                   start=(kc == 0), stop=(kc == KC - 1)),
                    reads=[sqb, self.ones_b], writes=[pb])
        rs, rsb = self.pool("rstd", [128, 512], F32, 2)
        self.op("act", lambda e: e.activation(out=rs[:, :tb], in_=pt[:, :tb], func=AF.Sqrt,
                                              scale=1.0 / D, bias=self.epsc[:, 0:1]),
                reads=[pb, self.epsb], writes=[rsb])
        self.op("dve", lambda e: e.reciprocal(out=rs[:, :tb], in_=rs[:, :tb]), reads=[rsb], writes=[rsb])
        ge = self.geff[l][n]
        mod = self.mod[l]
        for kc in range(KC):
            tmp, tmpb = self.pool("ntmp", [128, 512], F32, 3)
            self.op("dve", lambda e, kc=kc, tmp=tmp: e.scalar_tensor_tensor(
                out=tmp[:, :tb], in0=xt[:, kc, :tb], scalar=ge[:, kc, ci:ci + 1], in1=rs[:, :tb],
                op0=ALU.mult, op1=ALU.mult),
                reads=[xb, rsb, self.geffb[l][n]], writes=[tmpb])
            self.op("act", lambda e, kc=kc, tmp=tmp: e.activation(
                out=out_t[:, kc, :tb], in_=tmp[:, :tb], func=AF.Identity,
                bias=mod[:, shi * KC + kc, ci:ci + 1], scale=1.0),
                reads=[tmpb, self.modb[l]], writes=[out_b])

    def setup_eps(self):
        self.epsc = self.sb("epsc", [128, 4], F32)
        self.epsb = Buf("eps")
        self.op("pool", lambda e: e.memset(self.epsc[:, 0:1], NORM_EPS), writes=[self.epsb])
        self.op("pool", lambda e: e.memset(self.epsc[:, 1:2], 64e-5), writes=[self.epsb])
        self.op("pool", lambda e: e.memset(self.epsc[:, 2:3], 1e-5), writes=[self.epsb])
        self.op("pool", lambda e: e.memset(self.epsc[:, 3:4], 0.0), writes=[self.epsb])

    def phase_a(self, l, xsrc, xsrc_bufs):
        hx = self.sb(f"hx_res{l}", [128, KC, T], BF16)
        hxb = [Buf() for _ in TBS]
        for bi, (t0, tb) in enumerate(TBS):
            xt, xb = self.pool("xblk", [128, KC, 512], F32, 1)
            for q in range(4):
                self.dma("sp", xt[:, 4 * q:4 * q + 4, :tb],
                         xsrc[4 * q:4 * q + 4, :, t0:t0 + tb].rearrange("k p t -> p k t"),
                         reads=[xsrc_bufs[bi]], writes=[xb])
            self.norm_block(l, 0, xt, xb, t0, tb, bi == 0, hx[:, :, t0:t0 + tb], hxb[bi])
            self.dma("sp", self.HX[:, :, t0:t0 + tb].rearrange("k p t -> p k t"), hx[:, :, t0:t0 + tb],
                     reads=[hxb[bi]], writes=[self.HXb[bi]])
        for grp, n, wsrc, dst, dstb in [(64, N64, self.wa64, self.P64, self.P64b),
                                        (128, N128, self.wa128, self.P128, self.P128b)]:
            nj = n // 2 if grp == 64 else n
            for c in range(nj):
                wt, wb = self.pool("wa128", [128, KC, 128], BF16, 3)
                if grp == 64:
                    self.dma("pool", wt[:, :, 0:64], wsrc[l, 2 * c], writes=[wb])
                    self.dma("pool", wt[:, :, 64:128], wsrc[l, 2 * c + 1], writes=[wb])
                else:
                    self.dma("pool", wt[:], wsrc[l, c], writes=[wb])
                for bi, (t0, tb) in enumerate(TBS):
                    pt, pb = self.psum()
                    for kc in range(KC):
                        self.op("pe", lambda e, kc=kc, wt=wt, pt=pt: e.matmul(
                            pt[:, :tb], lhsT=wt[:, kc, :], rhs=hx[:, kc, t0:t0 + tb],
                            start=(kc == 0), stop=(kc == KC - 1)),
                            reads=[wb, hxb[bi]], writes=[pb])
                    ot, ob = self.pool(f"pa_o", [128, 512], F32, 4)
                    if bi % 2 == 0:
                        self.op("act", lambda e, ot=ot, pt=pt: e.copy(out=ot[:, :tb], in_=pt[:, :tb]),
                                reads=[pb], writes=[ob])
                    else:
                        self.op("dve", lambda e, ot=ot, pt=pt: e.tensor_copy(out=ot[:, :tb], in_=pt[:, :tb]),
                                reads=[pb], writes=[ob])
                    if grp == 64:
                        self.dma("sp", dst[2 * c, :, t0:t0 + tb], ot[0:64, :tb], reads=[ob], writes=[dstb[2 * c][bi]])
                        self.dma("sp", dst[2 * c + 1, :, t0:t0 + tb], ot[64:128, :tb], reads=[ob], writes=[dstb[2 * c + 1][bi]])
                    else:
                        self.dma("sp", dst[c, :, t0:t0 + tb], ot[:, :tb], reads=[ob], writes=[dstb[c][bi]])
        wt, wb = self.pool("wa128", [128, KC, 128], BF16, 3)
        self.dma("pool", wt[:], self.wav[l], writes=[wb])
        for ti in range(T // 128):
            bi = 0 if ti < 2 else 1 + (ti - 2) // 4
            pt, pb = self.psum()
            for kc in range(KC):
                self.op("pe", lambda e, kc=kc, pt=pt: e.matmul(
                    pt[:, :128], lhsT=hx[:, kc, ti * 128:(ti + 1) * 128], rhs=wt[:, kc, :],
                    start=(kc == 0), stop=(kc == KC - 1)),
                    reads=[wb, hxb[bi]], writes=[pb])
            ot, ob = self.pool("pav_o", [128, 128], BF16, 3)
            self.op("act", lambda e, ot=ot, pt=pt: e.copy(out=ot[:], in_=pt[:, :128]), reads=[pb], writes=[ob])
            self.dma("sp", self.VTM[ti * 128:(ti + 1) * 128, :], ot[:], reads=[ob], writes=[self.VTMb[ti]])
        return hx


def host_prep(inputs):
    f32 = np.float32
    x = np.asarray(inputs["x"], f32)
    ctx = np.asarray(inputs["ctx"], f32)
    c = np.asarray(inputs["c"], f32)
    c_ctx = np.asarray(inputs["c_ctx"], f32)
    B = x.shape[0]
    shared = {}
    ada_w = np.asarray(inputs["ada_w"], f32)
    shared["adaw"] = np.ascontiguousarray(
        ada_w.reshape(DEPTH, KC, 128, 96, 128).transpose(0, 3, 2, 1, 4))
    ada_b = np.asarray(inputs["ada_b"], f32)
    shared["adab"] = np.ascontiguousarray(ada_b.reshape(DEPTH, 96, 128).transpose(0, 2, 1))
    ng = np.stack([inputs["norm1_g"][0], inputs["norm1_g"][1], inputs["norm2_g"][0], inputs["norm2_g"][1],
                   inputs["final_g"]]).astype(f32)
    shared["ng"] = np.ascontiguousarray(ng.reshape(5, KC, 128).transpose(2, 0, 1))
    w_in = np.asarray(inputs["w_in"], f32)
    c64 = cols64()
    c128 = cols128()
    shared["wa64"] = np.stack([np.stack([wlayout(w_in[l][:, cc]) for cc in c64]) for l in range(DEPTH)])
    shared["wa128"] = np.stack([np.stack([wlayout(w_in[l][:, cc]) for cc in c128]) for l in range(DEPTH)])
    shared["wav"] = np.stack([wlayout(w_in[l][:, O_KV + 128:O_KV + 256]) for l in range(DEPTH)])
    per_core = []
    for b in range(B):
        xin = np.concatenate([ctx[b].T, x[b].T], axis=1)
        m = {"xin": np.ascontiguousarray(xin.reshape(KC, 128, T))}
        cc = np.stack([c[b], c_ctx], axis=-1)
        m["ccol"] = np.ascontiguousarray(cc.reshape(KC, 128, 2).transpose(1, 0, 2))
        per_core.append(m)
    return shared, per_core


TBR = 64
TRO = 128
DECAY_C = float(np.exp(-0.5))
RC_MU = 0
RC_W0 = 52
RC_A0 = 68
RC_KK = 84
RC_KA = 92
RC_RK = 100
RC_LG = 116
RC_LB = 124
RC_N = 132


def rwkv_consts():
    c = np.zeros((128, 1024), np.float32)
    s = np.arange(64)[:, None]
    t = np.arange(64)[None, :]
    for d in range(2):
        before = ((s < t) if d == 0 else (s > t)).astype(np.float32)
        beq = ((s <= t) if d == 0 else (s >= t)).astype(np.float32)
        o = d * 256
        c[0:64, o + 0:o + 64] = -before
        c[64:128, o + 0:o + 64] = before
        c[0:64, o + 64:o + 128] = beq
        c[64:128, o + 64:o + 128] = beq
        c[0:64, o + 128:o + 192] = -(before.T)
    c[:, 512:640] = np.eye(128, dtype=np.float32)
    m = np.ones(128, np.float32)
    m[0] = 0
    m[64] = 0
    c[:, 640:768] = m[None, :]
    return c


class MKR(MK):
    def declare_rwkv(self):
        self.rwc_d = self.dram("rwc", [DEPTH, 64, RC_N], F32, kind="ExternalInput")
        self.wup_d = self.dram("wup", [DEPTH, 2, 64, 512], F32, kind="ExternalInput")
        self.aup_d = self.dram("aup", [DEPTH, 2, 64, 512], F32, kind="ExternalInput")
        self.gup_d = self.dram("gup", [DEPTH, 128, 512], F32, kind="ExternalInput")
        self.rconst_d = self.dram("rconst", [128, 1024], F32, kind="ExternalInput")
        self.YD = self.dram("YD", [2, 8, 64, T], F32)
        self.BOND = self.dram("BOND", [2, 8, 64, T], F32)
        self.FEATS = self.dram("FEATS", [16, 128, T], BF16)
        self.FEATSb = [[Buf() for _ in range(T // 128)] for _ in range(4)]

    def rwkv_setup(self):
        self.rconst = self.sb("rconst_t", [128, 1024], F32)
        self.rconstb = Buf("rconst")
        self.dma("sp", self.rconst[:], self.rconst_d[:], writes=[self.rconstb])
        self.ident = self.rconst[:, 512:640]

    def rwkv(self, l, extra=()):
        k = self
        nblk = T // TBR
        rwc = k.sb(f"rwc{l}", [64, RC_N + 8], F32)
        rwcb = Buf("rwc")
        k.dma("sp", rwc[:, :RC_N], k.rwc_d[l], writes=[rwcb])
        k.op("dve", lambda e: e.tensor_scalar(out=rwc[:, RC_N:RC_N + 8], in0=rwc[:, RC_KA:RC_KA + 8],
                                              scalar1=-1.0, scalar2=1.0, op0=ALU.mult, op1=ALU.add),
             reads=[rwcb], writes=[rwcb])
        wup = k.sb(f"wup{l}", [64, 2, 512], F32)
        aup = k.sb(f"aup{l}", [64, 2, 512], F32)
        gup = k.sb(f"gup{l}", [128, 512], F32)
        wb_ = Buf("wupaup")
        k.dma("sp", wup[:], k.wup_d[l].rearrange("d j c -> j d c"), writes=[wb_])
        k.dma("sp", aup[:], k.aup_d[l].rearrange("d j c -> j d c"), writes=[wb_])
        k.dma("sp", gup[:], k.gup_d[l], writes=[wb_])
        YDb = [[[Buf() for _ in range(nblk)] for _ in range(2)] for _ in range(2)]
        k._uvub = [Buf(), Buf()]
        CB = [rwcb, k.rconstb]

        def bc(ap2, n):
            return ap2.unsqueeze(2).to_broadcast([64, 8, n])

        import os as _os
        FR = mybir.dt.float32r if _os.environ.get('RW_F32R', '1') == '1' else F32
        identr = k.sb("identr", [64, 64], FR)
        identrb = Buf()
        k.op("dve", lambda e: e.tensor_copy(out=identr[:], in_=k.ident[0:64, 0:64]), reads=[k.rconstb], writes=[identrb])
        def run_dir(d, h0, NH):
            kp = lambda name, shape, dt, n: k.pool(f"{name}_d{d}h{h0}", shape, dt, n)
            bcl = lambda col, n: rwc[:, col + h0:col + h0 + NH].unsqueeze(2).to_broadcast([64, NH, n])
            UVub_s = Buf()
            SV = [k.sb(f"SV{l}{d}{h0}{i}", [128, NH, 64], FR) for i in range(2)]
            SVs = [Buf(), Buf()]
            SVv = [Buf(), Buf()]
            k.op("dve", lambda e: e.tensor_scalar(out=SV[0][0:64].rearrange("p h x -> p (h x)"), in0=k.rconst[0:64, 0:NH * 64], scalar1=0.0, scalar2=None, op0=ALU.mult), reads=[k.rconstb], writes=[SVs[0]])
            cidx = 0
            order = list(range(nblk))
            if d == 1:
                nc_ = CTX // TBR
                order = list(range(nc_ - 1, -1, -1)) + list(range(nblk - 1, nc_ - 1, -1))
            nb_done = 0
            for bi in order:
                t0 = bi * TBR
                is_ctx = t0 < CTX
                seq_lo, seq_hi = (0, CTX) if is_ctx else (CTX, T)
                mixed = []
                for g in range(4):
                    ng = NH if g < 3 else 2
                    c0 = g * 8 + h0 if g < 3 else 24 + 2 * d
                    raw, rawb = kp(f"raw{min(g,3)}", [64, ng, TBR + 1], F32, 1)
                    if d == 0:
                        lo = t0 - 1
                        if lo < seq_lo:
                            k.op("pool", lambda e, raw=raw: e.memset(raw[:, :, 0:1], 0.0), writes=[rawb])
                            k.dma("sp", raw[:, :, 1:TBR + 1],
                                  k.P64[c0:c0 + ng, :, t0:t0 + TBR].rearrange("c p t -> p c t"),
                                  reads=[b_ for c_ in range(c0, c0 + ng) for b_ in k.P64b[c_]], writes=[rawb])
                        else:
                            k.dma("sp", raw[:, :, 0:TBR + 1],
                                  k.P64[c0:c0 + ng, :, lo:t0 + TBR].rearrange("c p t -> p c t"),
                                  reads=[b_ for c_ in range(c0, c0 + ng) for b_ in k.P64b[c_]], writes=[rawb])
                        cur = raw[:, :, 1:TBR + 1]
                        prev = raw[:, :, 0:TBR]
                    else:
                        hi = t0 + TBR + 1
                        if hi > seq_hi:
                            k.op("pool", lambda e, raw=raw: e.memset(raw[:, :, TBR:TBR + 1], 0.0), writes=[rawb])
                            k.dma("sp", raw[:, :, 0:TBR],
                                  k.P64[c0:c0 + ng, :, t0:t0 + TBR].rearrange("c p t -> p c t"),
                                  reads=[b_ for c_ in range(c0, c0 + ng) for b_ in k.P64b[c_]], writes=[rawb])
                        else:
                            k.dma("sp", raw[:, :, 0:TBR + 1],
                                  k.P64[c0:c0 + ng, :, t0:hi].rearrange("c p t -> p c t"),
                                  reads=[b_ for c_ in range(c0, c0 + ng) for b_ in k.P64b[c_]], writes=[rawb])
                        cur = raw[:, :, 0:TBR]
                        prev = raw[:, :, 1:TBR + 1]
                    if g == 2:
                        mx, mxb = kp("vpad", [64, NH, 64 + TBR], F32, 1)
                        mxv = mx[:, :, 64:64 + TBR]
                    else:
                        mx, mxb = kp(f"mix{g}", [64, ng, TBR], F32, 1)
                        mxv = mx[:, :, :]
                    mcol = RC_MU + d * 26 + (g * 8 + h0 if g < 3 else 24)
                    mub = rwc[:, mcol:mcol + ng].unsqueeze(2).to_broadcast([64, ng, TBR])
                    eng = "pool" if g % 2 == 0 else "dve"
                    k.op(eng, lambda e, mxv=mxv, prev=prev, cur=cur: e.tensor_tensor(
                        out=mxv, in0=prev, in1=cur, op=ALU.subtract), reads=[rawb], writes=[mxb])
                    k.op(eng, lambda e, mxv=mxv, mub=mub: e.tensor_tensor(
                        out=mxv, in0=mxv, in1=mub, op=ALU.mult), reads=[mxb, rwcb], writes=[mxb])
                    k.op(eng, lambda e, mxv=mxv, cur=cur: e.tensor_tensor(
                        out=mxv, in0=mxv, in1=cur, op=ALU.add), reads=[mxb, rawb], writes=[mxb])
                    mixed.append((mx, mxv, mxb))
                (rm, rmv, rmb), (km, kmv, kmb), (vp, vmv, vpb), (lm, lmv, lmb) = mixed
                if nb_done == 0:
                    k.op("pool", lambda e, vp=vp: e.memset(vp[:, :, 0:64], 0.0), writes=[vpb])
                yield
                tw, twb = kp("tw", [64, TBR], F32, 1)
                k.op("act", lambda e: e.activation(out=tw[:], in_=lm[:, 0, :], func=AF.Tanh),
                     reads=[lmb], writes=[twb])
                sg, sgb = kp("sg", [64, NH, TBR], F32, 1)
                at, ab = kp("a", [64, NH, TBR], F32, 1)
                for (dst, dstb, up, rhs, rhsb, bcol) in [(sg, sgb, wup, tw[:], twb, RC_W0),
                                                         (at, ab, aup, lm[:, 1, :], lmb, RC_A0)]:
                    for hh in range(NH // 4):
                        pt, pb = k.psum()
                        for h4 in range(4):
                            h = hh * 4 + h4
                            k.op("pe", lambda e, pt=pt, h=h, h4=h4, up=up, rhs=rhs: e.matmul(
                                pt[:64, h4 * TBR:(h4 + 1) * TBR], lhsT=up[:, d, (h0 + h) * 64:(h0 + h + 1) * 64], rhs=rhs,
                                start=True, stop=True), reads=[wb_, rhsb], writes=[pb])
                        for h4 in range(4):
                            h = hh * 4 + h4
                            k.op("act", lambda e, pt=pt, h=h, h4=h4, dst=dst, bcol=bcol: e.activation(
                                out=dst[:, h, :], in_=pt[:64, h4 * TBR:(h4 + 1) * TBR], func=AF.Sigmoid,
                                bias=rwc[:, bcol + d * 8 + h0 + h:bcol + d * 8 + h0 + h + 1], scale=1.0),
                                reads=[pb, rwcb], writes=[dstb])
                yield
                cs, csb = kp("cs", [64, NH, TBR], F32, 1)
                for h in range(NH):
                    k.op("dve", lambda e, h=h: e.tensor_tensor_scan(
                        out=cs[:, h, :], data0=k.rconst[0:64, 640:640 + TBR], data1=sg[:, h, :], initial=0.0,
                        op0=ALU.mult, op1=ALU.add), reads=[sgb, k.rconstb], writes=[csb])
                cs4 = cs[:].rearrange("p h (c t) -> p (h c) t", t=64)
                sg4 = sg[:].rearrange("p h (c t) -> p (h c) t", t=64)
                if d == 1:
                    tot, totb = kp("tot", [64, NH * (TBR // 64), 1], F32, 1)
                    k.op("pool", lambda e: e.tensor_copy(out=tot[:], in_=cs4[:, :, 63:64]), reads=[csb], writes=[totb])
                    k.op("dve", lambda e: e.tensor_tensor(out=cs4, in0=sg4, in1=cs4, op=ALU.subtract),
                         reads=[sgb, csb], writes=[csb])
                    k.op("dve", lambda e: e.tensor_tensor(out=cs4, in0=cs4, in1=tot[:].to_broadcast([64, NH * (TBR // 64), 64]),
                                                          op=ALU.add), reads=[csb, totb], writes=[csb])
                Pt, Pb = kp("P", [64, NH, TBR], F32, 1)
                Pi, Pib = kp("Pi", [64, NH, TBR], F32, 1)
                Pp, Ppb = kp("Pp", [64, NH, TBR], F32, 1)
                k.op("act", lambda e: e.activation(out=Pt[:], in_=cs[:], func=AF.Exp, scale=-DECAY_C),
                     reads=[csb], writes=[Pb])
                k.op("act", lambda e: e.activation(out=Pi[:], in_=cs[:], func=AF.Exp, scale=DECAY_C),
                     reads=[csb], writes=[Pib])
                k.op("pool", lambda e: e.tensor_tensor(out=Pp[:], in0=cs[:], in1=sg[:], op=ALU.subtract),
                     reads=[csb, sgb], writes=[Ppb])
                k.op("act", lambda e: e.activation(out=Pp[:], in_=Pp[:], func=AF.Exp, scale=-DECAY_C),
                     reads=[Ppb], writes=[Ppb])
                yield
                kap, kapb = kp("kap", [64, NH, TBR], F32, 1)
                k.op("dve", lambda e: e.tensor_tensor(out=kap[:], in0=kmv, in1=bcl(RC_KK, TBR),
                                                      op=ALU.mult), reads=[kmb, rwcb], writes=[kapb])
                sq, sqb = kp("rsq", [64, NH, TBR], F32, 1)
                k.op("act", lambda e: e.activation(out=sq[:], in_=kap[:], func=AF.Square), reads=[kapb], writes=[sqb])
                rin, rinb = kp("rin", [64, NH, TBR], F32, 1)
                for hh in range(NH // 4):
                    pt, pb = k.psum()
                    k.op("pe", lambda e, pt=pt, hh=hh: e.matmul(
                        pt[:64, :4 * TBR], lhsT=k.ones_f[0:64, 0:64],
                        rhs=sq[:, hh * 4:(hh + 1) * 4, :].rearrange("p h t -> p (h t)"),
                        start=True, stop=True), reads=[sqb, k.ones_fb], writes=[pb])
                    k.op("act", lambda e, pt=pt, hh=hh: e.activation(
                        out=rin[:, hh * 4:(hh + 1) * 4, :].rearrange("p h t -> p (h t)"), in_=pt[:64, :4 * TBR],
                        func=AF.Sqrt), reads=[pb], writes=[rinb])
                k.op("dve", lambda e: e.tensor_scalar(out=rin[:], in0=rin[:], scalar1=1e-12, scalar2=None,
                                                      op0=ALU.max), reads=[rinb], writes=[rinb])
                k.op("dve", lambda e: e.reciprocal(out=rin[:], in_=rin[:]), reads=[rinb], writes=[rinb])
                k.op("dve", lambda e: e.tensor_tensor(out=kap[:], in0=kap[:], in1=rin[:], op=ALU.mult),
                     reads=[kapb, rinb], writes=[kapb])
                yield
                kr, krb = kp("krep", [64, NH, TBR], F32, 1)
                k.op("pool", lambda e: e.tensor_tensor(out=kr[:], in0=at[:], in1=bcl(RC_KA, TBR),
                                                       op=ALU.mult), reads=[ab, rwcb], writes=[krb])
                k.op("pool", lambda e: e.tensor_tensor(out=kr[:], in0=kr[:], in1=bcl(RC_N, TBR),
                                                       op=ALU.add), reads=[krb, rwcb], writes=[krb])
                k.op("pool", lambda e: e.tensor_tensor(out=kr[:], in0=kr[:], in1=kmv, op=ALU.mult),
                     reads=[krb, kmb], writes=[krb])
                bt_, btb = kp("bb", [64, NH, TBR], F32, 1)
                k.op("pool", lambda e: e.tensor_tensor(out=bt_[:], in0=kap[:], in1=at[:], op=ALU.mult),
                     reads=[kapb, ab], writes=[btb])
                yield
                bon, bonb = kp("bon", [64, NH, TBR], F32, 1)
                k.op("dve", lambda e: e.tensor_tensor(out=bon[:], in0=rmv, in1=kr[:], op=ALU.mult),
                     reads=[rmb, krb], writes=[bonb])
                k.op("dve", lambda e: e.tensor_tensor(out=bon[:], in0=bon[:],
                                                      in1=bcl(RC_RK + d * 8, TBR), op=ALU.mult),
                     reads=[bonb, rwcb], writes=[bonb])
                for hh in range(NH // 4):
                    pt, pb = k.psum()
                    k.op("pe", lambda e, pt=pt, hh=hh: e.matmul(
                        pt[:64, :4 * TBR], lhsT=k.ones_f[0:64, 0:64],
                        rhs=bon[:, hh * 4:(hh + 1) * 4, :].rearrange("p h t -> p (h t)"),
                        start=True, stop=True), reads=[bonb, k.ones_fb], writes=[pb])
                    k.op("dve", lambda e, pt=pt, hh=hh: e.tensor_tensor(
                        out=bon[:, hh * 4:(hh + 1) * 4, :],
                        in0=pt[:64, :4 * TBR].rearrange("p (h t) -> p h t", t=TBR),
                        in1=vmv[:, hh * 4:(hh + 1) * 4, :], op=ALU.mult), reads=[pb, vpb, bonb], writes=[bonb])
                yield
                QR, QRb = kp("QR", [64, NH, TBR // 64, 128], FR, 1)
                BK, BKb = kp("BK", [64, NH, TBR // 64, 128], FR, 1)

                def v4(ap):
                    return ap.rearrange("p h (c t) -> p h c t", t=64)
                k.op("dve", lambda e: e.tensor_tensor(out=QR[:, :, :, 0:64], in0=v4(kap[:]), in1=v4(Pp[:]), op=ALU.mult),
                     reads=[kapb, Ppb], writes=[QRb])
                k.op("dve", lambda e: e.tensor_tensor(out=QR[:, :, :, 64:128], in0=v4(rmv), in1=v4(Pt[:]), op=ALU.mult),
                     reads=[rmb, Pb], writes=[QRb])
                k.op("dve", lambda e: e.tensor_tensor(out=BK[:, :, :, 0:64], in0=v4(bt_[:]), in1=v4(Pi[:]), op=ALU.mult),
                     reads=[btb, Pib], writes=[BKb])
                k.op("dve", lambda e: e.tensor_tensor(out=BK[:, :, :, 64:128], in0=v4(kr[:]), in1=v4(Pi[:]), op=ALU.mult),
                     reads=[krb, Pib], writes=[BKb])
                yield
                Yt, Yb = kp("Yt", [64, NH, TBR], F32, 1)
                mo = d * 256
                for c in (list(range(TBR // 64)) if d == 0 else list(range(TBR // 64 - 1, -1, -1))):
                    cur_sv = cidx % 2
                    nxt_sv = (cidx + 1) % 2
                    QB, QBb = kp("QB", [128, NH, 64], FR, 1)
                    AMR, AMRb = kp("AMR", [128, NH, 64], FR, 1)
                    XSa, XSab = kp("XS", [64, NH, 128], FR, 2)
                    XTa, XTab = kp("XT", [64, NH, 64], FR, 2)
                    Tm, Tmb = kp("Tm", [64, NH, 64], FR, 1)
                    k.op("act", lambda e, QB=QB, c=c: e.copy(out=QB[0:64, :, :], in_=QR[:, :, c, 0:64]),
                         reads=[QRb], writes=[QBb])
                    for hh in range(NH // 4):
                        pt, pb = k.psum()
                        for h4 in range(4):
                            h = hh * 4 + h4
                            k.op("pe", lambda e, pt=pt, h=h, h4=h4, c=c: e.matmul(
                                pt[:, h4 * 128:(h4 + 1) * 128], lhsT=BK[:, h, c, :], rhs=QR[:, h, c, :],
                                start=True, stop=True), reads=[BKb, QRb], writes=[pb])
                        p4 = pt[:, :].rearrange("p (h x) -> p h x", x=128)
                        hs = slice(hh * 4, hh * 4 + 4)
                        k.op("dve", lambda e, p4=p4, hs=hs, XSa=XSa: e.tensor_tensor(
                            out=XSa[:, hs, 0:64], in0=p4[0:64, :, 0:64],
                            in1=k.rconst[0:64, mo:mo + 64].unsqueeze(1).to_broadcast([64, 4, 64]), op=ALU.mult),
                            reads=[pb, k.rconstb], writes=[XSab])
                        k.op("dve", lambda e, p4=p4, hs=hs, QB=QB: e.tensor_tensor(
                            out=QB[64:128, hs, :], in0=p4[64:128, :, 0:64],
                            in1=k.rconst[64:128, mo:mo + 64].unsqueeze(1).to_broadcast([64, 4, 64]), op=ALU.mult),
                            reads=[pb, k.rconstb], writes=[QBb])
                        k.op("dve", lambda e, p4=p4, hs=hs, AMR=AMR: e.tensor_tensor(
                            out=AMR[:, hs, :], in0=p4[:, :, 64:128],
                            in1=k.rconst[:, mo + 64:mo + 128].unsqueeze(1).to_broadcast([128, 4, 64]), op=ALU.mult),
                            reads=[pb, k.rconstb], writes=[AMRb])
                    yield
                    pt, pb = k.psum()
                    for h in range(NH):
                        k.op("pe", lambda e, pt=pt, h=h, c=c: e.matmul(
                            pt[:64, h * 64:(h + 1) * 64], lhsT=QR[:, h, c, 0:64], rhs=BK[:, h, c, 0:64],
                            start=True, stop=True), reads=[BKb, QRb], writes=[pb])
                    k.op("dve", lambda e, pt=pt, XTa=XTa: e.tensor_tensor(
                        out=XTa[:], in0=pt[:64, :NH * 64].rearrange("p (h x) -> p h x", x=64),
                        in1=k.rconst[0:64, mo + 128:mo + 192].unsqueeze(1).to_broadcast([64, NH, 64]), op=ALU.mult),
                        reads=[pb, k.rconstb], writes=[XTab])
                    k.op("dve", lambda e, XSa=XSa: e.tensor_copy(
                        out=XSa[:, :, 64:128], in_=k.ident[0:64, 0:64].unsqueeze(1).to_broadcast([64, NH, 64])),
                        reads=[k.rconstb], writes=[XSab])
                    yield
                    for j in range(6):
                        if j < 5:
                            XSn, XSnb = kp("XS", [64, NH, 128], FR, 2)
                            for hh in range(NH // 4):
                                pt, pb = k.psum()
                                for h4 in range(4):
                                    h = hh * 4 + h4
                                    k.op("pe", lambda e, pt=pt, h=h, h4=h4, XSa=XSa, XTa=XTa: e.matmul(
                                        pt[:64, h4 * 128:(h4 + 1) * 128], lhsT=XTa[:, h, :], rhs=XSa[:, h, :],
                                        start=True, stop=True), reads=[XSab, XTab], writes=[pb])
                                p4 = pt[:64, :].rearrange("p (h x) -> p h x", x=128)
                                hs = slice(hh * 4, hh * 4 + 4)
                                k.op("act", lambda e, p4=p4, hs=hs, XSn=XSn: e.copy(
                                    out=XSn[:, hs, 0:64], in_=p4[:, :, 0:64]), reads=[pb], writes=[XSnb])
                                k.op("dve", lambda e, p4=p4, hs=hs, XSn=XSn, XSa=XSa: e.tensor_tensor(
                                    out=XSn[:, hs, 64:128], in0=p4[:, :, 64:128], in1=XSa[:, hs, 64:128], op=ALU.add),
                                    reads=[pb, XSab], writes=[XSnb])
                            XTn, XTnb = kp("XT", [64, NH, 64], FR, 2)
                            pt2, pb2 = k.psum()
                            for h in range(NH):
                                k.op("pe", lambda e, pt2=pt2, h=h, XSa=XSa, XTa=XTa: e.matmul(
                                    pt2[:64, h * 64:(h + 1) * 64], lhsT=XSa[:, h, 0:64], rhs=XTa[:, h, :],
                                    start=True, stop=True), reads=[XSab, XTab], writes=[pb2])
                            k.op("act", lambda e, pt2=pt2, XTn=XTn: e.copy(
                                out=XTn[:].rearrange("p h x -> p (h x)"), in_=pt2[:64, :NH * 64]), reads=[pb2], writes=[XTnb])
                            XSa, XSab = XSn, XSnb
                            XTa, XTab = XTn, XTnb
                        else:
                            pt3, pb3 = k.psum()
                            for h in range(NH):
                                k.op("pe", lambda e, pt3=pt3, h=h, XSa=XSa, XTa=XTa: e.matmul(
                                    pt3[:64, h * 64:(h + 1) * 64], lhsT=XTa[:, h, :], rhs=XSa[:, h, 64:128],
                                    start=True, stop=True), reads=[XSab, XTab], writes=[pb3])
                            k.op("dve", lambda e, pt3=pt3, Tm=Tm, XSa=XSa: e.tensor_tensor(
                                out=Tm[:], in0=pt3[:64, :NH * 64].rearrange("p (h x) -> p h x", x=64),
                                in1=XSa[:, :, 64:128], op=ALU.add), reads=[pb3, XSab], writes=[Tmb])
                        yield
                    BKT, BKTb = kp("BKT", [128, NH, 64], FR, 1)
                    pt, pb = k.psum()
                    for h in range(NH):
                        k.op("pe", lambda e, pt=pt, h=h, c=c: e.transpose(
                            pt[:, h * 64:(h + 1) * 64], BK[:, h, c, :].bitcast(F32), k.ident[0:64, 0:64]),
                            reads=[BKb, k.rconstb], writes=[pb])
                    k.op("act", lambda e, pt=pt, BKT=BKT: e.copy(
                        out=BKT[:].rearrange("p h x -> p (h x)"), in_=pt[:, :NH * 64]), reads=[pb], writes=[BKTb])
                    yield
                    UV, UVvb = kp("UV", [128, NH, 64], FR, 1)
                    UVub = UVub_s
                    pt, pb = k.psum()
                    for h in range(NH):
                        k.op("pe", lambda e, pt=pt, h=h, c=c: e.transpose(
                            pt[:, h * 64:(h + 1) * 64], vp[:, h, c * 64:c * 64 + 128], k.ident[0:64, 0:64]),
                            reads=[vpb, k.rconstb], writes=[pb])
                    k.op("dve", lambda e, pt=pt, UV=UV: e.tensor_copy(
                        out=UV[64:128].rearrange("p h x -> p (h x)"), in_=pt[64:128, :NH * 64]), reads=[pb], writes=[UVvb])
                    k.op("dve", lambda e, pt=pt, cur_sv=cur_sv: e.tensor_copy(
                        out=SV[cur_sv][64:128].rearrange("p h x -> p (h x)"), in_=pt[64:128, :NH * 64]),
                        reads=[pb], writes=[SVv[cur_sv]])
                    yield
                    WT, WTb = kp("WT", [64, NH, 64], FR, 1)
                    pt, pb = k.psum()
                    for h in range(NH):
                        k.op("pe", lambda e, pt=pt, h=h, QB=QB, cur_sv=cur_sv: e.matmul(
                            pt[:64, h * 64:(h + 1) * 64], lhsT=QB[:, h, :], rhs=SV[cur_sv][:, h, :],
                            start=True, stop=True), reads=[QBb, SVs[cur_sv], SVv[cur_sv]], writes=[pb])
                    k.op("act", lambda e, pt=pt, WT=WT: e.copy(
                        out=WT[:].rearrange("p h x -> p (h x)"), in_=pt[:64, :NH * 64]), reads=[pb], writes=[WTb])
                    yield
                    pt, pb = k.psum()
                    for h in range(NH):
                        k.op("pe", lambda e, pt=pt, h=h, Tm=Tm, WT=WT: e.matmul(
                            pt[:64, h * 64:(h + 1) * 64], lhsT=Tm[:, h, :], rhs=WT[:, h, :],
                            start=True, stop=True), reads=[Tmb, WTb], writes=[pb])
                    k.op("act", lambda e, pt=pt, UV=UV: e.mul(
                        out=UV[0:64].rearrange("p h x -> p (h x)"), in_=pt[:64, :NH * 64], mul=-1.0), reads=[pb], writes=[UVub])
                    yield
                    pt, pb = k.psum()
                    for h in range(NH):
                        k.op("pe", lambda e, pt=pt, h=h, c=c, cur_sv=cur_sv: e.matmul(
                            pt[:64, h * 64:(h + 1) * 64], lhsT=SV[cur_sv][0:64, h, :], rhs=QR[:, h, c, 64:128],
                            start=True, stop=False), reads=[SVs[cur_sv], QRb], writes=[pb])
                        k.op("pe", lambda e, pt=pt, h=h, UV=UV, AMR=AMR: e.matmul(
                            pt[:64, h * 64:(h + 1) * 64], lhsT=UV[:, h, :], rhs=AMR[:, h, :],
                            start=False, stop=True), reads=[UVub, UVvb, AMRb], writes=[pb])
                    k.op("act", lambda e, pt=pt, c=c: e.copy(
                        out=Yt[:, :, c * 64:(c + 1) * 64], in_=pt[:64, :NH * 64].rearrange("p (h x) -> p h x", x=64)),
                        reads=[pb], writes=[Yb])
                    yield
                    pt, pb = k.psum()
                    for h in range(NH):
                        k.op("pe", lambda e, pt=pt, h=h, cur_sv=cur_sv: e.matmul(
                            pt[:64, h * 64:(h + 1) * 64], lhsT=identr[:], rhs=SV[cur_sv][0:64, h, :],
                            start=True, stop=False), reads=[SVs[cur_sv], identrb], writes=[pb])
                        k.op("pe", lambda e, pt=pt, h=h, BKT=BKT, UV=UV: e.matmul(
                            pt[:64, h * 64:(h + 1) * 64], lhsT=BKT[:, h, :], rhs=UV[:, h, :],
                            start=False, stop=True), reads=[BKTb, UVub, UVvb], writes=[pb])
                    pcol = (c * 64 + 63) if d == 0 else (c * 64)
                    k.op("dve", lambda e, pt=pt, nxt_sv=nxt_sv, pcol=pcol: e.tensor_tensor(
                        out=SV[nxt_sv][0:64], in0=pt[:64, :NH * 64].rearrange("p (h x) -> p h x", x=64),
                        in1=Pt[:, :, pcol:pcol + 1].to_broadcast([64, NH, 64]), op=ALU.mult),
                        reads=[pb, Pb], writes=[SVs[nxt_sv]])
                    cidx += 1
                k.dma("sp", k.YD[d, h0:h0 + NH, :, t0:t0 + TBR].rearrange("h p t -> p h t"), Yt[:], reads=[Yb], writes=[YDb[d][h0 // NH][bi]])
                k.dma("sp", k.BOND[d, h0:h0 + NH, :, t0:t0 + TBR].rearrange("h p t -> p h t"), bon[:], reads=[bonb], writes=[YDb[d][h0 // NH][bi]])
                nb_done += 1
                yield

        with k.scope():
            NHG = int(_os.environ.get('RW_NH', '8'))
            gens = [run_dir(d_, h0_, NHG) for h0_ in range(0, 8, NHG) for d_ in range(2)] + list(extra)
            while gens:
                for g_ in list(gens):
                    try:
                        next(g_)
                    except StopIteration:
                        gens.remove(g_)
        k._ydb = YDb

    def rwkv_readout_gen(self, l):
        k = self
        YDb = k._ydb
        rwc = k.sb(f"rwc_ro{l}", [64, RC_N + 8], F32)
        rwcb = Buf("rwc_ro")
        k.dma("sp", rwc[:, :RC_N], k.rwc_d[l], writes=[rwcb])
        gup = k.sb(f"gup_ro{l}", [128, 512], F32)
        wb_ = Buf("gup_ro")
        k.dma("sp", gup[:], k.gup_d[l], writes=[wb_])
        for ro in range(T // TRO):
            yield
            k.rwkv_readout(l, ro, ro * TRO, [b_ for d_ in range(2) for g_ in range(2) for b_ in YDb[d_][g_][ro * (TRO // TBR):(ro + 1) * (TRO // TBR)]],
                           rwc, rwcb, gup, wb_)

    def rwkv_readout(self, l, bi, t0, ydbufs, rwc, rwcb, gup, gupb):
        k = self
        yf, yfbuf = k.pool("yf", [64, 8, TRO], F32, 1)
        bf_, bfbuf = k.pool("bonf", [64, 8, TRO], F32, 1)
        yb_, ybbuf = k.pool("yb", [64, 8, TRO], F32, 1)
        bb_, bbbuf = k.pool("bonb", [64, 8, TRO], F32, 1)
        for (dst, dstb, src, dd) in [(yf, yfbuf, k.YD, 0), (yb_, ybbuf, k.YD, 1), (bf_, bfbuf, k.BOND, 0), (bb_, bbbuf, k.BOND, 1)]:
            k.dma("sp", dst[:], src[dd, :, :, t0:t0 + TRO].rearrange("h p t -> p h t"), reads=ydbufs, writes=[dstb])
        k.op("pool", lambda e: e.tensor_tensor(out=yf[:], in0=yf[:], in1=yb_[:], op=ALU.add),
             reads=[yfbuf, ybbuf], writes=[yfbuf])
        k.op("pool", lambda e: e.tensor_tensor(out=bf_[:], in0=bf_[:], in1=bb_[:], op=ALU.add),
             reads=[bfbuf, bbbuf], writes=[bfbuf])
        if "rwkv_y" in k.debug and l == 0:
            if "o_y" not in k.dbg_out:
                k.dbg_out["o_y"] = k.dram("o_y", [8, 64, T], F32, kind="ExternalOutput")
            ob = Buf()
            k.dma("sp", k.dbg_out["o_y"][:, :, t0:t0 + TRO].rearrange("h p t -> p h t"), yf[:], reads=[yfbuf], writes=[ob])
            k.out_tokens.append(ob)
        mean, meanb = k.pool("gn_m", [64, 8, TRO], F32, 1)
        for hh in range(2):
            pt, pb = k.psum()
            k.op("pe", lambda e, pt=pt, hh=hh: e.matmul(
                pt[:64, :], lhsT=k.ones_f[0:64, 0:64], rhs=yf[:, hh * 4:(hh + 1) * 4, :].rearrange("p h t -> p (h t)"),
                start=True, stop=True), reads=[yfbuf, k.ones_fb], writes=[pb])
            k.op("dve", lambda e, pt=pt, hh=hh: e.scalar_tensor_tensor(
                out=mean[:, hh * 4:(hh + 1) * 4, :].rearrange("p h t -> p (h t)"), in0=pt[:64, :], scalar=-1.0 / 64,
                in1=yf[:, hh * 4:(hh + 1) * 4, :].rearrange("p h t -> p (h t)"), op0=ALU.mult, op1=ALU.add),
                reads=[pb, yfbuf], writes=[meanb])
        sq, sqb = k.pool("gn_sq", [64, 8, TRO], F32, 1)
        k.op("act", lambda e: e.activation(out=sq[:], in_=mean[:], func=AF.Square), reads=[meanb], writes=[sqb])
        rstd, rstdb = k.pool("gn_r", [64, 8, TRO], F32, 1)
        for hh in range(2):
            pt, pb = k.psum()
            k.op("pe", lambda e, pt=pt, hh=hh: e.matmul(
                pt[:64, :], lhsT=k.ones_f[0:64, 0:64], rhs=sq[:, hh * 4:(hh + 1) * 4, :].rearrange("p h t -> p (h t)"),
                start=True, stop=True), reads=[sqb, k.ones_fb], writes=[pb])
            k.op("act", lambda e, pt=pt, hh=hh: e.activation(
                out=rstd[:, hh * 4:(hh + 1) * 4, :].rearrange("p h t -> p (h t)"), in_=pt[:64, :], func=AF.Sqrt,
                scale=1.0 / 64, bias=k.epsc[0:64, 1:2]), reads=[pb, k.epsb], writes=[rstdb])
        k.op("dve", lambda e: e.reciprocal(out=rstd[:], in_=rstd[:]), reads=[rstdb], writes=[rstdb])
        k.op("dve", lambda e: e.tensor_tensor(out=mean[:], in0=mean[:], in1=rstd[:], op=ALU.mult),
             reads=[meanb, rstdb], writes=[meanb])
        k.op("pool", lambda e: e.tensor_tensor(
            out=mean[:], in0=mean[:], in1=rwc[:, RC_LG:RC_LG + 8].unsqueeze(2).to_broadcast([64, 8, TRO]), op=ALU.mult),
            reads=[meanb, rwcb], writes=[meanb])
        k.op("pool", lambda e: e.tensor_tensor(
            out=mean[:], in0=mean[:], in1=rwc[:, RC_LB:RC_LB + 8].unsqueeze(2).to_broadcast([64, 8, TRO]), op=ALU.add),
            reads=[meanb, rwcb], writes=[meanb])
        k.op("pool", lambda e: e.tensor_tensor(out=mean[:], in0=mean[:], in1=bf_[:], op=ALU.add),
             reads=[meanb, bfbuf], writes=[meanb])
        gs, gsb = k.pool("gsig", [128, TRO], F32, 1)
        k.dma("sp", gs[:], k.P128[0, :, t0:t0 + TRO], reads=k.P128b[0], writes=[gsb])
        k.op("act", lambda e: e.activation(out=gs[:], in_=gs[:], func=AF.Sigmoid), reads=[gsb], writes=[gsb])
        ot, otb = k.pool("rw_out", [64, 8, TRO], BF16, 1)
        for hh in range(2):
            pt, pb = k.psum()
            for h4 in range(4):
                h = hh * 4 + h4
                k.op("pe", lambda e, pt=pt, h=h, h4=h4: e.matmul(
                    pt[:64, h4 * TRO:(h4 + 1) * TRO], lhsT=gup[:, h * 64:(h + 1) * 64], rhs=gs[:],
                    start=True, stop=True), reads=[gupb, gsb], writes=[pb])
            k.op("dve", lambda e, pt=pt, hh=hh: e.tensor_tensor(
                out=ot[:, hh * 4:(hh + 1) * 4, :].rearrange("p h t -> p (h t)"), in0=pt[:64, :],
                in1=mean[:, hh * 4:(hh + 1) * 4, :].rearrange("p h t -> p (h t)"), op=ALU.mult),
                reads=[pb, meanb], writes=[otb])
        k.dma("sp", k.FEATS[4:8, :, t0:t0 + TRO].rearrange("c (two p) t -> p c two t", two=2),
              ot[:].rearrange("p (c two) t -> p c two t", two=2), reads=[otb], writes=[k.FEATSb[1][bi]])


def rwkv_host_prep(inputs):
    f32 = np.float32
    cols = np.zeros((DEPTH, 64, RC_N), f32)
    for l in range(DEPTH):
        for d in range(2):
            mu = np.asarray(inputs["rwkv_mu"], f32)[l, d]
            cols[l, :, RC_MU + d * 26:RC_MU + (d + 1) * 26] = mu.reshape(26, 64).T
            cols[l, :, RC_W0 + d * 8:RC_W0 + (d + 1) * 8] = np.asarray(inputs["rwkv_w0"], f32)[l, d].reshape(8, 64).T
            cols[l, :, RC_A0 + d * 8:RC_A0 + (d + 1) * 8] = np.asarray(inputs["rwkv_a0"], f32)[l, d].reshape(8, 64).T
            cols[l, :, RC_RK + d * 8:RC_RK + (d + 1) * 8] = np.asarray(inputs["rwkv_r_k"], f32)[l, d].T
        cols[l, :, RC_KK:RC_KK + 8] = np.asarray(inputs["rwkv_k_k"], f32)[l].reshape(8, 64).T
        cols[l, :, RC_KA:RC_KA + 8] = np.asarray(inputs["rwkv_k_a"], f32)[l].reshape(8, 64).T
        cols[l, :, RC_LG:RC_LG + 8] = np.asarray(inputs["rwkv_lnx_g"], f32)[l].reshape(8, 64).T
        cols[l, :, RC_LB:RC_LB + 8] = np.asarray(inputs["rwkv_lnx_b"], f32)[l].reshape(8, 64).T
    return {"rwc": cols, "wup": np.ascontiguousarray(np.asarray(inputs["rwkv_w_up"], f32)),
            "aup": np.ascontiguousarray(np.asarray(inputs["rwkv_a_up"], f32)),
            "gup": np.ascontiguousarray(np.asarray(inputs["rwkv_g_up"], f32)),
            "rconst": rwkv_consts()}


NEG = -1e30


def att_consts():
    t = np.arange(SEQ)
    row = (t // 64).astype(np.float32)
    col = (t % 64).astype(np.float32)
    inv = (10000.0 ** (-np.arange(16, dtype=np.float32) / 16)).astype(np.float32)
    tab = np.zeros((64, 2, SEQ), np.float32)
    for dd in range(64):
        half = dd // 32
        i = dd % 32
        pos = row if half == 0 else col
        fi = i % 16
        ang = (pos * inv[fi]).astype(np.float32)
        tab[dd, 0] = np.cos(ang)
        tab[dd, 1] = (-np.sin(ang)) if i < 16 else np.sin(ang)
    i = np.arange(128)[:, None]
    j = np.arange(384)[None, :]
    band = np.where((j >= i) & (j <= i + 256), 0.0, NEG).astype(np.float32)
    return tab, band


class MKA(MKR):
    def declare_att(self):
        self.rope_d = self.dram("rope", [64, 2, SEQ], F32, kind="ExternalInput")
        self.band_d = self.dram("band", [128, 384], F32, kind="ExternalInput")
        self.sink_d = self.dram("sinkb", [DEPTH, 128, 8], F32, kind="ExternalInput")

    def attention(self, l):
        k = self
        scale = 0.125
        Qb = k.sb("Qb", [64, 8, T], BF16)
        Kb = k.sb("Kb", [64, 2, T], BF16)
        Vr = k.sb("Vr", [128, T // 128, 128], BF16)
        Qbb = [Buf() for _ in TBS]
        Kbb = [Buf() for _ in TBS]
        Vrb = Buf()
        k.dma("sp", Vr[:], k.VTM[:, :].rearrange("(n p) c -> p n c", p=128), reads=k.VTMb, writes=[Vrb])
        band = k.sb("band_t", [128, 384], F32)
        bandb = Buf()
        k.dma("sp", band[:], k.band_d[:], writes=[bandb])
        sink = k.sb("sink_t", [128, 8], F32)
        sinkb = Buf()
        k.dma("sp", sink[:], k.sink_d[l], writes=[sinkb])
        identb = k.sb("identb", [128, 128], BF16)
        identbb = Buf()
        k.op("dve", lambda e: e.tensor_copy(out=identb[:], in_=k.ident), reads=[k.rconstb], writes=[identbb])
        for bi, (t0, tb) in enumerate(TBS):
            for (dst, dstb, c0, nh) in [(Qb, Qbb, 28, 8), (Kb, Kbb, 44, 2)]:
                raw, rawb = k.pool(f"araw{nh}", [64, nh, 512], F32, 1)
                k.dma("sp", raw[:, :, :tb], k.P64[c0:c0 + nh, :, t0:t0 + tb].rearrange("c p t -> p c t"),
                      reads=[b_ for c_ in range(c0, c0 + nh) for b_ in k.P64b[c_]], writes=[rawb])
                if bi == 0:
                    k.op("act", lambda e, raw=raw, dst=dst: e.copy(out=dst[:, :, t0:t0 + tb], in_=raw[:, :, :tb]),
                         reads=[rawb], writes=[dstb[bi]])
                    continue
                sw, swb = k.pool(f"asw{nh}", [64, nh, 512], F32, 1)
                k.dma("sp", sw[:, :, :tb], k.P64[c0 + nh:c0 + 2 * nh, :, t0:t0 + tb].rearrange("c p t -> p c t"),
                      reads=[b_ for c_ in range(c0 + nh, c0 + 2 * nh) for b_ in k.P64b[c_]], writes=[swb])
                tab, tabb = k.pool("ropetab", [64, 2, 512], F32, 2)
                k.dma("sp", tab[:, :, :tb], k.rope_d[:, :, t0 - CTX:t0 - CTX + tb], writes=[tabb])
                k.op("dve", lambda e, raw=raw, tab=tab, nh=nh: e.tensor_tensor(
                    out=raw[:, :, :tb], in0=raw[:, :, :tb], in1=tab[:, 0:1, :tb].to_broadcast([64, nh, tb]), op=ALU.mult),
                    reads=[rawb, tabb], writes=[rawb])
                k.op("pool", lambda e, sw=sw, tab=tab, nh=nh: e.tensor_tensor(
                    out=sw[:, :, :tb], in0=sw[:, :, :tb], in1=tab[:, 1:2, :tb].to_broadcast([64, nh, tb]), op=ALU.mult),
                    reads=[swb, tabb], writes=[swb])
                k.op("dve", lambda e, raw=raw, sw=sw, dst=dst: e.tensor_tensor(
                    out=dst[:, :, t0:t0 + tb], in0=raw[:, :, :tb], in1=sw[:, :, :tb], op=ALU.add),
                    reads=[rawb, swb], writes=[dstb[bi]])
        allq = Qbb + Kbb
        yield
        for qt in range(T // 128):
            yield
            q0 = qt * 128
            is_ctx = qt < 2
            if is_ctx:
                lat_tiles = []
            else:
                n = qt - 2
                lat_tiles = [tt for tt in (n - 1, n, n + 1) if 0 <= tt < 16]
            jlo = 0
            if not is_ctx:
                jlo = (lat_tiles[0] - (n - 1)) * 128
            nb = len(lat_tiles) * 128
            key_tiles = [2 + tt for tt in lat_tiles] + [0, 1]
            ob_t, ob_b = k.psum_hold()
            rinv, rinvb = k.pool("a_rinv", [128, 8], F32, 2)
            for h in range(8):
                g = h // 4
                s, sb_ = k.pool("a_s", [128, 640], F32, 2)
                if nb < 384:
                    k.op("pool", lambda e, s=s: e.memset(s[:, 0:384], NEG), writes=[sb_])
                if nb > 0:
                    pa, pab = k.psum()
                    k0 = CTX + lat_tiles[0] * 128
                    k.op("pe", lambda e, pa=pa, h=h, g=g, k0=k0: e.matmul(
                        pa[:, :nb], lhsT=Qb[:, h, q0:q0 + 128], rhs=Kb[:, g, k0:k0 + nb], start=True, stop=True),
                        reads=allq, writes=[pab])
                    k.op("dve", lambda e, pa=pa, s=s: e.tensor_tensor(
                        out=s[:, jlo:jlo + nb], in0=pa[:, :nb], in1=band[:, jlo:jlo + nb], op=ALU.add),
                        reads=[pab, bandb], writes=[sb_])
                pc_, pcb = k.psum()
                k.op("pe", lambda e, pc_=pc_, h=h, g=g: e.matmul(
                    pc_[:, :256], lhsT=Qb[:, h, q0:q0 + 128], rhs=Kb[:, g, 0:256], start=True, stop=True),
                    reads=allq, writes=[pcb])
                k.op("act", lambda e, pc_=pc_, s=s: e.copy(out=s[:, 384:640], in_=pc_[:, :256]),
                     reads=[pcb], writes=[sb_])
                st, stb = k.pool("a_stat", [128, 4], F32, 4)
                k.op("dve", lambda e, s=s, st=st: e.reduce_max(out=st[:, 0:1], in_=s[:, :], axis=AX.X),
                     reads=[sb_], writes=[stb])
                k.op("dve", lambda e, st=st: e.tensor_scalar(out=st[:, 1:2], in0=st[:, 0:1], scalar1=-scale, scalar2=None,
                                                           op0=ALU.mult), reads=[stb], writes=[stb])
                P, Pb_ = k.pool("a_P", [128, 640], BF16, 2)
                k.op("act", lambda e, s=s, st=st, P=P: e.activation(
                    out=P[:], in_=s[:], func=AF.Exp, bias=st[:, 1:2], scale=scale, accum_out=st[:, 2:3]),
                    reads=[sb_, stb], writes=[Pb_, stb])
                k.op("act", lambda e, st=st, h=h: e.activation(
                    out=st[:, 3:4], in_=st[:, 1:2], func=AF.Exp, bias=sink[:, h:h + 1], scale=1.0),
                    reads=[stb, sinkb], writes=[stb])
                k.op("dve", lambda e, st=st: e.tensor_tensor(out=st[:, 2:3], in0=st[:, 2:3], in1=st[:, 3:4], op=ALU.add),
                     reads=[stb], writes=[stb])
                k.op("dve", lambda e, st=st, h=h: e.reciprocal(out=rinv[:, h:h + 1], in_=st[:, 2:3]),
                     reads=[stb], writes=[rinvb])
                pt_, ptb = k.psum()
                ptv = pt_[:].bitcast(BF16)
                cols = [jlo // 128 + i for i in range(len(lat_tiles))] + [3, 4]
                for ci, cb in enumerate(cols):
                    k.op("pe", lambda e, ptv=ptv, P=P, ci=ci, cb=cb: e.transpose(
                        ptv[:, ci * 128:(ci + 1) * 128], P[:, cb * 128:(cb + 1) * 128], identb[:]),
                        reads=[Pb_, identbb], writes=[ptb])
                nk = len(cols)
                PT, PTb = k.pool("a_PT", [128, 5, 128], BF16, 2)
                k.op("act" if h % 2 == 0 else "dve",
                     (lambda e, ptv=ptv, PT=PT, nk=nk: e.copy(out=PT[:, :nk, :].rearrange("p a b -> p (a b)"), in_=ptv[:, :nk * 128]))
                     if h % 2 == 0 else
                     (lambda e, ptv=ptv, PT=PT, nk=nk: e.tensor_copy(out=PT[:, :nk, :].rearrange("p a b -> p (a b)"), in_=ptv[:, :nk * 128])),
                     reads=[ptb], writes=[PTb])
                for ci, kt in enumerate(key_tiles):
                    k.op("pe", lambda e, PT=PT, ci=ci, kt=kt, h=h, g=g: e.matmul(
                        ob_t[:, h * 64:(h + 1) * 64], lhsT=PT[:, ci, :], rhs=Vr[:, kt, g * 64:(g + 1) * 64],
                        start=(ci == 0), stop=(ci == nk - 1)), reads=[PTb, Vrb], writes=[ob_b])
            o_tm, o_tmb = k.pool("a_otm", [128, 8, 64], BF16, 2)
            k.op("dve", lambda e, o_tm=o_tm, rinv=rinv: e.tensor_tensor(
                out=o_tm[:], in0=ob_t[:, :].rearrange("p (h d) -> p h d", d=64),
                in1=rinv[:, :].unsqueeze(2).to_broadcast([128, 8, 64]), op=ALU.mult),
                reads=[ob_b, rinvb], writes=[o_tmb])
            pt_, ptb = k.psum()
            ptv = pt_[:].bitcast(BF16)
            for c4 in range(4):
                k.op("pe", lambda e, ptv=ptv, o_tm=o_tm, c4=c4: e.transpose(
                    ptv[:, c4 * 128:(c4 + 1) * 128],
                    o_tm[:, 2 * c4:2 * c4 + 2, :].rearrange("p h d -> p (h d)"), identb[:]),
                    reads=[o_tmb, identbb], writes=[ptb])
            o_fm, o_fmb = k.pool("a_ofm", [128, 4, 128], BF16, 2)
            k.op("act", lambda e, ptv=ptv, o_fm=o_fm: e.copy(out=o_fm[:].rearrange("p a b -> p (a b)"), in_=ptv[:, :512]),
                 reads=[ptb], writes=[o_fmb])
            k.dma("sp", k.FEATS[8:12, :, q0:q0 + 128].rearrange("c p t -> p c t"), o_fm[:],
                  reads=[o_fmb], writes=[k.FEATSb[2][qt]])


def att_host_prep(inputs):
    tab, band = att_consts()
    sink = np.asarray(inputs["att_sink"], np.float32)
    return {"rope": tab, "band": band,
            "sinkb": np.ascontiguousarray(np.broadcast_to(sink[:, None, :], (DEPTH, 128, 8)))}


def fno_consts():
    c = np.arange(128)
    ang = 2 * np.pi * np.outer(c, c) / 128.0
    c128 = np.concatenate([np.cos(ang), np.sin(ang)], axis=1) / np.sqrt(128.0)
    tabs = {}
    for L in (SEQ, CTX):
        l = np.arange(L, dtype=np.int64)
        m = np.outer(l, l) % L
        a = 2 * np.pi * m / L
        tabs[L] = (np.stack([np.cos(a), -np.sin(a)], axis=1) / np.sqrt(L))
    return (c128.astype(ml_dtypes.bfloat16), tabs[SEQ].astype(ml_dtypes.bfloat16), tabs[CTX].astype(ml_dtypes.bfloat16))


class MKC(MKA):
    def declare_cf(self):
        self.convp_d = self.dram("convp", [DEPTH, 128, 4, 34], F32, kind="ExternalInput")
        self.c128_d = self.dram("c128", [128, 256], BF16, kind="ExternalInput")
        self.fL_d = self.dram("fL", [SEQ, 2, SEQ], BF16, kind="ExternalInput")
        self.fC_d = self.dram("fC", [CTX, 2, CTX], BF16, kind="ExternalInput")

    def conv(self, l):
        k = self
        cp = k.sb("convp_t", [128, 4, 34], F32)
        cpb = Buf()
        k.dma("sp", cp[:], k.convp_d[l], writes=[cpb])
        for (s0, L) in [(0, CTX), (CTX, SEQ)]:
            co = k.sb(f"convo{L}", [128, 4, L], F32)
            cob = [Buf() for _ in range(4)]
            for j in range(4):
                yield
                at, ab = k.pool(f"cv_a{L}", [128, L], F32, 1)
                bt, bb = k.pool(f"cv_b{L}", [128, L], F32, 1)
                k.dma("sp", at[:], k.P128[5 + j, :, s0:s0 + L], reads=k.P128b[5 + j], writes=[ab])
                k.dma("sp", bt[:], k.P128[9 + j, :, s0:s0 + L], reads=k.P128b[9 + j], writes=[bb])
                k.op("act", lambda e, bt=bt: e.activation(out=bt[:], in_=bt[:], func=AF.Sigmoid), reads=[bb], writes=[bb])
                hp, hpb = k.pool(f"cv_h{L}", [128, L + 30], F32, 2)
                k.op("pool", lambda e, hp=hp: e.memset(hp[:, 0:15], 0.0), writes=[hpb])
                k.op("pool", lambda e, hp=hp: e.memset(hp[:, L + 15:L + 30], 0.0), writes=[hpb])
                k.op("pool", lambda e, hp=hp, at=at, bt=bt: e.tensor_tensor(out=hp[:, 15:15 + L], in0=at[:], in1=bt[:], op=ALU.mult),
                     reads=[ab, bb], writes=[hpb])
                k.op("dve", lambda e, hp=hp, j=j: e.tensor_scalar(
                    out=co[:, j, :], in0=hp[:, 0:L], scalar1=cp[:, j, 0:1], scalar2=cp[:, j, 31:32],
                    op0=ALU.mult, op1=ALU.add), reads=[hpb, cpb], writes=[cob[j]])
                for tap in range(1, 31):
                    k.op("dve", lambda e, hp=hp, j=j, tap=tap: e.scalar_tensor_tensor(
                        out=co[:, j, :], in0=hp[:, tap:tap + L], scalar=cp[:, j, tap:tap + 1], in1=co[:, j, :],
                        op0=ALU.mult, op1=ALU.add), reads=[hpb, cpb, cob[j]], writes=[cob[j]])
            for t0 in range(0, L, 512):
                yield
                tb = min(512, L - t0)
                pm, pmb = k.psum()
                for j in range(4):
                    k.op("pe", lambda e, pm=pm, j=j: e.matmul(pm[:, :tb], lhsT=k.ones_f[:], rhs=co[:, j, t0:t0 + tb],
                                                             start=(j == 0), stop=(j == 3)), reads=[cob[j], k.ones_fb], writes=[pmb])
                for j in range(4):
                    k.op("dve", lambda e, pm=pm, j=j: e.scalar_tensor_tensor(
                        out=co[:, j, t0:t0 + tb], in0=pm[:, :tb], scalar=-1.0 / 512, in1=co[:, j, t0:t0 + tb],
                        op0=ALU.mult, op1=ALU.add), reads=[pmb, cob[j]], writes=[cob[j]])
                pv, pvb = k.psum()
                for j in range(4):
                    sq, sqb = k.pool("cv_sq", [128, 512], F32, 2)
                    k.op("act", lambda e, sq=sq, j=j: e.activation(out=sq[:, :tb], in_=co[:, j, t0:t0 + tb], func=AF.Square),
                         reads=[cob[j]], writes=[sqb])
                    k.op("pe", lambda e, pv=pv, sq=sq, j=j: e.matmul(pv[:, :tb], lhsT=k.ones_f[:], rhs=sq[:, :tb],
                                                                    start=(j == 0), stop=(j == 3)), reads=[sqb, k.ones_fb], writes=[pvb])
                rs, rsb = k.pool("cv_rs", [128, 512], F32, 2)
                k.op("act", lambda e, rs=rs, pv=pv: e.activation(out=rs[:, :tb], in_=pv[:, :tb], func=AF.Sqrt,
                                                                scale=1.0 / 512, bias=k.epsc[:, 2:3]), reads=[pvb, k.epsb], writes=[rsb])
                k.op("dve", lambda e, rs=rs: e.reciprocal(out=rs[:, :tb], in_=rs[:, :tb]), reads=[rsb], writes=[rsb])
                for j in range(4):
                    y, yb = k.pool("cv_y", [128, 512], F32, 2)
                    k.op("dve", lambda e, y=y, rs=rs, j=j: e.tensor_tensor(out=y[:, :tb], in0=co[:, j, t0:t0 + tb], in1=rs[:, :tb],
                                                                          op=ALU.mult), reads=[cob[j], rsb], writes=[yb])
                    k.op("act", lambda e, y=y, j=j: e.activation(out=y[:, :tb], in_=y[:, :tb], func=AF.Identity,
                                                                 scale=cp[:, j, 32:33], bias=cp[:, j, 33:34]), reads=[yb, cpb], writes=[yb])
                    sg, sgb = k.pool("cv_sg", [128, 512], F32, 2)
                    k.op("act", lambda e, y=y, sg=sg: e.activation(out=sg[:, :tb], in_=y[:, :tb], func=AF.Sigmoid),
                         reads=[yb], writes=[sgb])
                    o, ob = k.pool("cv_o", [128, 512], BF16, 2)
                    k.op("pool", lambda e, y=y, sg=sg, o=o: e.tensor_tensor(out=o[:, :tb], in0=y[:, :tb], in1=sg[:, :tb], op=ALU.mult),
                         reads=[yb, sgb], writes=[ob])
                    tt0 = s0 + t0
                    k.dma("sp", k.FEATS[12 + j, :, tt0:tt0 + tb], o[:, :tb], reads=[ob],
                          writes=[k.FEATSb[3][(tt0 // 128) + i] for i in range(tb // 128)])

    def fno(self, l, with_ctx=True):
        k = self
        c128 = k.sb("c128_t", [128, 256], BF16)
        c128b = Buf()
        k.dma("sp", c128[:], k.c128_d[:], writes=[c128b])
        for (s0, L, ftab) in [(0, CTX, k.fC_d), (CTX, SEQ, k.fL_d)]:
            if L == CTX and not with_ctx:
                continue
            nt = L // 128
            A = k.sb(f"fnoA{L}", [128, nt, 4, 256], BF16)
            Ab = Buf()
            for g in range(4):
                yield
                u, ub = k.pool(f"fno_u{L}", [128, L], F32, 1)
                k.dma("sp", u[:], k.P128[1 + g, :, s0:s0 + L], reads=k.P128b[1 + g], writes=[ub])
                ubf, ubfb = k.pool(f"fno_ub{L}", [128, L], BF16, 2)
                k.op("act", lambda e, u=u, ubf=ubf: e.copy(out=ubf[:], in_=u[:]), reads=[ub], writes=[ubfb])
                for ti in range(nt):
                    pt, pb = k.psum()
                    k.op("pe", lambda e, pt=pt, ubf=ubf, ti=ti: e.matmul(
                        pt[:, :256], lhsT=ubf[:, ti * 128:(ti + 1) * 128], rhs=c128[:], start=True, stop=True),
                        reads=[ubfb, c128b], writes=[pb])
                    if ti % 2 == 0:
                        k.op("act", lambda e, pt=pt, ti=ti, g=g: e.copy(out=A[:, ti, g, :], in_=pt[:, :256]), reads=[pb], writes=[Ab])
                    else:
                        k.op("dve", lambda e, pt=pt, ti=ti, g=g: e.tensor_copy(out=A[:, ti, g, :], in_=pt[:, :256]), reads=[pb], writes=[Ab])
            for m0 in range(0, L, 512):
                mb = min(512, L - m0)
                F, Fb = k.pool(f"fno_F{L}", [128, nt, 2, mb], BF16, 1)
                for hlf in range(2):
                    n0, n1 = (hlf * nt) // 2, ((hlf + 1) * nt) // 2
                    for s in range(2):
                        k.dma("sp", F[:, n0:n1, s, :], ftab[n0 * 128:n1 * 128, s, m0:m0 + mb].rearrange("(n p) m -> p n m", p=128),
                              writes=[Fb])
                for g in range(4):
                    yield
                    pt, pb = k.psum()
                    for ti in range(nt):
                        for s in range(2):
                            k.op("pe", lambda e, pt=pt, F=F, ti=ti, s=s, g=g: e.matmul(
                                pt[:, :mb], lhsT=A[:, ti, g, s * 128:(s + 1) * 128], rhs=F[:, ti, s, :],
                                start=(ti == 0 and s == 0), stop=(ti == nt - 1 and s == 1)), reads=[Ab, Fb], writes=[pb])
                    o, ob = k.pool("fno_o", [128, 512], BF16, 2)
                    k.op("act" if g % 2 == 0 else "dve",
                         (lambda e, pt=pt, o=o: e.copy(out=o[:, :mb], in_=pt[:, :mb])) if g % 2 == 0 else
                         (lambda e, pt=pt, o=o: e.tensor_copy(out=o[:, :mb], in_=pt[:, :mb])), reads=[pb], writes=[ob])
                    tt0 = s0 + m0
                    k.dma("sp", k.FEATS[g, :, tt0:tt0 + mb], o[:, :mb], reads=[ob],
                          writes=[k.FEATSb[0][(tt0 // 128) + i] for i in range(mb // 128)])


def cf_host_prep(inputs):
    f32 = np.float32
    cp = np.zeros((DEPTH, 128, 4, 34), f32)
    for l in range(DEPTH):
        dw = np.asarray(inputs["conv_dw"], f32)[l]
        cp[l, :, :, 0:31] = dw.T.reshape(4, 128, 31).transpose(1, 0, 2)
        cp[l, :, :, 31] = np.asarray(inputs["conv_dw_b"], f32)[l].reshape(4, 128).T
        cp[l, :, :, 32] = np.asarray(inputs["conv_ln_g"], f32)[l].reshape(4, 128).T
        cp[l, :, :, 33] = np.asarray(inputs["conv_ln_b"], f32)[l].reshape(4, 128).T
    c128, fL, fC = fno_consts()
    return {"convp": cp, "c128": c128, "fL": fL, "fC": fC}


class MKB(MKC):
    def declare_b(self):
        self.wg_d = self.dram("wg", [DEPTH, 4, 16, 128, KC, 128], F32, kind="ExternalInput")
        self.wb_d = self.dram("wbr", [DEPTH, 4, 16, 128, 4, 128], F32, kind="ExternalInput")
        self.wo_d = self.dram("wo", [DEPTH, 16, 128, KC, 128], F32, kind="ExternalInput")
        self.w1_d = self.dram("w1", [DEPTH, 64, 128, KC, 128], F32, kind="ExternalInput")
        self.w2_d = self.dram("w2", [DEPTH, 16, 4, 128, KC, 128], F32, kind="ExternalInput")
        self.XS = self.dram("XS", [KC, 128, T], F32)
        self.XSb = [Buf() for _ in TBS]
        self.yout = self.dram("yout", [KC, 128, SEQ], F32, kind="ExternalOutput")
        self.youtb = Buf()

    def wload(self, src, kcn=KC):
        wt, wb = self.pool(f"wB{kcn}", [128, kcn, 128], BF16, 4)
        self.dma("pool", wt[:], src, writes=[wb])
        return wt, wb

    def phase_b(self, l, xsrc, xsrc_bufs, last):
        k = self
        mod = k.mod[l]
        import os
        only = os.environ.get('PB_ONLY')
        for bi, (t0, tb) in enumerate(TBS):
            if bi == 0 and last:
                continue
            if only is not None and bi != int(only):
                continue
            ci = 1 if bi == 0 else 0
            xt, xb = k.pool("xblk", [128, KC, 512], F32, 1)
            for q in range(4):
                k.dma("sp", xt[:, 4 * q:4 * q + 4, :tb],
                      xsrc[4 * q:4 * q + 4, :, t0:t0 + tb].rearrange("k p t -> p k t"),
                      reads=[xsrc_bufs[bi]], writes=[xb])
            hx, hxb = k.pool("b_hx", [128, KC, 512], BF16, 1)
            ft, ftb = k.pool("b_ft", [128, KC, 512], BF16, 1)
            k.dma("sp", hx[:, :, :tb], k.HX[:, :, t0:t0 + tb].rearrange("k p t -> p k t"), reads=[k.HXb[bi]], writes=[hxb])
            fr = [b_ for br in range(4) for b_ in k.FEATSb[br][t0 // 128:(t0 + tb) // 128]]
            for q in range(4):
                k.dma("sp", ft[:, 4 * q:4 * q + 4, :tb], k.FEATS[4 * q:4 * q + 4, :, t0:t0 + tb].rearrange("k p t -> p k t"),
                      reads=fr, writes=[ftb])
            mt, mb_ = k.pool("b_m", [128, KC, 512], BF16, 1)
            for dc in range(16):
                macc, maccb = k.pool("b_macc", [128, 512], F32, 2)
                for i in range(4):
                    wg, wgb = k.wload(k.wg_d[l, i, dc])
                    wbt, wbb = k.wload(k.wb_d[l, i, dc], 4)
                    pg, pgb = k.psum()
                    for kc in range(KC):
                        k.op("pe", lambda e, pg=pg, wg=wg, kc=kc: e.matmul(pg[:, :tb], lhsT=wg[:, kc, :], rhs=hx[:, kc, :tb],
                                                                          start=(kc == 0), stop=(kc == KC - 1)),
                             reads=[wgb, hxb], writes=[pgb])
                    pp, ppb = k.psum()
                    for kc in range(4):
                        k.op("pe", lambda e, pp=pp, wbt=wbt, kc=kc, i=i: e.matmul(pp[:, :tb], lhsT=wbt[:, kc, :], rhs=ft[:, 4 * i + kc, :tb],
                                                                                 start=(kc == 0), stop=(kc == 3)),
                             reads=[wbb, ftb], writes=[ppb])
                    sg, sgb = k.pool("b_sig", [128, 512], F32, 2)
                    k.op("act", lambda e, pg=pg, sg=sg: e.activation(out=sg[:, :tb], in_=pg[:, :tb], func=AF.Sigmoid),
                         reads=[pgb], writes=[sgb])
                    if i == 0:
                        k.op("dve", lambda e, pp=pp, sg=sg, macc=macc: e.tensor_tensor(out=macc[:, :tb], in0=pp[:, :tb], in1=sg[:, :tb], op=ALU.mult),
                             reads=[ppb, sgb], writes=[maccb])
                    else:
                        k.op("dve", lambda e, pp=pp, sg=sg: e.tensor_tensor(out=sg[:, :tb], in0=pp[:, :tb], in1=sg[:, :tb], op=ALU.mult),
                             reads=[ppb, sgb], writes=[sgb])
                        if i < 3:
                            k.op("dve", lambda e, sg=sg, macc=macc: e.tensor_tensor(out=macc[:, :tb], in0=macc[:, :tb], in1=sg[:, :tb], op=ALU.add),
                                 reads=[maccb, sgb], writes=[maccb])
                        else:
                            k.op("dve", lambda e, sg=sg, macc=macc, dc=dc: e.tensor_tensor(out=mt[:, dc, :tb], in0=macc[:, :tb], in1=sg[:, :tb], op=ALU.add),
                                 reads=[maccb, sgb], writes=[mb_])
            for dc in range(16):
                wo, wob = k.wload(k.wo_d[l, dc])
                py, pyb = k.psum()
                for kc in range(KC):
                    k.op("pe", lambda e, py=py, wo=wo, kc=kc: e.matmul(py[:, :tb], lhsT=wo[:, kc, :], rhs=mt[:, kc, :tb],
                                                                      start=(kc == 0), stop=(kc == KC - 1)),
                         reads=[wob, mb_], writes=[pyb])
                k.op("dve", lambda e, py=py, dc=dc: e.scalar_tensor_tensor(
                    out=xt[:, dc, :tb], in0=py[:, :tb], scalar=mod[:, 2 * KC + dc, ci:ci + 1], in1=xt[:, dc, :tb],
                    op0=ALU.mult, op1=ALU.add), reads=[pyb, xb, k.modb[l]], writes=[xb])
            k.norm_block(l, 1, xt, xb, t0, tb, bi == 0, hx, hxb, sq_tile=(ft, ftb))
            hm, hmb = k.pool("b_hmid", [128, 64, 512], BF16, 1)
            for fc in range(64):
                w1, w1b = k.wload(k.w1_d[l, fc])
                ph, phb = k.psum()
                for kc in range(KC):
                    k.op("pe", lambda e, ph=ph, w1=w1, kc=kc: e.matmul(ph[:, :tb], lhsT=w1[:, kc, :], rhs=hx[:, kc, :tb],
                                                                      start=(kc == 0), stop=(kc == KC - 1)),
                         reads=[w1b, hxb], writes=[phb])
                rl, rlb = k.pool("b_relu", [128, 512], F32, 2)
                k.op("act", lambda e, ph=ph, rl=rl: e.activation(out=rl[:, :tb], in_=ph[:, :tb], func=AF.Relu),
                     reads=[phb], writes=[rlb])
                k.op("dve", lambda e, rl=rl, fc=fc: e.tensor_tensor(out=hm[:, fc, :tb], in0=rl[:, :tb], in1=rl[:, :tb], op=ALU.mult),
                     reads=[rlb], writes=[hmb])
            for dc in range(16):
                py, pyb = k.psum()
                for q4 in range(4):
                    w2, w2b = k.wload(k.w2_d[l, dc, q4])
                    for kc in range(KC):
                        k.op("pe", lambda e, py=py, w2=w2, kc=kc, q4=q4: e.matmul(
                            py[:, :tb], lhsT=w2[:, kc, :], rhs=hm[:, q4 * KC + kc, :tb],
                            start=(q4 == 0 and kc == 0), stop=(q4 == 3 and kc == KC - 1)),
                            reads=[w2b, hmb], writes=[pyb])
                k.op("dve", lambda e, py=py, dc=dc: e.scalar_tensor_tensor(
                    out=xt[:, dc, :tb], in0=py[:, :tb], scalar=mod[:, 5 * KC + dc, ci:ci + 1], in1=xt[:, dc, :tb],
                    op0=ALU.mult, op1=ALU.add), reads=[pyb, xb, k.modb[l]], writes=[xb])
            if not last:
                for q in range(4):
                    k.dma("sp", k.XS[4 * q:4 * q + 4, :, t0:t0 + tb].rearrange("k p t -> p k t"), xt[:, 4 * q:4 * q + 4, :tb],
                          reads=[xb], writes=[k.XSb[bi]])
            else:
                k.final_norm(xt, xb, t0, tb, sq_tile=(ft, ftb))

    def final_norm(self, xt, xb, t0, tb, sq_tile=None):
        k = self
        sq, sqb = sq_tile if sq_tile is not None else k.pool("sq", [128, KC, 512], BF16, 1)
        k.op("act", lambda e: e.activation(out=sq[:, :, :tb], in_=xt[:, :, :tb], func=AF.Square), reads=[xb], writes=[sqb])
        pt, pb = k.psum()
        for kc in range(KC):
            k.op("pe", lambda e, kc=kc: e.matmul(pt[:, :tb], lhsT=k.ones_bf[:], rhs=sq[:, kc, :tb],
                                                 start=(kc == 0), stop=(kc == KC - 1)), reads=[sqb, k.ones_b], writes=[pb])
        rs, rsb = k.pool("rstd", [128, 512], F32, 2)
        k.op("act", lambda e: e.activation(out=rs[:, :tb], in_=pt[:, :tb], func=AF.Sqrt, scale=1.0 / D, bias=k.epsc[:, 0:1]),
             reads=[pb, k.epsb], writes=[rsb])
        k.op("dve", lambda e: e.reciprocal(out=rs[:, :tb], in_=rs[:, :tb]), reads=[rsb], writes=[rsb])
        for kc in range(KC):
            k.op("dve", lambda e, kc=kc: e.scalar_tensor_tensor(
                out=xt[:, kc, :tb], in0=xt[:, kc, :tb], scalar=k.ngt[:, 4, kc:kc + 1], in1=rs[:, :tb],
                op0=ALU.mult, op1=ALU.mult), reads=[xb, rsb, k.ngb], writes=[xb])
        for q in range(4):
            k.dma("sp", k.yout[4 * q:4 * q + 4, :, t0 - CTX:t0 - CTX + tb].rearrange("k p t -> p k t"), xt[:, 4 * q:4 * q + 4, :tb],
                  reads=[xb], writes=[k.youtb])

    def build_all(self):
        k = self
        k.declare_inputs()
        k.declare_rwkv()
        k.declare_att()
        k.declare_cf()
        k.declare_b()
        k.setup_eps()
        k.setup_consts()
        k.rwkv_setup()
        xsrc, xbufs = k.xin, [Buf() for _ in TBS]
        for l in range(DEPTH):
            last = (l == DEPTH - 1)
            if l == 0:
                with k.scope():
                    k.run_gens([k.phase_mod(l)])
            with k.scope():
                k.phase_a(l, xsrc, xbufs)
            with k.scope():
                k.rwkv(l, extra=[k.phase_mod(l + 1)] if l + 1 < DEPTH else [])
            with k.scope():
                k.run_gens([k.rwkv_readout_gen(l), k.attention(l)])
            with k.scope():
                k.run_gens([k.conv(l), k.fno(l, with_ctx=not last)])
            with k.scope():
                k.phase_b(l, xsrc, xbufs, last)
            xsrc, xbufs = k.XS, k.XSb
        k.finish([k.youtb])


def b_host_prep(inputs):
    f32 = np.float32
    w_in = np.asarray(inputs["w_in"], f32)
    wbr = np.asarray(inputs["w_branch"], f32)
    wo = np.asarray(inputs["w_out"], f32)
    w1 = np.asarray(inputs["w_mlp1"], f32)
    w2 = np.asarray(inputs["w_mlp2"], f32)
    out = {}
    g = w_in[:, :, O_GATE:].reshape(DEPTH, KC, 128, 4, 16, 128)
    out["wg"] = np.ascontiguousarray(g.transpose(0, 3, 4, 2, 1, 5))
    b = wbr.reshape(DEPTH, 4, 4, 128, 16, 128)
    out["wbr"] = np.ascontiguousarray(b.transpose(0, 1, 4, 3, 2, 5))
    o = wo.reshape(DEPTH, KC, 128, 16, 128)
    out["wo"] = np.ascontiguousarray(o.transpose(0, 3, 2, 1, 4))
    a = w1.reshape(DEPTH, KC, 128, 64, 128)
    out["w1"] = np.ascontiguousarray(a.transpose(0, 3, 2, 1, 4))
    c = w2.reshape(DEPTH, 4, KC, 128, 16, 128)
    out["w2"] = np.ascontiguousarray(c.transpose(0, 4, 1, 3, 2, 5))
    return out


def full_host_prep(inputs):
    shared, per_core = host_prep(inputs)
    shared.update(rwkv_host_prep(inputs))
    shared.update(att_host_prep(inputs))
    shared.update(cf_host_prep(inputs))
    shared.update(b_host_prep(inputs))
    return shared, per_core


def kernel(**inputs):
    shared, per_core = full_host_prep(inputs)
    k = MKB()
    k.build_all()
    n = 8
    in_maps = []
    for i in range(n):
        m = dict(shared)
        m.update(per_core[i % len(per_core)])
        in_maps.append(m)
    res = run_bass_kernel_spmd(k.nc, in_maps, core_ids=list(range(n)))
    B = len(per_core)
    out = np.stack([np.asarray(res.results[b]["yout"]).reshape(D, SEQ).T for b in range(B)])
    return np.ascontiguousarray(out.astype(np.float32))
```
